# Optimizing a Trainium2 kernel written in Bass

```python
import math
import jax, jax.numpy as jnp
from jax import lax
import numpy as np

D_MODEL = 1024
BATCH = 8
SEQ = 8192
DEPTH = 1

HEAD_DIM = 64
N_ATTN_HEADS = 8
ATTN_WIDTH = N_ATTN_HEADS * HEAD_DIM
N_SSM_HEADS = 24
SSM_WIDTH = N_SSM_HEADS * HEAD_DIM
MIX_WIDTH = ATTN_WIDTH + SSM_WIDTH
SSM_GROUPS = 2
SSM_STATE = 128
CONV_WIDTH = 4
CONV_DIM = SSM_WIDTH + 2 * SSM_GROUPS * SSM_STATE
SSD_CHUNK = 128
DILATED_PATTERNS = ((128, 1), (512, 4), (2048, 16))
IN_PROJ = 3 * ATTN_WIDTH + SSM_WIDTH + CONV_DIM + N_SSM_HEADS
D_FF = -(-8 * D_MODEL // (3 * 256)) * 256
PLE_DIM = 256
EPS = 1e-6

kernel_name = "hymba_ssd_dilated_swa_block"


def rms_norm(x, gain):
    xf = x.astype(jnp.float32)
    y = xf * lax.rsqrt(jnp.mean(xf * xf, axis=-1, keepdims=True) + EPS)
    return (y * gain.astype(jnp.float32)).astype(x.dtype)


def dilated_window_attention(q, k, v, window, dilation):
    b, s, h, dh = q.shape
    n = window // dilation
    span = n * dilation
    s_pad = -(-s // span) * span
    nb = s_pad // span

    def blocks(t):
        t = jnp.pad(t, ((0, 0), (0, s_pad - s), (0, 0), (0, 0)))
        return t.reshape(b, nb, n, dilation, h, dh)

    def with_prev(t):
        prev = jnp.pad(t[:, :-1], ((0, 0), (1, 0), (0, 0), (0, 0), (0, 0), (0, 0)))
        return jnp.concatenate([prev, t], axis=2)

    qb = blocks(q)
    kc = with_prev(blocks(k))
    vc = with_prev(blocks(v))
    scores = jnp.einsum('bnidhe,bnjdhe->bndhij', qb, kc,
                        preferred_element_type=jnp.float32)
    qi = jnp.arange(n)[:, None]
    kj = jnp.arange(2 * n)[None, :]
    dist = qi + n - kj
    band = (dist >= 0) & (dist <= n)
    first = (jnp.arange(nb) == 0)[:, None, None] & (kj < n)[None]
    mask = band[None] & ~first
    scores = jnp.where(mask[:, None, None], scores, -jnp.inf)
    m = jnp.max(scores, axis=-1, keepdims=True)
    e = jnp.exp(scores - m)
    den = jnp.sum(e, axis=-1, keepdims=True)
    out = jnp.einsum('bndhij,bnjdhe->bndhie', e, vc.astype(jnp.float32)) / den
    lse = (m + jnp.log(den))[..., 0]
    out = out.transpose(0, 1, 4, 2, 3, 5).reshape(b, s_pad, h, dh)[:, :s]
    lse = lse.transpose(0, 1, 4, 2, 3).reshape(b, s_pad, h)[:, :s]
    return out, lse


def mixture_of_dilations(q, k, v):
    outs, lses = [], []
    for window, dilation in DILATED_PATTERNS:
        o, l = dilated_window_attention(q, k, v, window, dilation)
        outs.append(o)
        lses.append(l)
    wts = jax.nn.softmax(jnp.stack(lses, axis=0), axis=0)
    return jnp.einsum('pbsh,pbshe->bshe', wts, jnp.stack(outs, axis=0))


def causal_depthwise_conv(u, w, bias):
    c = u.shape[-1]
    out = lax.conv_general_dilated(u, w[:, None, :].astype(u.dtype), window_strides=(1,),
                                   padding=[(w.shape[0] - 1, 0)],
                                   dimension_numbers=('NWC', 'WIO', 'NWC'),
                                   feature_group_count=c)
    return out + bias.astype(u.dtype)


def ssd_chunked(x, dt, a, bm, cm):
    b, s, h, pdim = x.shape
    g, n = bm.shape[2], bm.shape[3]
    hg = h // g
    q = SSD_CHUNK
    s_pad = -(-s // q) * q
    nc = s_pad // q

    def pad(t):
        return jnp.pad(t, [(0, 0), (0, s_pad - s)] + [(0, 0)] * (t.ndim - 2))

    xc = pad(x * dt[..., None]).reshape(b, nc, q, g, hg, pdim)
    da = pad(dt * a).reshape(b, nc, q, g, hg)
    bc = pad(bm).reshape(b, nc, q, g, n)
    cc = pad(cm).reshape(b, nc, q, g, n)
    acs = jnp.cumsum(da, axis=2)

    seg = acs[:, :, :, None] - acs[:, :, None, :]
    causal = jnp.tril(jnp.ones((q, q), dtype=bool))
    decay = jnp.exp(jnp.where(causal[:, :, None, None], seg, -jnp.inf))
    cb = jnp.einsum('bclgn,bcsgn->bclsg', cc, bc)
    y_diag = jnp.einsum('bclsgh,bcsghp->bclghp', cb[..., None] * decay, xc)

    decay_to_end = jnp.exp(acs[:, :, -1:] - acs)
    states = jnp.einsum('bcsgn,bcsgh,bcsghp->bcghpn', bc, decay_to_end, xc)
    chunk_decay = jnp.exp(acs[:, :, -1])

    def step(carry, inp):
        st, dec = inp
        return carry * dec[..., None, None] + st, carry

    init = jnp.zeros((b, g, hg, pdim, n), jnp.float32)
    _, prev_states = lax.scan(step, init, (jnp.moveaxis(states, 1, 0),
                                           jnp.moveaxis(chunk_decay, 1, 0)))
    prev_states = jnp.moveaxis(prev_states, 0, 1)
    y_off = jnp.einsum('bclgn,bcghpn,bclgh->bclghp', cc, prev_states, jnp.exp(acs))
    return (y_diag + y_off).reshape(b, s_pad, h, pdim)[:, :s]


def gated_group_rms_norm(y, z, gain):
    b, s, w = y.shape
    u = (y.astype(jnp.float32) * jax.nn.silu(z.astype(jnp.float32))).reshape(b, s, SSM_GROUPS, w // SSM_GROUPS)
    u = u * lax.rsqrt(jnp.mean(u * u, axis=-1, keepdims=True) + EPS)
    return u.reshape(b, s, w) * gain.astype(jnp.float32)


def setup_inputs(seed: int = 0) -> dict:
    key = jax.random.key(seed)
    ks = jax.random.split(key, 24)
    L = DEPTH
    f32 = jnp.float32

    def dense(k, shape, fan_in):
        return jax.random.normal(k, shape, f32) * fan_in ** -0.5

    def gain(k, shape):
        return 1.0 + 0.02 * jax.random.normal(k, shape, f32)

    dt0 = jnp.exp(jax.random.uniform(ks[10], (L, N_SSM_HEADS), f32,
                                     minval=math.log(1e-3), maxval=math.log(1e-1)))
    return {
        "x": jax.random.normal(ks[0], (BATCH, SEQ, D_MODEL), f32),
        "p": jax.random.normal(ks[1], (DEPTH, BATCH, SEQ, PLE_DIM), f32),
        "mix_norm": gain(ks[2], (L, D_MODEL)),
        "w_in": dense(ks[3], (L, D_MODEL, IN_PROJ), D_MODEL),
        "q_norm": gain(ks[4], (L, HEAD_DIM)),
        "k_norm": gain(ks[5], (L, HEAD_DIM)),
        "conv_w": dense(ks[6], (L, CONV_WIDTH, CONV_DIM), CONV_WIDTH),
        "conv_b": 0.02 * jax.random.normal(ks[7], (L, CONV_DIM), f32),
        "dt_bias": dt0 + jnp.log(-jnp.expm1(-dt0)),
        "a_log": jnp.log(jax.random.uniform(ks[8], (L, N_SSM_HEADS), f32, minval=1.0, maxval=16.0)),
        "d_skip": gain(ks[9], (L, N_SSM_HEADS)),
        "ssm_norm": gain(ks[11], (L, SSM_WIDTH)),
        "w_out": dense(ks[12], (L, MIX_WIDTH, D_MODEL), MIX_WIDTH),
        "ffn_norm": gain(ks[13], (L, D_MODEL)),
        "w_ffn_gate": dense(ks[14], (L, D_MODEL, D_FF), D_MODEL),
        "w_ffn_up": dense(ks[15], (L, D_MODEL, D_FF), D_MODEL),
        "w_ffn_down": dense(ks[16], (L, D_FF, D_MODEL), D_FF),
        "ple_gate_norm": gain(ks[17], (L, D_MODEL)),
        "w_ple_gate": dense(ks[18], (L, D_MODEL, D_MODEL), D_MODEL),
        "b_ple_gate": 0.02 * jax.random.normal(ks[19], (L, D_MODEL), f32),
        "w_ple": dense(ks[20], (L, PLE_DIM, D_MODEL), PLE_DIM),
        "ple_norm": gain(ks[21], (L, D_MODEL)),
    }


def reference(x, p, mix_norm, w_in, q_norm, k_norm, conv_w, conv_b, dt_bias, a_log, d_skip,
              ssm_norm, w_out, ffn_norm, w_ffn_gate, w_ffn_up, w_ffn_down, ple_gate_norm,
              w_ple_gate, b_ple_gate, w_ple, ple_norm):
    b, s, _ = x.shape
    splits = [ATTN_WIDTH, 2 * ATTN_WIDTH, 3 * ATTN_WIDTH,
              3 * ATTN_WIDTH + SSM_WIDTH, 3 * ATTN_WIDTH + SSM_WIDTH + CONV_DIM]
    for i in range(DEPTH):
        h = rms_norm(x, mix_norm[i])
        proj = h @ w_in[i]
        q, k, v, z, xbc, dt_raw = jnp.split(proj, splits, axis=-1)

        q = rms_norm(q.reshape(b, s, N_ATTN_HEADS, HEAD_DIM), q_norm[i]) * (HEAD_DIM ** -0.5)
        k = rms_norm(k.reshape(b, s, N_ATTN_HEADS, HEAD_DIM), k_norm[i])
        v = v.reshape(b, s, N_ATTN_HEADS, HEAD_DIM)
        attn = mixture_of_dilations(q, k, v).reshape(b, s, ATTN_WIDTH)

        xbc = jax.nn.silu(causal_depthwise_conv(xbc, conv_w[i], conv_b[i]))
        xs, bm, cm = jnp.split(xbc, [SSM_WIDTH, SSM_WIDTH + SSM_GROUPS * SSM_STATE], axis=-1)
        xs = xs.reshape(b, s, N_SSM_HEADS, HEAD_DIM).astype(jnp.float32)
        dt = jax.nn.softplus(dt_raw.astype(jnp.float32) + dt_bias[i].astype(jnp.float32))
        a = -jnp.exp(a_log[i].astype(jnp.float32))
        y = ssd_chunked(xs, dt, a,
                        bm.reshape(b, s, SSM_GROUPS, SSM_STATE).astype(jnp.float32),
                        cm.reshape(b, s, SSM_GROUPS, SSM_STATE).astype(jnp.float32))
        y = y + d_skip[i].astype(jnp.float32)[:, None] * xs
        y = gated_group_rms_norm(y.reshape(b, s, SSM_WIDTH), z, ssm_norm[i])

        mixed = jnp.concatenate([attn.astype(x.dtype), y.astype(x.dtype)], axis=-1)
        x = x + mixed @ w_out[i]

        h = rms_norm(x, ffn_norm[i])
        x = x + (jax.nn.silu(h @ w_ffn_gate[i]) * (h @ w_ffn_up[i])) @ w_ffn_down[i]

        gate = jax.nn.sigmoid(rms_norm(x, ple_gate_norm[i]) @ w_ple_gate[i] + b_ple_gate[i])
        e = rms_norm(p[i] @ w_ple[i], ple_norm[i])
        x = x + gate * e
    return x
```

```python
from contextlib import ExitStack
import numpy as np
import concourse.bass as bass
import concourse.mybir as mybir
from concourse.bass_utils import run_bass_kernel_spmd

F32 = mybir.dt.float32
BF16 = mybir.dt.bfloat16
AF = mybir.ActivationFunctionType
ALU = mybir.AluOpType
AX = mybir.AxisListType

ENG = ["pe", "act", "dve", "pool", "sp"]

D_MODEL = 1024
SEQ_FULL = 8192
NH_A = 8
HD = 64
AW = 512
NH_S = 24
SW = 1536
CONV_DIM = 2048
NSTATE = 128
IN_PROJ = 5144
D_FF = 2816
PLE = 256
EPS = 1e-6


class Buf:
    def __init__(self, t, name, space):
        self.t = t
        self.name = name
        self.space = space
        self.w = {}
        self.r = {}

    def __getitem__(self, k):
        return self.t[k]


class Prog:
    def __init__(self, nc, stack):
        self.nc = nc
        self.stack = stack
        self.q = {e: [] for e in ENG}
        self.emitted = {e: 0 for e in ENG}
        self.known = {e: {} for e in ENG}
        self.semval = {e: 0 for e in ENG}
        self.resolved = {e: {} for e in ENG}
        self.esem = {e: stack.enter_context(nc.semaphore("es_" + e)) for e in ENG}
        self.dsem = {}
        self.dcount = {}

    def sb(self, stack, name, shape, dtype):
        t = stack.enter_context(self.nc.sbuf_tensor(name, list(shape), dtype))
        return Buf(t, name, "sb")

    def ps(self, stack, name, shape, dtype=F32):
        t = stack.enter_context(self.nc.psum_tensor(name, list(shape), dtype))
        return Buf(t, name, "ps")

    def dr(self, t, name):
        return Buf(t, name, "dr")

    def _collect(self, e, reads, writes, skipkey=None):
        waits = {}

        def need(k, v):
            if k == skipkey:
                return
            if k == "pe" and e == "pe":
                return
            if self.known[e].get(k, -1) >= v:
                return
            if waits.get(k, -1) < v:
                waits[k] = v

        for b in reads:
            for k, v in b.w.items():
                need(k, v)
        for b in writes:
            for k, v in b.w.items():
                need(k, v)
            for k, v in b.r.items():
                need(k, v)
        for k, v in waits.items():
            self.known[e][k] = v
            if k in self.q:
                self.q[k][v]["inc"] = True
        return list(waits.items())

    def op(self, e, fn, reads=(), writes=()):
        idx = len(self.q[e])
        waits = self._collect(e, reads, writes)
        self.q[e].append(dict(fn=fn, waits=waits, inc=False, dkey=None))
        for b in reads:
            b.r[e] = idx
        for b in writes:
            b.w = {e: idx}
            b.r = {}
        return idx

    def dma(self, e, dst, dst_ap, src, src_ap, **kw):
        if dst.space != "dr":
            key = ("in", dst.name)
        elif src.space != "dr":
            key = ("out", src.name)
        else:
            key = ("dd", dst.name)
        waits = self._collect(e, [src], [dst], skipkey=key)
        cnt = self.dcount.get(key, 0) + 1
        self.dcount[key] = cnt
        val = 16 * cnt

        def fn(eng, dst_ap=dst_ap, src_ap=src_ap, kw=kw):
            return eng.dma_start(out=dst_ap, in_=src_ap, **kw)

        self.q[e].append(dict(fn=fn, waits=waits, inc=False, dkey=key))
        src.r[key] = val
        keep = {k: v for k, v in dst.w.items() if k == key}
        dst.w = keep
        dst.w[key] = val
        dst.r = {}

    def barrier(self):
        for e in ENG:
            waits = {}
            for k in ENG:
                if k == e or not self.q[k] or k == "sp":
                    continue
                v = -1
                for i in range(len(self.q[k]) - 1, -1, -1):
                    if self.q[k][i]["fn"] is not None and self.q[k][i]["dkey"] is None:
                        v = i
                        break
                if v < 0:
                    continue
                if self.known[e].get(k, -1) < v:
                    waits[k] = v
            for k, c in self.dcount.items():
                v = 16 * c
                if self.known[e].get(k, -1) < v:
                    waits[k] = v
            for k, v in waits.items():
                self.known[e][k] = v
                if k in self.q:
                    self.q[k][v]["inc"] = True
            self.q[e].append(dict(fn=None, waits=list(waits.items()), inc=False, dkey=None))

    def _sem(self, k):
        if k in self.esem:
            return self.esem[k]
        if k not in self.dsem:
            self.dsem[k] = self.stack.enter_context(
                self.nc.semaphore("ds%d" % len(self.dsem)))
        return self.dsem[k]

    def emit(self):
        nc = self.nc
        for e in ENG:
            for i in range(self.emitted[e], len(self.q[e])):
                r = self.q[e][i]
                if r["inc"] and r["fn"] is not None and r["dkey"] is None:
                    self.semval[e] += 1
                    self.resolved[e][i] = self.semval[e]
        for k in list(self.dcount):
            self._sem(k)

        def run(e, eng):
            recs = self.q[e]
            for i in range(self.emitted[e], len(recs)):
                rec = recs[i]
                for k, v in rec["waits"]:
                    if k in self.esem:
                        v = self.resolved[k][v]
                    eng.wait_ge(self._sem(k), v)
                if rec["fn"] is None:
                    continue
                ins = rec["fn"](eng)
                if rec["dkey"] is not None:
                    ins.then_inc(self._sem(rec["dkey"]), 16)
                elif rec["inc"]:
                    ins.then_inc(self.esem[e], 1)
            self.emitted[e] = len(recs)

        with nc.Block() as block:
            @block.tensor
            def _(eng):
                run("pe", eng)

            @block.scalar
            def _(eng):
                run("act", eng)

            @block.vector
            def _(eng):
                run("dve", eng)

            @block.gpsimd
            def _(eng):
                run("pool", eng)

            @block.sync
            def _(eng):
                run("sp", eng)

    def act(self, ob, o, ib, i, func, rd=(), wr=(), **kw):
        self.op("act", lambda e: e.activation(out=o, in_=i, func=func, **kw),
                [ib, *rd], [ob, *wr])

    def tt(self, eng, ob, o, ab, a, bb, b, op):
        self.op(eng, lambda e: e.tensor_tensor(out=o, in0=a, in1=b, op=op), [ab, bb], [ob])

    def ts(self, eng, ob, o, ab, a, s1, s2, op0, op1=None, rd=()):
        if op1 is None:
            self.op(eng, lambda e: e.tensor_scalar(out=o, in0=a, scalar1=s1, scalar2=None, op0=op0),
                    [ab, *rd], [ob])
        else:
            self.op(eng, lambda e: e.tensor_scalar(out=o, in0=a, scalar1=s1, scalar2=s2, op0=op0, op1=op1),
                    [ab, *rd], [ob])

    def stt(self, eng, ob, o, ab, a, s, bb, b, op0, op1, rd=(), wr=(), **kw):
        self.op(eng, lambda e: e.scalar_tensor_tensor(out=o, in0=a, scalar=s, in1=b, op0=op0, op1=op1, **kw),
                [ab, bb, *rd], [ob, *wr])

    def copy(self, eng, ob, o, ib, i):
        if eng == "act":
            self.op("act", lambda e: e.activation(out=o, in_=i, func=AF.Copy), [ib], [ob])
        else:
            self.op(eng, lambda e: e.tensor_copy(out=o, in_=i), [ib], [ob])

    def mm(self, ob, o, lb, l, rb, r, start, stop):
        self.op("pe", lambda e: e.matmul(o, lhsT=l, rhs=r, start=start, stop=stop), [lb, rb], [ob])

    def tr(self, ob, o, ib, i, idb, idap):
        self.op("pe", lambda e: e.transpose(o, i, idap), [ib, idb], [ob])

    def rstd(self, ssb, src, tmp, dst, inv_n):
        self.ts("dve", ssb, tmp, ssb, src, inv_n, EPS, ALU.mult, ALU.add)
        self.act(ssb, tmp, ssb, tmp, AF.Sqrt)
        self.op("dve", lambda e: e.reciprocal(out=dst, in_=tmp), [ssb], [ssb])


def bc_ap(handle, offset, dims):
    return bass.AP(tensor=handle, offset=offset, ap=[list(d) for d in dims])


def build(SEQ=SEQ_FULL, debug=False, phases="ABCD"):
    nc = bass.Bass("TRN2", target_bir_lowering=False)
    NT = SEQ // 128
    skind = "ExternalOutput" if debug else "Internal"

    def din(name, shape):
        return nc.dram_tensor(name, list(shape), F32, kind="ExternalInput")

    x_d = din("x", [SEQ, D_MODEL])
    p_d = din("p", [SEQ, PLE])
    win_d = din("w_in", [128, 8, IN_PROJ])
    wout_a_d = din("w_out_a", [64, 8, D_MODEL])
    wout_s_d = din("w_out_s", [128, 12, D_MODEL])
    wg_d = din("w_g", [22, 128, 8, 128])
    wu_d = din("w_u", [22, 128, 8, 128])
    wd_d = din("w_d", [128, 22, D_MODEL])
    wpg_d = din("w_pg", [128, 8, D_MODEL])
    wple_d = din("w_ple", [128, 2, D_MODEL])
    vec_d = din("vecs", [1, 17408])
    cst_d = din("consts", [128, 1024])
    out_d = nc.dram_tensor("out", [SEQ, D_MODEL], F32, kind="ExternalOutput")

    qT_d = nc.dram_tensor("qT_s", [128, 4, SEQ], BF16, kind=skind)
    kT_d = nc.dram_tensor("kT_s", [128, 4, SEQ], BF16, kind=skind)
    v_d = nc.dram_tensor("v_s", [SEQ, AW], BF16, kind=skind)
    u_d = nc.dram_tensor("u_s", [SEQ + 3, CONV_DIM], BF16, kind=skind)
    sz_d = nc.dram_tensor("sz_s", [SEQ, SW], BF16, kind=skind)
    dt_d = nc.dram_tensor("dt_s", [SEQ, NH_S], F32, kind=skind)
    attnT_d = nc.dram_tensor("attnT_s", [64, 8, SEQ], BF16, kind=skind)

    with ExitStack() as st:
        P = Prog(nc, st)
        X = P.dr(x_d, "x")
        WIN = P.dr(win_d, "w_in")
        VEC = P.dr(vec_d, "vecs")
        CST = P.dr(cst_d, "consts")
        QT = P.dr(qT_d, "qT_s")
        KT = P.dr(kT_d, "kT_s")
        V = P.dr(v_d, "v_s")
        U = P.dr(u_d, "u_s")
        SZ = P.dr(sz_d, "sz_s")
        DT = P.dr(dt_d, "dt_s")

        AT = P.dr(attnT_d, "attnT_s")
        if "A" in phases:
            phase_a(nc, P, SEQ, X, x_d, WIN, win_d, VEC, vec_d, CST, cst_d,
                    QT, qT_d, KT, kT_d, V, v_d, U, u_d, SZ, sz_d, DT, dt_d)
        if "B" in phases:
            phase_b(nc, P, SEQ, CST, cst_d, QT, qT_d, KT, kT_d, V, v_d, AT, attnT_d)
        OUT = P.dr(out_d, "out")
        if "C" in phases:
            phase_c(nc, P, SEQ, X, x_d, VEC, vec_d, CST, cst_d, U, u_d, SZ, sz_d, DT, dt_d,
                    AT, attnT_d, P.dr(wout_a_d, "w_out_a"), wout_a_d, P.dr(wout_s_d, "w_out_s"), wout_s_d,
                    OUT, out_d)
        if "D" in phases:
            phase_d1(nc, P, SEQ, VEC, vec_d, CST, cst_d, P.dr(wg_d, "w_g"), wg_d, P.dr(wu_d, "w_u"), wu_d,
                     P.dr(wd_d, "w_d"), wd_d, OUT, out_d)
            phase_d2(nc, P, SEQ, VEC, vec_d, CST, cst_d, P.dr(p_d, "p"), p_d, P.dr(wpg_d, "w_pg"), wpg_d,
                     P.dr(wple_d, "w_ple"), wple_d, OUT, out_d)
    return nc


VO = dict(mix_norm=0, q_norm=1024, k_norm=1088, conv_w=1152, conv_b=1152 + 8192,
          dt_bias=11392, a_log=11416, d_skip=11440, ssm_norm=11464, ffn_norm=13000,
          ple_gate_norm=14024, b_ple_gate=15048)
VO["ple_norm"] = 16072
NVEC = 17408


def phase_a(nc, P, SEQ, X, x_d, WIN, win_d, VEC, vec_d, CST, cst_d,
            QT, qT_d, KT, kT_d, V, v_d, U, u_d, SZ, sz_d, DT, dt_d):
    NT = SEQ // 128
    with ExitStack() as ph:
        Win = P.sb(ph, "Win", [128, 8, IN_PROJ], BF16)
        ident = P.sb(ph, "identA", [128, 128], BF16)
        gmix = P.sb(ph, "gmix", [128, D_MODEL], F32)
        qg = P.sb(ph, "qg", [128, 8, 64], F32)
        kg = P.sb(ph, "kg", [128, 8, 64], F32)
        dtb = P.sb(ph, "dtb", [128, NH_S], F32)
        zer = P.sb(ph, "zer", [4, CONV_DIM], BF16)
        xt = [P.sb(ph, "xt%d" % i, [128, D_MODEL], F32) for i in range(2)]
        junk = P.sb(ph, "junkA", [128, D_MODEL], F32)
        ss = [P.sb(ph, "ssA%d" % i, [128, 8], F32) for i in range(2)]
        hb = P.sb(ph, "hb", [128, D_MODEL], BF16)
        hT = [P.sb(ph, "hT%d" % i, [128, 8, 128], BF16) for i in range(2)]
        tmpf = [P.sb(ph, "tmpf%d" % i, [128, 8, 64], F32) for i in range(2)]
        tmpg = [P.sb(ph, "tmpg%d" % i, [128, 8, 64], F32) for i in range(2)]
        ssh = [P.sb(ph, "ssh%d" % i, [128, 24], F32) for i in range(2)]
        qb = [P.sb(ph, "qb%d" % i, [128, 512], BF16) for i in range(2)]
        qst = [P.sb(ph, "qst%d" % i, [128, 4, 512], BF16) for i in range(2)]
        kst = [P.sb(ph, "kst%d" % i, [128, 4, 512], BF16) for i in range(2)]
        vb = [P.sb(ph, "vb%d" % i, [128, 512], BF16) for i in range(2)]
        szb = [P.sb(ph, "szb%d" % i, [128, SW], BF16) for i in range(2)]
        ub = [P.sb(ph, "ub%d" % i, [128, CONV_DIM], BF16) for i in range(2)]
        dts = [P.sb(ph, "dts%d" % i, [128, 4, NH_S], F32) for i in range(2)]
        psT = P.ps(ph, "psT", [128, 8, 128], BF16)
        psQ = P.ps(ph, "psQ", [128, 4, 128], BF16)
        pj = [P.ps(ph, "pj%d" % i, [128, 512], F32) for i in range(4)]

        for kc in range(8):
            P.dma("pool", Win, Win[:, kc, :], WIN, win_d[:, kc, :])
        P.dma("pool", ident, ident[:], CST, cst_d[:, 0:128])
        P.dma("sp", gmix, gmix[:], VEC, bc_ap(vec_d, VO["mix_norm"], [[0, 128], [1, D_MODEL]]))
        P.dma("sp", qg, qg[:], VEC, bc_ap(vec_d, VO["q_norm"], [[0, 128], [0, 8], [1, 64]]))
        P.dma("sp", kg, kg[:], VEC, bc_ap(vec_d, VO["k_norm"], [[0, 128], [0, 8], [1, 64]]))
        P.dma("sp", dtb, dtb[:], VEC, bc_ap(vec_d, VO["dt_bias"], [[0, 128], [1, NH_S]]))
        P.ts("dve", qg, qg[:], qg, qg[:], HD ** -0.5, None, ALU.mult)
        P.op("dve", lambda e: e.memset(zer[:], 0.0), [], [zer])
        P.dma("pool", U, u_d[0:3, :], zer, zer[0:3, :])

        P.dma("sp", xt[0], xt[0][:], X, x_d[0:128, :])
        cnt = 0
        for t in range(NT):
            b = t % 2
            if t + 1 < NT:
                P.dma("sp", xt[1 - b], xt[1 - b][:], X, x_d[(t + 1) * 128:(t + 2) * 128, :])
            s = ss[b]
            P.act(junk, junk[:], xt[b], xt[b][:], AF.Square, wr=[s], accum_out=s[:, 0:1])
            P.rstd(s, s[:, 0:1], s[:, 1:2], s[:, 2:3], 1.0 / D_MODEL)
            P.stt("dve", hb, hb[:], xt[b], xt[b][:], s[:, 2:3], gmix, gmix[:], ALU.mult, ALU.mult, rd=[s])
            for kc in range(8):
                P.tr(psT, psT[:, kc, :], hb, hb[:, kc * 128:(kc + 1) * 128], ident, ident[:])
            P.copy("act", hT[b], hT[b][:], psT, psT[:])
            for j in range(11):
                c0 = j * 512
                w = min(512, IN_PROJ - c0)
                pb = pj[cnt % 4]
                cnt += 1
                for kc in range(8):
                    P.mm(pb, pb[:, 0:w], hT[b], hT[b][:, kc, :], Win, Win[:, kc, c0:c0 + w], kc == 0, kc == 7)
                if j < 2:
                    g = qg if j == 0 else kg
                    stg = (qst if j == 0 else kst)[(t // 4) % 2]
                    tf, tg, sh, qbb = tmpf[j], tmpg[j], ssh[j], qb[j]
                    pv = pb[:, :].rearrange("p (h e) -> p h e", h=8)
                    P.act(tf, tf[:], pb, pv, AF.Square)
                    P.op("dve", lambda e, sh=sh, tf=tf: e.tensor_reduce(out=sh[:, 0:8], in_=tf[:], axis=AX.X, op=ALU.add),
                         [tf], [sh])
                    P.rstd(sh, sh[:, 0:8], sh[:, 8:16], sh[:, 16:24], 1.0 / HD)
                    P.tt("dve", tg, tg[:], pb, pv, sh, sh[:, 16:24].unsqueeze(2).to_broadcast([128, 8, 64]), ALU.mult)
                    P.tt("pool", qbb, qbb[:].rearrange("p (h e) -> p h e", h=8), tg, tg[:], g, g[:], ALU.mult)
                    for pr in range(4):
                        P.tr(psQ, psQ[:, pr, :], qbb, qbb[:, pr * 128:(pr + 1) * 128], ident, ident[:])
                    tc0 = (t % 4) * 128
                    P.copy("act", stg, stg[:, :, tc0:tc0 + 128], psQ, psQ[:])
                    if t % 4 == 3 or t == NT - 1:
                        t0 = (t // 4) * 512
                        n = (t % 4 + 1) * 128
                        D_, d_ = (QT, qT_d) if j == 0 else (KT, kT_d)
                        P.dma("pool", D_, d_[:, :, t0:t0 + n], stg, stg[:, :, 0:n])
                elif j == 2:
                    P.copy("act", vb[b], vb[b][:], pb, pb[:])
                    P.dma("pool", V, v_d[t * 128:(t + 1) * 128, :], vb[b], vb[b][:])
                elif j < 6:
                    jj = j - 3
                    P.act(szb[b], szb[b][:, jj * 512:(jj + 1) * 512], pb, pb[:], AF.Silu)
                    if jj == 2:
                        P.dma("pool", SZ, sz_d[t * 128:(t + 1) * 128, :], szb[b], szb[b][:])
                elif j < 10:
                    jj = j - 6
                    P.copy("dve", ub[b], ub[b][:, jj * 512:(jj + 1) * 512], pb, pb[:])
                    if jj == 3:
                        P.dma("pool", U, u_d[3 + t * 128:3 + (t + 1) * 128, :], ub[b], ub[b][:])
                else:
                    d = dts[b]
                    P.tt("dve", d, d[:, 0, :], pb, pb[:, 0:NH_S], dtb, dtb[:], ALU.add)
                    P.stt("dve", d, d[:, 1, :], d, d[:, 0, :], -1.0, d, d[:, 0, :], ALU.mult, ALU.max)
                    P.act(d, d[:, 1, :], d, d[:, 1, :], AF.Exp, scale=-1.0)
                    P.act(d, d[:, 1, :], d, d[:, 1, :], AF.Ln, bias=1.0)
                    P.stt("dve", d, d[:, 2, :], d, d[:, 0, :], 0.0, d, d[:, 1, :], ALU.max, ALU.add)
                    P.dma("pool", DT, dt_d[t * 128:(t + 1) * 128, :], d, d[:, 2, :])
        P.barrier()
        P.emit()


PATTERNS = (1, 4, 16)
B_STOP = 99
C_STOP = 99
C_VAR = 0
HI_LIST = (0, 1)
MASK_ENG = 'pool'


def phase_b(nc, P, SEQ, CST, cst_d, QT, qT_d, KT, kT_d, V, v_d, AT, attnT_d):
    SBW = 2048
    NSB = SEQ // SBW
    with ExitStack() as ph:
        kTw = [P.sb(ph, "kTw%d" % i, [128, 4, SBW], BF16) for i in range(2)]
        qTw = [P.sb(ph, "qTw%d" % i, [128, 4, SBW], BF16) for i in range(1)]
        accn = P.sb(ph, "accn", [64, 8, SBW], F32)
        accd = P.sb(ph, "accd", [64, 8, SBW], F32)
        ast = [P.sb(ph, "ast%d" % i, [64, 8, 512], BF16) for i in range(1)]
        vt = [P.sb(ph, "vt%d" % i, [128, 512], BF16) for i in range(6)]
        pT = [P.sb(ph, "pT%d" % i, [128, 512], BF16) for i in range(3)]
        mask4 = P.sb(ph, "mask4", [128, 512], BF16)
        maskc = P.sb(ph, "maskc", [128, 256], BF16)
        ones_b = P.sb(ph, "ones_b", [128, 64], BF16)
        psS = [P.ps(ph, "psS%d" % i, [128, 2, 512], F32) for i in range(2)]
        psN = [P.ps(ph, "psN%d" % i, [128, 2, 128], F32) for i in range(2)]
        psD = [P.ps(ph, "psD%d" % i, [128, 512], F32) for i in range(2)]

        P.dma("pool", mask4, mask4[:, 0:256], CST, cst_d[:, 256:512])
        P.dma("pool", mask4, mask4[:, 256:512], CST, cst_d[:, 256:512])
        P.dma("pool", maskc, maskc[:, 0:128], CST, cst_d[:, 384:512])
        P.dma("pool", maskc, maskc[:, 128:256], CST, cst_d[:, 384:512])
        P.op("dve", lambda e: e.memset(ones_b[:], 1.0), [], [ones_b])

        vi = 0
        ui = 0
        for sb in range(NSB):
            T0 = sb * SBW
            kc_, qc_ = kTw[sb % 2], qTw[0]
            kp_ = kTw[(sb - 1) % 2]
            if B_STOP not in (-2, -4):
                P.dma("sp", kc_, kc_[:], KT, kT_d[:, :, T0:T0 + SBW])
                P.dma("sp", qc_, qc_[:], QT, qT_d[:, :, T0:T0 + SBW])
            if B_STOP in (-2, -3):
                continue
            for d in PATTERNS:
                span = 128 * d
                first = (d == 1)
                for b in range(SBW // span):
                    for r in range(d):
                        q0 = b * span + r
                        g0 = T0 + q0
                        has_prev = (T0 + b * span) >= span
                        kbs = []
                        if has_prev:
                            vp = vt[vi % 6]
                            vi += 1
                            P.dma("sp", vp, vp[:], V, v_d[g0 - span:g0 - span + (127 * d + 1):d, :])
                            if b == 0:
                                kbs.append((kp_, SBW - span + r, vp))
                            else:
                                kbs.append((kc_, q0 - span, vp))
                        vc = vt[vi % 6]
                        vi += 1
                        P.dma("sp", vc, vc[:], V, v_d[g0:g0 + (127 * d + 1):d, :])
                        kbs.append((kc_, q0, vc))
                        nk = len(kbs)
                        qs = slice(q0, q0 + 127 * d + 1, d)
                        for hp in range(4):
                            S = psS[ui % 2]
                            N = psN[ui % 2]
                            Dn = psD[ui % 2]
                            pt = pT[ui % 3]
                            ui += 1
                            W = 2 * nk * 128
                            for hi in (HI_LIST if B_STOP >= 0 else ()):
                                pl = slice(64 * hi, 64 * hi + 64)
                                for ki, (kb, k0, _) in enumerate(kbs):
                                    c = ki * 128
                                    P.mm(S, S[:, hi, c:c + 128], kb, kb[pl, hp, k0:k0 + 127 * d + 1:d],
                                         qc_, qc_[pl, hp, qs], True, True)
                            if B_STOP < 1:
                                continue
                            P.act(pt, pt[:, 0:W].rearrange("p (h c) -> p h c", h=2), S, S[:, :, 0:nk * 128], AF.Exp)
                            if B_STOP < 2:
                                continue
                            mk = mask4 if nk == 2 else maskc
                            P.tt(MASK_ENG, pt, pt[:, 0:W], pt, pt[:, 0:W], mk, mk[:, 0:W], ALU.mult)
                            if B_STOP < 3:
                                continue
                            for hi in range(2):
                                h = 2 * hp + hi
                                for ki, (kb, k0, vb_) in enumerate(kbs):
                                    c = (hi * nk + ki) * 128
                                    P.mm(N, N[0:64, hi, :], vb_, vb_[:, h * 64:(h + 1) * 64],
                                         pt, pt[:, c:c + 128], ki == 0, ki == nk - 1)
                            if B_STOP < 4:
                                continue
                            P.mm(Dn, Dn[0:64, 0:W], ones_b, ones_b[:, 0:64], pt, pt[:, 0:W], True, True)
                            if B_STOP < 5:
                                continue
                            an = accn[:, 2 * hp:2 * hp + 2, qs]
                            ad = accd[:, 2 * hp:2 * hp + 2, qs]
                            dv = Dn[0:64, 0:W].rearrange("p (h k q) -> p h k q", h=2, k=nk)
                            if first:
                                P.copy("dve", accn, an, N, N[0:64, :, :])
                                P.copy("dve", accd, ad, Dn, dv[:, :, 0, :])
                            else:
                                P.tt("dve", accn, an, accn, an, N, N[0:64, :, :], ALU.add)
                                P.tt("dve", accd, ad, accd, ad, Dn, dv[:, :, 0, :], ALU.add)
                            if nk == 2:
                                P.tt("dve", accd, ad, accd, ad, Dn, dv[:, :, 1, :], ALU.add)
            for c in range(SBW // 512 if B_STOP >= 6 else 0):
                c0 = c * 512
                a_ = ast[0]
                P.op("dve", lambda e, c0=c0: e.reciprocal(out=accd[:, :, c0:c0 + 512], in_=accd[:, :, c0:c0 + 512]), [accd], [accd])
                P.tt("dve", a_, a_[:], accn, accn[:, :, c0:c0 + 512], accd, accd[:, :, c0:c0 + 512], ALU.mult)
                P.dma("pool", AT, attnT_d[:, :, T0 + c0:T0 + c0 + 512], a_, a_[:])
        P.barrier()
        P.emit()


def phase_c(nc, P, SEQ, X, x_d, VEC, vec_d, CST, cst_d, U, u_d, SZ, sz_d, DT, dt_d,
            AT, attnT_d, WOA, woa_d, WOS, wos_d, OUT, out_d):
    NT = SEQ // 128
    with ExitStack() as ph:
        trif = P.sb(ph, "trif", [128, 128], F32)
        identb = P.sb(ph, "identC", [128, 128], BF16)
        onesf = P.sb(ph, "onesfC", [128, 128], F32)
        cw = P.sb(ph, "cw", [128, 4, CONV_DIM], F32)
        cb = P.sb(ph, "cb", [128, CONV_DIM], F32)
        abc = P.sb(ph, "abc", [128, NH_S], F32)
        dbc = P.sb(ph, "dbc", [128, NH_S], F32)
        sgbc = P.sb(ph, "sgbc", [128, SW], F32)
        Woa = P.sb(ph, "Woa", [64, 8, D_MODEL], BF16)
        Wos = P.sb(ph, "Wos", [128, 12, D_MODEL], BF16)
        uk = [P.sb(ph, "uk%d" % k, [128, CONV_DIM], BF16) for k in range(4)]
        tk = [P.sb(ph, "tk%d" % k, [128, CONV_DIM], F32) for k in range(3)]
        xcb = [P.sb(ph, "xcb%d" % i, [128, CONV_DIM], BF16) for i in range(2)]
        szt = P.sb(ph, "szt", [128, SW], BF16)
        dtt = [P.sb(ph, "dtt%d" % i, [128, NH_S], F32) for i in range(2)]
        att = [P.sb(ph, "att%d" % i, [64, 8, 128], BF16) for i in range(2)]
        xt = [P.sb(ph, "xtC%d" % i, [128, D_MODEL], F32) for i in range(2)]
        sm = [P.sb(ph, "smC%d" % i, [128, 8, NH_S], F32) for i in range(2)]
        st2 = P.sb(ph, "st2", [128, 8], F32)
        BCT = P.sb(ph, "BCT", [128, 4, 128], BF16)
        Gm = P.sb(ph, "Gm", [128, 2, 128], F32)
        R4 = [P.sb(ph, "R4%d" % i, [128, 4, 128], F32) for i in range(2)]
        dtmp = [P.sb(ph, "dtmp%d" % i, [128, 4, 128], F32) for i in range(2)]
        E2 = [P.sb(ph, "E2%d" % i, [128, 4, 128], BF16) for i in range(2)]
        Mt = [P.sb(ph, "Mt%d" % i, [128, 4, 128], BF16) for i in range(2)]
        CTs = [P.sb(ph, "CTs%d" % i, [128, 4, 128], BF16) for i in range(2)]
        S = P.sb(ph, "Sst", [128, 2, 768], F32)
        Sb = P.sb(ph, "Sbf", [128, 2, 768], BF16)
        xwb = P.sb(ph, "xwb", [128, SW], BF16)
        tmpD = P.sb(ph, "tmpD", [128, SW], F32)
        ysb = P.sb(ph, "ysb", [128, SW], F32)
        yb = P.sb(ph, "yb", [128, SW], BF16)
        yT = P.sb(ph, "yT", [128, 12, 128], BF16)
        psAG = P.ps(ph, "psAG", [128, 512], F32)
        psBC = P.ps(ph, "psBC", [128, 4, 128], F32)
        psY = P.ps(ph, "psY", [128, SW], F32)
        psSt = P.ps(ph, "psSt", [128, 1024], F32)
        psTr = P.ps(ph, "psTrC", [128, 4, 128], BF16)

        P.dma("sp", trif, trif[:], CST, cst_d[:, 128:256])
        P.dma("pool", identb, identb[:], CST, cst_d[:, 0:128])
        P.op("dve", lambda e: e.memset(onesf[:], 1.0), [], [onesf])
        P.dma("sp", cw, cw[:], VEC, bc_ap(vec_d, VO["conv_w"], [[0, 128], [CONV_DIM, 4], [1, CONV_DIM]]))
        P.dma("sp", cb, cb[:], VEC, bc_ap(vec_d, VO["conv_b"], [[0, 128], [1, CONV_DIM]]))
        P.dma("sp", abc, abc[:], VEC, bc_ap(vec_d, VO["a_log"], [[0, 128], [1, NH_S]]))
        P.dma("sp", dbc, dbc[:], VEC, bc_ap(vec_d, VO["d_skip"], [[0, 128], [1, NH_S]]))
        P.dma("sp", sgbc, sgbc[:], VEC, bc_ap(vec_d, VO["ssm_norm"], [[0, 128], [1, SW]]))
        P.act(abc, abc[:], abc, abc[:], AF.Exp)
        P.ts("dve", abc, abc[:], abc, abc[:], -1.0, None, ALU.mult)
        P.dma("pool", Woa, Woa[:], WOA, woa_d[:, :, :])
        for j in range(12):
            P.dma("pool", Wos, Wos[:, j, :], WOS, wos_d[:, j, :])

        for t in range(NT):
            b = t % 2
            r0 = t * 128
            for k in range(4):
                P.dma("sp", uk[k], uk[k][:], U, u_d[r0 + k:r0 + k + 128, :])
            P.dma("sp", szt, szt[:], SZ, sz_d[r0:r0 + 128, :])
            P.dma("sp", dtt[b], dtt[b][:], DT, dt_d[r0:r0 + 128, :])
            P.dma("sp", att[b], att[b][:], AT, attnT_d[:, :, r0:r0 + 128])
            P.dma("sp", xt[b], xt[b][:], X, x_d[r0:r0 + 128, :])
            P.tt("pool", tk[0], tk[0][:], uk[0], uk[0][:], cw, cw[:, 0, :], ALU.mult)
            P.tt("pool", tk[1], tk[1][:], uk[1], uk[1][:], cw, cw[:, 1, :], ALU.mult)
            P.tt("dve", tk[0], tk[0][:], tk[0], tk[0][:], tk[1], tk[1][:], ALU.add)
            P.tt("pool", tk[2], tk[2][:], uk[2], uk[2][:], cw, cw[:, 2, :], ALU.mult)
            P.tt("dve", tk[0], tk[0][:], tk[0], tk[0][:], tk[2], tk[2][:], ALU.add)
            P.tt("pool", tk[1], tk[1][:], uk[3], uk[3][:], cw, cw[:, 3, :], ALU.mult)
            P.tt("dve", tk[0], tk[0][:], tk[0], tk[0][:], tk[1], tk[1][:], ALU.add)
            P.tt("dve", tk[0], tk[0][:], tk[0], tk[0][:], cb, cb[:], ALU.add)
            xc = xcb[b]
            P.act(xc, xc[:], tk[0], tk[0][:], AF.Silu)
            if C_STOP < 1:
                continue
            m = sm[b]
            d_ = dtt[b]
            P.tt("dve", m, m[:, 0, :], d_, d_[:], abc, abc[:], ALU.mult)
            P.mm(psAG, psAG[:, 0:NH_S], trif, trif[:], m, m[:, 0, :], True, True)
            P.mm(psAG, psAG[:, 32:32 + NH_S], onesf, onesf[:], m, m[:, 0, :], True, True)
            P.copy("dve", m, m[:, 1, :], psAG, psAG[:, 0:NH_S])
            P.copy("dve", m, m[:, 2, :], psAG, psAG[:, 32:32 + NH_S])
            P.tt("dve", m, m[:, 3, :], m, m[:, 2, :], m, m[:, 1, :], ALU.subtract)
            P.act(m, m[:, 3, :], m, m[:, 3, :], AF.Exp)
            P.tt("dve", m, m[:, 3, :], m, m[:, 3, :], d_, d_[:], ALU.mult)
            P.act(m, m[:, 4, :], m, m[:, 2, :], AF.Exp)
            if C_STOP < 2:
                continue
            for j in range(4):
                P.tr(psTr, psTr[:, j, :], xc, xc[:, SW + j * 128:SW + (j + 1) * 128], identb, identb[:])
            P.copy("act", BCT, BCT[:], psTr, psTr[:])
            if C_STOP < 2.3:
                continue
            for g in range(2):
                P.mm(psSt, psSt[:, g * 128:128 + g * 128], BCT, BCT[:, g, :], BCT, BCT[:, 2 + g, :], True, True)
            if C_STOP < 2.6:
                continue
            P.copy("act", Gm, Gm[:], psSt, psSt[:, 0:256].rearrange("p (g l) -> p g l", g=2))
            for g in range(2 if C_STOP != 2.9 else 0):
                P.tt("dve", Gm, Gm[:, g, :], Gm, Gm[:, g, :], trif, trif[:], ALU.mult)
            if C_STOP < 3:
                continue
            for q4 in range(6):
                g = q4 // 3
                i2 = (t * 6 + q4) % 2
                r4, dt4, e2, mt, cts = R4[i2], dtmp[i2], E2[i2], Mt[i2], CTs[i2]
                P.tt("pool", r4, r4[:], m, m[:, 0, 4 * q4:4 * q4 + 4].unsqueeze(2).to_broadcast([128, 4, 128]),
                     trif, trif[:].unsqueeze(1).to_broadcast([128, 4, 128]), ALU.mult)
                P.mm(psBC, psBC[:], onesf, onesf[:], r4, r4[:], True, True)
                if t > 0:
                    P.act(e2, e2[:], psBC, psBC[:], AF.Exp)
                    if C_VAR != 1:
                        P.tt("pool", cts, cts[:], e2, e2[:], BCT, BCT[:, 2 + g, :].unsqueeze(1).to_broadcast([128, 4, 128]), ALU.mult)
                for j in range(4):
                    h = 4 * q4 + j
                    P.ts("dve", dt4, dt4[:, j, :], psBC, psBC[:, j, :], m[:, 1, h:h + 1], 0.0, ALU.subtract, ALU.min,
                         rd=[m] + ([e2] if t > 0 else []))
                P.act(dt4, dt4[:], dt4, dt4[:], AF.Exp)
                for j in range(4):
                    h = 4 * q4 + j
                    P.stt("dve", mt, mt[:, j, :], dt4, dt4[:, j, :], d_[:, h:h + 1], Gm, Gm[:, g, :], ALU.mult, ALU.mult, rd=[d_])
                for j in range(4):
                    h = 4 * q4 + j
                    hl = h - 12 * g
                    P.mm(psY, psY[:, h * 64:(h + 1) * 64], mt, mt[:, j, :], xc, xc[:, h * 64:(h + 1) * 64], True, t == 0)
                    if t > 0 and C_VAR != 2:
                        P.mm(psY, psY[:, h * 64:(h + 1) * 64], cts, cts[:, j, :], Sb, Sb[:, g, hl * 64:(hl + 1) * 64], False, True)
            if C_STOP < 4:
                continue
            xs3 = xc[:, 0:SW].rearrange("p (h e) -> p h e", h=NH_S)
            P.tt("pool", tmpD, tmpD[:].rearrange("p (h e) -> p h e", h=NH_S), xc, xs3,
                 dbc, dbc[:].unsqueeze(2).to_broadcast([128, NH_S, 64]), ALU.mult)
            P.tt("dve", ysb, ysb[:], psY, psY[:], tmpD, tmpD[:], ALU.add)
            P.tt("dve", ysb, ysb[:], ysb, ysb[:], szt, szt[:], ALU.mult)
            for g in range(2):
                P.act(tmpD, tmpD[:, g * 768:(g + 1) * 768], ysb, ysb[:, g * 768:(g + 1) * 768], AF.Square,
                      wr=[st2], accum_out=st2[:, g:g + 1])
            P.rstd(st2, st2[:, 0:2], st2[:, 2:4], st2[:, 4:6], 1.0 / 768)
            for g in range(2):
                P.stt("dve", yb, yb[:, g * 768:(g + 1) * 768], ysb, ysb[:, g * 768:(g + 1) * 768], st2[:, 4 + g:5 + g],
                      sgbc, sgbc[:, g * 768:(g + 1) * 768], ALU.mult, ALU.mult, rd=[st2])
            if C_STOP < 5:
                continue
            P.tt("dve", xwb, xwb[:].rearrange("p (h e) -> p h e", h=NH_S), xc, xs3,
                 m, m[:, 3, :].unsqueeze(2).to_broadcast([128, NH_S, 64]), ALU.mult)
            for g in range(2):
                bs = xc[:, SW + g * 128:SW + (g + 1) * 128]
                P.mm(psSt, psSt[:, 0:512], xc, bs, xwb, xwb[:, g * 768:g * 768 + 512], True, True)
                P.mm(psSt, psSt[:, 512:768], xc, bs, xwb, xwb[:, g * 768 + 512:(g + 1) * 768], True, True)
                if t == 0:
                    P.copy("dve", S, S[:, g, :], psSt, psSt[:, 0:768])
                else:
                    sg3 = S[:, g, :].rearrange("p (h e) -> p h e", h=12)
                    P.tt("dve", S, sg3, S, sg3, m, m[:, 4, 12 * g:12 * g + 12].unsqueeze(2).to_broadcast([128, 12, 64]), ALU.mult)
                    P.tt("dve", S, S[:, g, :], S, S[:, g, :], psSt, psSt[:, 0:768], ALU.add)
            P.copy("act", Sb, Sb[:], S, S[:])
            if C_STOP < 6:
                continue
            for i in range(3):
                for j in range(4):
                    c = 4 * i + j
                    P.tr(psTr, psTr[:, j, :], yb, yb[:, c * 128:(c + 1) * 128], identb, identb[:])
                P.copy("act", yT, yT[:, 4 * i:4 * i + 4, :], psTr, psTr[:])
            a_ = att[b]
            for hf in range(2):
                o = psY[:, hf * 512:(hf + 1) * 512]
                for h in range(8):
                    P.mm(psY, o, a_, a_[0:64, h, :], Woa, Woa[0:64, h, hf * 512:(hf + 1) * 512], h == 0, False)
                for j in range(12):
                    P.mm(psY, o, yT, yT[:, j, :], Wos, Wos[:, j, hf * 512:(hf + 1) * 512], False, j == 11)
            P.tt("dve", xt[b], xt[b][:], xt[b], xt[b][:], psY, psY[:, 0:D_MODEL], ALU.add)
            P.dma("pool", OUT, out_d[r0:r0 + 128, :], xt[b], xt[b][:])
        P.barrier()
        P.emit()


def norm_T(P, xt_b, xt_ap, gbc, s, junk, hb, psT, hT, ident):
    P.act(junk, junk[:], xt_b, xt_ap, AF.Square, wr=[s], accum_out=s[:, 0:1])
    P.rstd(s, s[:, 0:1], s[:, 1:2], s[:, 2:3], 1.0 / D_MODEL)
    P.stt("dve", hb, hb[:], xt_b, xt_ap, s[:, 2:3], gbc, gbc[:], ALU.mult, ALU.mult, rd=[s])
    for kc in range(8):
        P.tr(psT, psT[:, kc, :], hb, hb[:, kc * 128:(kc + 1) * 128], ident, ident[:])
    P.copy("act", hT, hT[:], psT, psT[:])


def phase_d1(nc, P, SEQ, VEC, vec_d, CST, cst_d, WG, wg_d, WU, wu_d, WD, wd_d, OUT, out_d):
    NT = SEQ // 128
    with ExitStack() as ph:
        Wg = P.sb(ph, "Wg", [128, 8, D_FF], BF16)
        Wu = P.sb(ph, "Wu", [128, 8, D_FF], BF16)
        Wd = P.sb(ph, "Wd", [128, 22, D_MODEL], BF16)
        ident = P.sb(ph, "identD", [128, 128], BF16)
        gbc = P.sb(ph, "gffn", [128, D_MODEL], F32)
        xt = [P.sb(ph, "xtD%d" % i, [128, D_MODEL], F32) for i in range(2)]
        junk = P.sb(ph, "junkD", [128, D_MODEL], F32)
        ss = [P.sb(ph, "ssD%d" % i, [128, 8], F32) for i in range(2)]
        hb = P.sb(ph, "hbD", [128, D_MODEL], BF16)
        hT = [P.sb(ph, "hTD%d" % i, [128, 8, 128], BF16) for i in range(2)]
        sg = [P.sb(ph, "sgD%d" % i, [128, 512], F32) for i in range(2)]
        ab = P.sb(ph, "abD", [128, D_FF], BF16)
        aT = P.sb(ph, "aTD", [128, 22, 128], BF16)
        psT = P.ps(ph, "psTD", [128, 8, 128], BF16)
        psG = [P.ps(ph, "psGD%d" % i, [128, 512], F32) for i in range(2)]
        psU = [P.ps(ph, "psUD%d" % i, [128, 512], F32) for i in range(2)]
        psO = P.ps(ph, "psOD", [128, D_MODEL], F32)
        for fc in range(22):
            P.dma("pool", Wg, Wg[:, :, fc * 128:(fc + 1) * 128], WG, wg_d[fc])
            P.dma("pool", Wu, Wu[:, :, fc * 128:(fc + 1) * 128], WU, wu_d[fc])
        for j in range(0, 22, 2):
            P.dma("pool", Wd, Wd[:, j:j + 2, :], WD, wd_d[:, j:j + 2, :])
        P.dma("pool", ident, ident[:], CST, cst_d[:, 0:128])
        P.dma("sp", gbc, gbc[:], VEC, bc_ap(vec_d, VO["ffn_norm"], [[0, 128], [1, D_MODEL]]))
        P.dma("sp", xt[0], xt[0][:], OUT, out_d[0:128, :])
        cnt = 0
        for t in range(NT):
            b = t % 2
            if t + 1 < NT:
                P.dma("sp", xt[1 - b], xt[1 - b][:], OUT, out_d[(t + 1) * 128:(t + 2) * 128, :])
            norm_T(P, xt[b], xt[b][:], gbc, ss[b], junk, hb, psT, hT[b], ident)
            for fc in range(6):
                f0 = fc * 512
                w = min(512, D_FF - f0)
                G_, U_, sg_ = psG[cnt % 2], psU[cnt % 2], sg[cnt % 2]
                cnt += 1
                for kc in range(8):
                    P.mm(G_, G_[:, 0:w], hT[b], hT[b][:, kc, :], Wg, Wg[:, kc, f0:f0 + w], kc == 0, kc == 7)
                for kc in range(8):
                    P.mm(U_, U_[:, 0:w], hT[b], hT[b][:, kc, :], Wu, Wu[:, kc, f0:f0 + w], kc == 0, kc == 7)
                P.act(sg_, sg_[:, 0:w], G_, G_[:, 0:w], AF.Silu)
                P.tt("dve", ab, ab[:, f0:f0 + w], sg_, sg_[:, 0:w], U_, U_[:, 0:w], ALU.mult)
            for i in range(3):
                n = 8 if i < 2 else 6
                for j in range(n):
                    c = 8 * i + j
                    P.tr(psT, psT[:, j, :], ab, ab[:, c * 128:(c + 1) * 128], ident, ident[:])
                P.copy("act", aT, aT[:, 8 * i:8 * i + n, :], psT, psT[:, 0:n, :])
            for hf in range(2):
                for j in range(22):
                    P.mm(psO, psO[:, hf * 512:(hf + 1) * 512], aT, aT[:, j, :], Wd, Wd[:, j, hf * 512:(hf + 1) * 512], j == 0, j == 21)
            P.tt("dve", xt[b], xt[b][:], xt[b], xt[b][:], psO, psO[:], ALU.add)
            P.dma("pool", OUT, out_d[t * 128:(t + 1) * 128, :], xt[b], xt[b][:])
        P.barrier()
        P.emit()


def phase_d2(nc, P, SEQ, VEC, vec_d, CST, cst_d, PIN, p_d, WPG, wpg_d, WPLE, wple_d, OUT, out_d):
    NT = SEQ // 128
    with ExitStack() as ph:
        Wpg = P.sb(ph, "Wpg", [128, 8, D_MODEL], BF16)
        Wpl = P.sb(ph, "Wpl", [128, 2, D_MODEL], BF16)
        ident = P.sb(ph, "identE", [128, 128], BF16)
        gbc = P.sb(ph, "gpg", [128, D_MODEL], F32)
        bbc = P.sb(ph, "bpg", [128, D_MODEL], F32)
        pgbc = P.sb(ph, "gple", [128, D_MODEL], F32)
        xt = [P.sb(ph, "xtE%d" % i, [128, D_MODEL], F32) for i in range(2)]
        pt = [P.sb(ph, "ptE%d" % i, [128, PLE], F32) for i in range(2)]
        pb = P.sb(ph, "pbE", [128, PLE], BF16)
        pT = P.sb(ph, "pTE", [128, 2, 128], BF16)
        junk = P.sb(ph, "junkE", [128, D_MODEL], F32)
        ss = [P.sb(ph, "ssE%d" % i, [128, 8], F32) for i in range(2)]
        hb = P.sb(ph, "hbE", [128, D_MODEL], BF16)
        hT = [P.sb(ph, "hTE%d" % i, [128, 8, 128], BF16) for i in range(2)]
        gt = P.sb(ph, "gtE", [128, D_MODEL], F32)
        en = P.sb(ph, "enE", [128, D_MODEL], F32)
        psT = P.ps(ph, "psTE", [128, 8, 128], BF16)
        psGt = P.ps(ph, "psGt", [128, D_MODEL], F32)
        psE = P.ps(ph, "psE", [128, D_MODEL], F32)
        for kc in range(0, 8, 2):
            P.dma("pool", Wpg, Wpg[:, kc:kc + 2, :], WPG, wpg_d[:, kc:kc + 2, :])
        P.dma("pool", Wpl, Wpl[:], WPLE, wple_d[:, :, :])
        P.dma("pool", ident, ident[:], CST, cst_d[:, 0:128])
        P.dma("sp", gbc, gbc[:], VEC, bc_ap(vec_d, VO["ple_gate_norm"], [[0, 128], [1, D_MODEL]]))
        P.dma("sp", bbc, bbc[:], VEC, bc_ap(vec_d, VO["b_ple_gate"], [[0, 128], [1, D_MODEL]]))
        P.dma("sp", pgbc, pgbc[:], VEC, bc_ap(vec_d, VO["ple_norm"], [[0, 128], [1, D_MODEL]]))
        for t in range(NT):
            b = t % 2
            r0 = t * 128
            P.dma("sp", xt[b], xt[b][:], OUT, out_d[r0:r0 + 128, :])
            P.dma("sp", pt[b], pt[b][:], PIN, p_d[r0:r0 + 128, :])
            s = ss[b]
            norm_T(P, xt[b], xt[b][:], gbc, s, junk, hb, psT, hT[b], ident)
            for hf in range(2):
                for kc in range(8):
                    P.mm(psGt, psGt[:, hf * 512:(hf + 1) * 512], hT[b], hT[b][:, kc, :], Wpg, Wpg[:, kc, hf * 512:(hf + 1) * 512], kc == 0, kc == 7)
            P.copy("dve", pb, pb[:], pt[b], pt[b][:])
            for j in range(2):
                P.tr(psT, psT[:, j, :], pb, pb[:, j * 128:(j + 1) * 128], ident, ident[:])
            P.copy("act", pT, pT[:], psT, psT[:, 0:2, :])
            for hf in range(2):
                for j in range(2):
                    P.mm(psE, psE[:, hf * 512:(hf + 1) * 512], pT, pT[:, j, :], Wpl, Wpl[:, j, hf * 512:(hf + 1) * 512], j == 0, j == 1)
            P.tt("dve", gt, gt[:], psGt, psGt[:], bbc, bbc[:], ALU.add)
            P.act(gt, gt[:], gt, gt[:], AF.Sigmoid)
            P.act(junk, junk[:], psE, psE[:], AF.Square, wr=[s], accum_out=s[:, 4:5])
            P.rstd(s, s[:, 4:5], s[:, 5:6], s[:, 6:7], 1.0 / D_MODEL)
            P.stt("dve", en, en[:], psE, psE[:], s[:, 6:7], pgbc, pgbc[:], ALU.mult, ALU.mult, rd=[s])
            P.tt("pool", en, en[:], en, en[:], gt, gt[:], ALU.mult)
            P.tt("dve", xt[b], xt[b][:], xt[b], xt[b][:], en, en[:], ALU.add)
            P.dma("pool", OUT, out_d[r0:r0 + 128, :], xt[b], xt[b][:])
        P.barrier()
        P.emit()


def make_consts():
    c = np.zeros((128, 1024), np.float32)
    i = np.arange(128)
    c[:, 0:128] = np.eye(128, dtype=np.float32)
    c[:, 128:256] = (i[:, None] <= i[None, :])
    c[:, 256:384] = (i[:, None] >= i[None, :])
    c[:, 384:512] = (i[:, None] <= i[None, :])
    return c


def pack_inputs(inp, SEQ=SEQ_FULL):
    f = lambda a: np.ascontiguousarray(np.asarray(a, dtype=np.float32))
    vec = np.zeros((1, NVEC), np.float32)
    for k, o in VO.items():
        a = f(inp[k][0]).reshape(-1)
        vec[0, o:o + a.size] = a
    shared = dict(
        w_in=f(f(inp["w_in"][0]).reshape(8, 128, IN_PROJ).transpose(1, 0, 2)),
        w_out_a=f(f(inp["w_out"][0])[0:AW].reshape(8, 64, D_MODEL).transpose(1, 0, 2)),
        w_out_s=f(f(inp["w_out"][0])[AW:].reshape(12, 128, D_MODEL).transpose(1, 0, 2)),
        w_g=f(f(inp["w_ffn_gate"][0]).reshape(8, 128, 22, 128).transpose(2, 1, 0, 3)),
        w_u=f(f(inp["w_ffn_up"][0]).reshape(8, 128, 22, 128).transpose(2, 1, 0, 3)),
        w_d=f(f(inp["w_ffn_down"][0]).reshape(22, 128, D_MODEL).transpose(1, 0, 2)),
        w_pg=f(f(inp["w_ple_gate"][0]).reshape(8, 128, D_MODEL).transpose(1, 0, 2)),
        w_ple=f(f(inp["w_ple"][0]).reshape(2, 128, D_MODEL).transpose(1, 0, 2)),
        vecs=vec,
        consts=make_consts(),
    )
    maps = []
    for b in range(inp["x"].shape[0]):
        m = dict(shared)
        m["x"] = f(inp["x"][b][:SEQ])
        m["p"] = f(inp["p"][0][b][:SEQ])
        maps.append(m)
    return maps


_NC_CACHE = {}


def kernel(**inputs):
    maps = pack_inputs(inputs)
    if "nc" not in _NC_CACHE:
        _NC_CACHE["nc"] = build()
    nc = _NC_CACHE["nc"]
    res = run_bass_kernel_spmd(nc, maps, core_ids=list(range(len(maps))))
    return np.stack([np.asarray(r["out"], dtype=np.float32) for r in res.results], axis=0)
```

```python
from contextlib import ExitStack
import numpy as np
import concourse.bass as bass
import concourse.mybir as mybir
from concourse.bass_utils import run_bass_kernel_spmd

F32 = mybir.dt.float32
BF16 = mybir.dt.bfloat16
AF = mybir.ActivationFunctionType
ALU = mybir.AluOpType
AX = mybir.AxisListType

ENG = ["pe", "act", "dve", "pool", "sp"]

D_MODEL = 1024
SEQ_FULL = 8192
NH_A = 8
HD = 64
AW = 512
NH_S = 24
SW = 1536
CONV_DIM = 2048
NSTATE = 128
IN_PROJ = 5144
D_FF = 2816
PLE = 256
EPS = 1e-6


class Buf:
    def __init__(self, t, name, space):
        self.t = t
        self.name = name
        self.space = space
        self.w = {}
        self.r = {}

    def __getitem__(self, k):
        return self.t[k]


class Prog:
    def __init__(self, nc, stack):
        self.nc = nc
        self.stack = stack
        self.q = {e: [] for e in ENG}
        self.emitted = {e: 0 for e in ENG}
        self.known = {e: {} for e in ENG}
        self.semval = {e: 0 for e in ENG}
        self.resolved = {e: {} for e in ENG}
        self.esem = {e: stack.enter_context(nc.semaphore("es_" + e)) for e in ENG}
        self.dsem = {}
        self.dcount = {}

    def sb(self, stack, name, shape, dtype):
        t = stack.enter_context(self.nc.sbuf_tensor(name, list(shape), dtype))
        return Buf(t, name, "sb")

    def ps(self, stack, name, shape, dtype=F32):
        t = stack.enter_context(self.nc.psum_tensor(name, list(shape), dtype))
        return Buf(t, name, "ps")

    def dr(self, t, name):
        return Buf(t, name, "dr")

    def _collect(self, e, reads, writes, skipkey=None):
        waits = {}

        def need(k, v):
            if k == skipkey:
                return
            if k == "pe" and e == "pe":
                return
            if self.known[e].get(k, -1) >= v:
                return
            if waits.get(k, -1) < v:
                waits[k] = v

        for b in reads:
            for k, v in b.w.items():
                need(k, v)
        for b in writes:
            for k, v in b.w.items():
                need(k, v)
            for k, v in b.r.items():
                need(k, v)
        for k, v in waits.items():
            self.known[e][k] = v
            if k in self.q:
                self.q[k][v]["inc"] = True
        return list(waits.items())

    def op(self, e, fn, reads=(), writes=()):
        idx = len(self.q[e])
        waits = self._collect(e, reads, writes)
        self.q[e].append(dict(fn=fn, waits=waits, inc=False, dkey=None))
        for b in reads:
            b.r[e] = idx
        for b in writes:
            b.w = {e: idx}
            b.r = {}
        return idx

    def dma(self, e, dst, dst_ap, src, src_ap, **kw):
        if dst.space != "dr":
            key = ("in", dst.name)
        elif src.space != "dr":
            key = ("out", src.name)
        else:
            key = ("dd", dst.name)
        waits = self._collect(e, [src], [dst], skipkey=key)
        cnt = self.dcount.get(key, 0) + 1
        self.dcount[key] = cnt
        val = 16 * cnt

        def fn(eng, dst_ap=dst_ap, src_ap=src_ap, kw=kw):
            return eng.dma_start(out=dst_ap, in_=src_ap, **kw)

        self.q[e].append(dict(fn=fn, waits=waits, inc=False, dkey=key))
        src.r[key] = val
        keep = {k: v for k, v in dst.w.items() if k == key}
        dst.w = keep
        dst.w[key] = val
        dst.r = {}

    def barrier(self):
        for e in ENG:
            waits = {}
            for k in ENG:
                if k == e or not self.q[k] or k == "sp":
                    continue
                v = -1
                for i in range(len(self.q[k]) - 1, -1, -1):
                    if self.q[k][i]["fn"] is not None and self.q[k][i]["dkey"] is None:
                        v = i
                        break
                if v < 0:
                    continue
                if self.known[e].get(k, -1) < v:
                    waits[k] = v
            for k, c in self.dcount.items():
                v = 16 * c
                if self.known[e].get(k, -1) < v:
                    waits[k] = v
            for k, v in waits.items():
                self.known[e][k] = v
                if k in self.q:
                    self.q[k][v]["inc"] = True
            self.q[e].append(dict(fn=None, waits=list(waits.items()), inc=False, dkey=None))

    def _sem(self, k):
        if k in self.esem:
            return self.esem[k]
        if k not in self.dsem:
            self.dsem[k] = self.stack.enter_context(
                self.nc.semaphore("ds%d" % len(self.dsem)))
        return self.dsem[k]

    def emit(self):
        nc = self.nc
        for e in ENG:
            for i in range(self.emitted[e], len(self.q[e])):
                r = self.q[e][i]
                if r["inc"] and r["fn"] is not None and r["dkey"] is None:
                    self.semval[e] += 1
                    self.resolved[e][i] = self.semval[e]
        for k in list(self.dcount):
            self._sem(k)

        def run(e, eng):
            recs = self.q[e]
            for i in range(self.emitted[e], len(recs)):
                rec = recs[i]
                for k, v in rec["waits"]:
                    if k in self.esem:
                        v = self.resolved[k][v]
                    eng.wait_ge(self._sem(k), v)
                if rec["fn"] is None:
                    continue
                ins = rec["fn"](eng)
                if rec["dkey"] is not None:
                    ins.then_inc(self._sem(rec["dkey"]), 16)
                elif rec["inc"]:
                    ins.then_inc(self.esem[e], 1)
            self.emitted[e] = len(recs)

        with nc.Block() as block:
            @block.tensor
            def _(eng):
                run("pe", eng)

            @block.scalar
            def _(eng):
                run("act", eng)

            @block.vector
            def _(eng):
                run("dve", eng)

            @block.gpsimd
            def _(eng):
                run("pool", eng)

            @block.sync
            def _(eng):
                run("sp", eng)

    def act(self, ob, o, ib, i, func, rd=(), wr=(), **kw):
        self.op("act", lambda e: e.activation(out=o, in_=i, func=func, **kw),
                [ib, *rd], [ob, *wr])

    def tt(self, eng, ob, o, ab, a, bb, b, op):
        self.op(eng, lambda e: e.tensor_tensor(out=o, in0=a, in1=b, op=op), [ab, bb], [ob])

    def ts(self, eng, ob, o, ab, a, s1, s2, op0, op1=None, rd=()):
        if op1 is None:
            self.op(eng, lambda e: e.tensor_scalar(out=o, in0=a, scalar1=s1, scalar2=None, op0=op0),
                    [ab, *rd], [ob])
        else:
            self.op(eng, lambda e: e.tensor_scalar(out=o, in0=a, scalar1=s1, scalar2=s2, op0=op0, op1=op1),
                    [ab, *rd], [ob])

    def stt(self, eng, ob, o, ab, a, s, bb, b, op0, op1, rd=(), wr=(), **kw):
        self.op(eng, lambda e: e.scalar_tensor_tensor(out=o, in0=a, scalar=s, in1=b, op0=op0, op1=op1, **kw),
                [ab, bb, *rd], [ob, *wr])

    def copy(self, eng, ob, o, ib, i):
        if eng == "act":
            self.op("act", lambda e: e.activation(out=o, in_=i, func=AF.Copy), [ib], [ob])
        else:
            self.op(eng, lambda e: e.tensor_copy(out=o, in_=i), [ib], [ob])

    def mm(self, ob, o, lb, l, rb, r, start, stop):
        self.op("pe", lambda e: e.matmul(o, lhsT=l, rhs=r, start=start, stop=stop), [lb, rb], [ob])

    def tr(self, ob, o, ib, i, idb, idap):
        self.op("pe", lambda e: e.transpose(o, i, idap), [ib, idb], [ob])

    def rstd(self, ssb, src, tmp, dst, inv_n):
        self.ts("dve", ssb, tmp, ssb, src, inv_n, EPS, ALU.mult, ALU.add)
        self.act(ssb, tmp, ssb, tmp, AF.Sqrt)
        self.op("dve", lambda e: e.reciprocal(out=dst, in_=tmp), [ssb], [ssb])


def bc_ap(handle, offset, dims):
    return bass.AP(tensor=handle, offset=offset, ap=[list(d) for d in dims])


def build(SEQ=SEQ_FULL, debug=False, phases="ABCD"):
    nc = bass.Bass("TRN2", target_bir_lowering=False)
    NT = SEQ // 128
    skind = "ExternalOutput" if debug else "Internal"

    def din(name, shape):
        return nc.dram_tensor(name, list(shape), F32, kind="ExternalInput")

    x_d = din("x", [SEQ, D_MODEL])
    p_d = din("p", [SEQ, PLE])
    win_d = din("w_in", [128, 8, IN_PROJ])
    wout_a_d = din("w_out_a", [64, 8, D_MODEL])
    wout_s_d = din("w_out_s", [128, 12, D_MODEL])
    wg_d = din("w_g", [22, 128, 8, 128])
    wu_d = din("w_u", [22, 128, 8, 128])
    wd_d = din("w_d", [128, 22, D_MODEL])
    wpg_d = din("w_pg", [128, 8, D_MODEL])
    wple_d = din("w_ple", [128, 2, D_MODEL])
    vec_d = din("vecs", [1, 17408])
    cst_d = din("consts", [128, 1024])
    out_d = nc.dram_tensor("out", [SEQ, D_MODEL], F32, kind="ExternalOutput")

    qT_d = nc.dram_tensor("qT_s", [128, 4, SEQ], BF16, kind=skind)
    kT_d = nc.dram_tensor("kT_s", [128, 4, SEQ], BF16, kind=skind)
    v_d = nc.dram_tensor("v_s", [SEQ, AW], BF16, kind=skind)
    u_d = nc.dram_tensor("u_s", [SEQ + 3, CONV_DIM], BF16, kind=skind)
    sz_d = nc.dram_tensor("sz_s", [SEQ, SW], BF16, kind=skind)
    dt_d = nc.dram_tensor("dt_s", [SEQ, NH_S], F32, kind=skind)
    attnT_d = nc.dram_tensor("attnT_s", [64, 8, SEQ], BF16, kind=skind)

    with ExitStack() as st:
        P = Prog(nc, st)
        X = P.dr(x_d, "x")
        WIN = P.dr(win_d, "w_in")
        VEC = P.dr(vec_d, "vecs")
        CST = P.dr(cst_d, "consts")
        QT = P.dr(qT_d, "qT_s")
        KT = P.dr(kT_d, "kT_s")
        V = P.dr(v_d, "v_s")
        U = P.dr(u_d, "u_s")
        SZ = P.dr(sz_d, "sz_s")
        DT = P.dr(dt_d, "dt_s")

        AT = P.dr(attnT_d, "attnT_s")
        if "A" in phases:
            phase_a(nc, P, SEQ, X, x_d, WIN, win_d, VEC, vec_d, CST, cst_d,
                    QT, qT_d, KT, kT_d, V, v_d, U, u_d, SZ, sz_d, DT, dt_d)
        if "B" in phases:
            phase_b(nc, P, SEQ, CST, cst_d, QT, qT_d, KT, kT_d, V, v_d, AT, attnT_d)
        OUT = P.dr(out_d, "out")
        if "C" in phases:
            phase_c(nc, P, SEQ, X, x_d, VEC, vec_d, CST, cst_d, U, u_d, SZ, sz_d, DT, dt_d,
                    AT, attnT_d, P.dr(wout_a_d, "w_out_a"), wout_a_d, P.dr(wout_s_d, "w_out_s"), wout_s_d,
                    OUT, out_d)
        if "D" in phases:
            phase_d1(nc, P, SEQ, VEC, vec_d, CST, cst_d, P.dr(wg_d, "w_g"), wg_d, P.dr(wu_d, "w_u"), wu_d,
                     P.dr(wd_d, "w_d"), wd_d, OUT, out_d)
            phase_d2(nc, P, SEQ, VEC, vec_d, CST, cst_d, P.dr(p_d, "p"), p_d, P.dr(wpg_d, "w_pg"), wpg_d,
                     P.dr(wple_d, "w_ple"), wple_d, OUT, out_d)
    return nc


VO = dict(mix_norm=0, q_norm=1024, k_norm=1088, conv_w=1152, conv_b=1152 + 8192,
          dt_bias=11392, a_log=11416, d_skip=11440, ssm_norm=11464, ffn_norm=13000,
          ple_gate_norm=14024, b_ple_gate=15048)
VO["ple_norm"] = 16072
NVEC = 17408


def phase_a(nc, P, SEQ, X, x_d, WIN, win_d, VEC, vec_d, CST, cst_d,
            QT, qT_d, KT, kT_d, V, v_d, U, u_d, SZ, sz_d, DT, dt_d):
    NT = SEQ // 128
    with ExitStack() as ph:
        Win = P.sb(ph, "Win", [128, 8, IN_PROJ], BF16)
        ident = P.sb(ph, "identA", [128, 128], BF16)
        gmix = P.sb(ph, "gmix", [128, D_MODEL], F32)
        qg = P.sb(ph, "qg", [128, 8, 64], F32)
        kg = P.sb(ph, "kg", [128, 8, 64], F32)
        dtb = P.sb(ph, "dtb", [128, NH_S], F32)
        zer = P.sb(ph, "zer", [4, CONV_DIM], BF16)
        xt = [P.sb(ph, "xt%d" % i, [128, D_MODEL], F32) for i in range(2)]
        junk = P.sb(ph, "junkA", [128, D_MODEL], F32)
        ss = [P.sb(ph, "ssA%d" % i, [128, 8], F32) for i in range(2)]
        hb = P.sb(ph, "hb", [128, D_MODEL], BF16)
        hT = [P.sb(ph, "hT%d" % i, [128, 8, 128], BF16) for i in range(2)]
        tmpf = [P.sb(ph, "tmpf%d" % i, [128, 8, 64], F32) for i in range(2)]
        tmpg = [P.sb(ph, "tmpg%d" % i, [128, 8, 64], F32) for i in range(2)]
        ssh = [P.sb(ph, "ssh%d" % i, [128, 24], F32) for i in range(2)]
        qb = [P.sb(ph, "qb%d" % i, [128, 512], BF16) for i in range(2)]
        qst = [P.sb(ph, "qst%d" % i, [128, 4, 512], BF16) for i in range(2)]
        kst = [P.sb(ph, "kst%d" % i, [128, 4, 512], BF16) for i in range(2)]
        vb = [P.sb(ph, "vb%d" % i, [128, 512], BF16) for i in range(2)]
        szb = [P.sb(ph, "szb%d" % i, [128, SW], BF16) for i in range(2)]
        ub = [P.sb(ph, "ub%d" % i, [128, CONV_DIM], BF16) for i in range(2)]
        dts = [P.sb(ph, "dts%d" % i, [128, 4, NH_S], F32) for i in range(2)]
        psT = P.ps(ph, "psT", [128, 8, 128], BF16)
        psQ = P.ps(ph, "psQ", [128, 4, 128], BF16)
        pj = [P.ps(ph, "pj%d" % i, [128, 512], F32) for i in range(4)]

        for kc in range(8):
            P.dma("pool", Win, Win[:, kc, :], WIN, win_d[:, kc, :])
        P.dma("pool", ident, ident[:], CST, cst_d[:, 0:128])
        P.dma("sp", gmix, gmix[:], VEC, bc_ap(vec_d, VO["mix_norm"], [[0, 128], [1, D_MODEL]]))
        P.dma("sp", qg, qg[:], VEC, bc_ap(vec_d, VO["q_norm"], [[0, 128], [0, 8], [1, 64]]))
        P.dma("sp", kg, kg[:], VEC, bc_ap(vec_d, VO["k_norm"], [[0, 128], [0, 8], [1, 64]]))
        P.dma("sp", dtb, dtb[:], VEC, bc_ap(vec_d, VO["dt_bias"], [[0, 128], [1, NH_S]]))
        P.ts("dve", qg, qg[:], qg, qg[:], HD ** -0.5, None, ALU.mult)
        P.op("dve", lambda e: e.memset(zer[:], 0.0), [], [zer])
        P.dma("pool", U, u_d[0:3, :], zer, zer[0:3, :])

        def norm_front(t):
            b = t % 2
            s_ = ss[b]
            P.act(junk, junk[:], xt[b], xt[b][:], AF.Square, wr=[s_], accum_out=s_[:, 0:1])
            P.rstd(s_, s_[:, 0:1], s_[:, 1:2], s_[:, 2:3], 1.0 / D_MODEL)
            P.stt("dve", hb, hb[:], xt[b], xt[b][:], s_[:, 2:3], gmix, gmix[:], ALU.mult, ALU.mult, rd=[s_])
            if t + 2 < NT:
                P.dma("sp", xt[b], xt[b][:], X, x_d[(t + 2) * 128:(t + 3) * 128, :])

        def norm_back(t):
            b = t % 2
            for kc in range(8):
                P.tr(psT, psT[:, kc, :], hb, hb[:, kc * 128:(kc + 1) * 128], ident, ident[:])
            P.copy("act", hT[b], hT[b][:], psT, psT[:])

        def qk_back(t, j):
            qbb = qb[j]
            stg = (qst if j == 0 else kst)[(t // 4) % 2]
            for pr in range(4):
                P.tr(psQ, psQ[:, pr, :], qbb, qbb[:, pr * 128:(pr + 1) * 128], ident, ident[:])
            tc0 = (t % 4) * 128
            P.copy("act", stg, stg[:, :, tc0:tc0 + 128], psQ, psQ[:])
            if t % 4 == 3 or t == NT - 1:
                t0 = (t // 4) * 512
                n = (t % 4 + 1) * 128
                D_, d_ = (QT, qT_d) if j == 0 else (KT, kT_d)
                P.dma("pool", D_, d_[:, :, t0:t0 + n], stg, stg[:, :, 0:n])

        P.dma("sp", xt[0], xt[0][:], X, x_d[0:128, :])
        if NT > 1:
            P.dma("sp", xt[1], xt[1][:], X, x_d[128:256, :])
        norm_front(0)
        norm_back(0)
        cnt = 0
        for t in range(NT):
            b = t % 2
            for j in range(11):
                c0 = j * 512
                w = min(512, IN_PROJ - c0)
                pb = pj[cnt % 4]
                cnt += 1
                for kc in range(8):
                    P.mm(pb, pb[:, 0:w], hT[b], hT[b][:, kc, :], Win, Win[:, kc, c0:c0 + w], kc == 0, kc == 7)
                if j < 2:
                    g = qg if j == 0 else kg
                    tf, tg, sh, qbb = tmpf[j], tmpg[j], ssh[j], qb[j]
                    pv = pb[:, :].rearrange("p (h e) -> p h e", h=8)
                    P.act(tf, tf[:], pb, pv, AF.Square)
                    P.op("dve", lambda e, sh=sh, tf=tf: e.tensor_reduce(out=sh[:, 0:8], in_=tf[:], axis=AX.X, op=ALU.add),
                         [tf], [sh])
                    P.rstd(sh, sh[:, 0:8], sh[:, 8:16], sh[:, 16:24], 1.0 / HD)
                    P.tt("dve", tg, tg[:], pb, pv, sh, sh[:, 16:24].unsqueeze(2).to_broadcast([128, 8, 64]), ALU.mult)
                    P.tt("pool", qbb, qbb[:].rearrange("p (h e) -> p h e", h=8), tg, tg[:], g, g[:], ALU.mult)
                elif j == 2:
                    P.copy("act", vb[b], vb[b][:], pb, pb[:])
                    P.dma("pool", V, v_d[t * 128:(t + 1) * 128, :], vb[b], vb[b][:])
                    if t + 1 < NT:
                        norm_front(t + 1)
                elif j < 6:
                    jj = j - 3
                    P.act(szb[b], szb[b][:, jj * 512:(jj + 1) * 512], pb, pb[:], AF.Silu)
                    if jj == 2:
                        P.dma("pool", SZ, sz_d[t * 128:(t + 1) * 128, :], szb[b], szb[b][:])
                    if jj == 0:
                        qk_back(t, 0)
                    if jj == 2:
                        qk_back(t, 1)
                elif j < 10:
                    jj = j - 6
                    P.copy("dve", ub[b], ub[b][:, jj * 512:(jj + 1) * 512], pb, pb[:])
                    if jj == 3:
                        P.dma("pool", U, u_d[3 + t * 128:3 + (t + 1) * 128, :], ub[b], ub[b][:])
                    if jj == 1 and t + 1 < NT:
                        norm_back(t + 1)
                else:
                    d = dts[b]
                    P.tt("dve", d, d[:, 0, :], pb, pb[:, 0:NH_S], dtb, dtb[:], ALU.add)
                    P.stt("dve", d, d[:, 1, :], d, d[:, 0, :], -1.0, d, d[:, 0, :], ALU.mult, ALU.max)
                    P.act(d, d[:, 1, :], d, d[:, 1, :], AF.Exp, scale=-1.0)
                    P.act(d, d[:, 1, :], d, d[:, 1, :], AF.Ln, bias=1.0)
                    P.stt("dve", d, d[:, 2, :], d, d[:, 0, :], 0.0, d, d[:, 1, :], ALU.max, ALU.add)
                    P.dma("pool", DT, dt_d[t * 128:(t + 1) * 128, :], d, d[:, 2, :])
        P.barrier()
        P.emit()


PATTERNS = (1, 4, 16)
B_STOP = 99
C_STOP = 99
C_VAR = 0
HI_LIST = (0, 1)
MASK_ENG = 'pool'


def phase_b(nc, P, SEQ, CST, cst_d, QT, qT_d, KT, kT_d, V, v_d, AT, attnT_d):
    SBW = 2048
    NSB = SEQ // SBW
    with ExitStack() as ph:
        kTw = [P.sb(ph, "kTw%d" % i, [128, 4, SBW], BF16) for i in range(2)]
        qTw = [P.sb(ph, "qTw%d" % i, [128, 4, SBW], BF16) for i in range(1)]
        accn = P.sb(ph, "accn", [64, 8, SBW], F32)
        accd = P.sb(ph, "accd", [64, 8, SBW], F32)
        ast = [P.sb(ph, "ast%d" % i, [64, 8, 512], BF16) for i in range(1)]
        vt = [P.sb(ph, "vt%d" % i, [128, 512], BF16) for i in range(6)]
        pT = [P.sb(ph, "pT%d" % i, [128, 512], BF16) for i in range(3)]
        mask4 = P.sb(ph, "mask4", [128, 512], BF16)
        maskc = P.sb(ph, "maskc", [128, 256], BF16)
        ones_b = P.sb(ph, "ones_b", [128, 64], BF16)
        psS = [P.ps(ph, "psS%d" % i, [128, 2, 512], F32) for i in range(2)]
        psN = [P.ps(ph, "psN%d" % i, [128, 2, 128], F32) for i in range(2)]
        psD = [P.ps(ph, "psD%d" % i, [128, 512], F32) for i in range(2)]

        P.dma("pool", mask4, mask4[:, 0:256], CST, cst_d[:, 256:512])
        P.dma("pool", mask4, mask4[:, 256:512], CST, cst_d[:, 256:512])
        P.dma("pool", maskc, maskc[:, 0:128], CST, cst_d[:, 384:512])
        P.dma("pool", maskc, maskc[:, 128:256], CST, cst_d[:, 384:512])
        P.op("dve", lambda e: e.memset(ones_b[:], 1.0), [], [ones_b])

        vi = 0
        ui = 0
        for sb in range(NSB):
            T0 = sb * SBW
            kc_, qc_ = kTw[sb % 2], qTw[0]
            kp_ = kTw[(sb - 1) % 2]
            if B_STOP not in (-2, -4):
                P.dma("sp", kc_, kc_[:], KT, kT_d[:, :, T0:T0 + SBW])
                P.dma("sp", qc_, qc_[:], QT, qT_d[:, :, T0:T0 + SBW])
            if B_STOP in (-2, -3):
                continue
            for d in PATTERNS:
                span = 128 * d
                first = (d == 1)
                for b in range(SBW // span):
                    for r in range(d):
                        q0 = b * span + r
                        g0 = T0 + q0
                        has_prev = (T0 + b * span) >= span
                        kbs = []
                        if has_prev:
                            vp = vt[vi % 6]
                            vi += 1
                            P.dma("sp", vp, vp[:], V, v_d[g0 - span:g0 - span + (127 * d + 1):d, :])
                            if b == 0:
                                kbs.append((kp_, SBW - span + r, vp))
                            else:
                                kbs.append((kc_, q0 - span, vp))
                        vc = vt[vi % 6]
                        vi += 1
                        P.dma("sp", vc, vc[:], V, v_d[g0:g0 + (127 * d + 1):d, :])
                        kbs.append((kc_, q0, vc))
                        nk = len(kbs)
                        qs = slice(q0, q0 + 127 * d + 1, d)
                        for hp in range(4):
                            S = psS[ui % 2]
                            N = psN[ui % 2]
                            Dn = psD[ui % 2]
                            pt = pT[ui % 3]
                            ui += 1
                            W = 2 * nk * 128
                            for hi in (HI_LIST if B_STOP >= 0 else ()):
                                pl = slice(64 * hi, 64 * hi + 64)
                                for ki, (kb, k0, _) in enumerate(kbs):
                                    c = ki * 128
                                    P.mm(S, S[:, hi, c:c + 128], kb, kb[pl, hp, k0:k0 + 127 * d + 1:d],
                                         qc_, qc_[pl, hp, qs], True, True)
                            if B_STOP < 1:
                                continue
                            P.act(pt, pt[:, 0:W].rearrange("p (h c) -> p h c", h=2), S, S[:, :, 0:nk * 128], AF.Exp)
                            if B_STOP < 2:
                                continue
                            mk = mask4 if nk == 2 else maskc
                            P.tt(MASK_ENG, pt, pt[:, 0:W], pt, pt[:, 0:W], mk, mk[:, 0:W], ALU.mult)
                            if B_STOP < 3:
                                continue
                            for hi in range(2):
                                h = 2 * hp + hi
                                for ki, (kb, k0, vb_) in enumerate(kbs):
                                    c = (hi * nk + ki) * 128
                                    P.mm(N, N[0:64, hi, :], vb_, vb_[:, h * 64:(h + 1) * 64],
                                         pt, pt[:, c:c + 128], ki == 0, ki == nk - 1)
                            if B_STOP < 4:
                                continue
                            P.mm(Dn, Dn[0:64, 0:W], ones_b, ones_b[:, 0:64], pt, pt[:, 0:W], True, True)
                            if B_STOP < 5:
                                continue
                            an = accn[:, 2 * hp:2 * hp + 2, qs]
                            ad = accd[:, 2 * hp:2 * hp + 2, qs]
                            dv = Dn[0:64, 0:W].rearrange("p (h k q) -> p h k q", h=2, k=nk)
                            if first:
                                P.copy("dve", accn, an, N, N[0:64, :, :])
                                P.copy("dve", accd, ad, Dn, dv[:, :, 0, :])
                            else:
                                P.tt("dve", accn, an, accn, an, N, N[0:64, :, :], ALU.add)
                                P.tt("dve", accd, ad, accd, ad, Dn, dv[:, :, 0, :], ALU.add)
                            if nk == 2:
                                P.tt("dve", accd, ad, accd, ad, Dn, dv[:, :, 1, :], ALU.add)
            for c in range(SBW // 512 if B_STOP >= 6 else 0):
                c0 = c * 512
                a_ = ast[0]
                P.op("dve", lambda e, c0=c0: e.reciprocal(out=accd[:, :, c0:c0 + 512], in_=accd[:, :, c0:c0 + 512]), [accd], [accd])
                P.tt("dve", a_, a_[:], accn, accn[:, :, c0:c0 + 512], accd, accd[:, :, c0:c0 + 512], ALU.mult)
                P.dma("pool", AT, attnT_d[:, :, T0 + c0:T0 + c0 + 512], a_, a_[:])
        P.barrier()
        P.emit()


def phase_c(nc, P, SEQ, X, x_d, VEC, vec_d, CST, cst_d, U, u_d, SZ, sz_d, DT, dt_d,
            AT, attnT_d, WOA, woa_d, WOS, wos_d, OUT, out_d):
    NT = SEQ // 128
    with ExitStack() as ph:
        trif = P.sb(ph, "trif", [128, 128], F32)
        identb = P.sb(ph, "identC", [128, 128], BF16)
        onesf = P.sb(ph, "onesfC", [128, 128], F32)
        cw = P.sb(ph, "cw", [128, 4, CONV_DIM], F32)
        cb = P.sb(ph, "cb", [128, CONV_DIM], F32)
        abc = P.sb(ph, "abc", [128, NH_S], F32)
        dbc = P.sb(ph, "dbc", [128, NH_S], F32)
        sgbc = P.sb(ph, "sgbc", [128, SW], F32)
        Woa = P.sb(ph, "Woa", [64, 8, D_MODEL], BF16)
        Wos = P.sb(ph, "Wos", [128, 12, D_MODEL], BF16)
        uk = [P.sb(ph, "uk%d" % k, [128, CONV_DIM], BF16) for k in range(4)]
        tk = [P.sb(ph, "tk%d" % k, [128, CONV_DIM], F32) for k in range(4)]
        xcb = [P.sb(ph, "xcb%d" % i, [128, CONV_DIM], BF16) for i in range(2)]
        szt = P.sb(ph, "szt", [128, SW], BF16)
        dtt = [P.sb(ph, "dtt%d" % i, [128, NH_S], F32) for i in range(2)]
        att = [P.sb(ph, "att%d" % i, [64, 8, 128], BF16) for i in range(2)]
        xt = [P.sb(ph, "xtC%d" % i, [128, D_MODEL], F32) for i in range(2)]
        sm = [P.sb(ph, "smC%d" % i, [128, 8, NH_S], F32) for i in range(2)]
        st2 = P.sb(ph, "st2", [128, 8], F32)
        BCT = P.sb(ph, "BCT", [128, 4, 128], BF16)
        Gm = P.sb(ph, "Gm", [128, 2, 128], F32)
        dtmp = [P.sb(ph, "dtmp%d" % i, [128, 4, 128], F32) for i in range(2)]
        Mt = [P.sb(ph, "Mt%d" % i, [128, 4, 128], BF16) for i in range(2)]
        S = P.sb(ph, "Sst", [128, 2, 768], F32)
        Sb = P.sb(ph, "Sbf", [128, 2, 768], BF16)
        xwb = P.sb(ph, "xwb", [128, SW], BF16)
        xdt = P.sb(ph, "xdt", [128, SW], BF16)
        tmpD = P.sb(ph, "tmpD", [128, SW], F32)
        ysb = P.sb(ph, "ysb", [128, SW], F32)
        yb = P.sb(ph, "yb", [128, SW], BF16)
        yT = P.sb(ph, "yT", [128, 12, 128], BF16)
        psBC = [P.ps(ph, "psBC%d" % i, [128, 4, 128], F32) for i in range(2)]
        psY = P.ps(ph, "psY", [128, SW], F32)
        psSt = P.ps(ph, "psSt", [128, 1024], F32)
        psHi = Buf(psSt.t, "psSt_hi", "ps")
        psTr = P.ps(ph, "psTrC", [128, 4, 128], BF16)

        P.dma("sp", trif, trif[:], CST, cst_d[:, 128:256])
        P.dma("pool", identb, identb[:], CST, cst_d[:, 0:128])
        P.op("dve", lambda e: e.memset(onesf[:], 1.0), [], [onesf])
        P.dma("sp", cw, cw[:], VEC, bc_ap(vec_d, VO["conv_w"], [[0, 128], [CONV_DIM, 4], [1, CONV_DIM]]))
        P.dma("sp", cb, cb[:], VEC, bc_ap(vec_d, VO["conv_b"], [[0, 128], [1, CONV_DIM]]))
        P.dma("sp", abc, abc[:], VEC, bc_ap(vec_d, VO["a_log"], [[0, 128], [1, NH_S]]))
        P.dma("sp", dbc, dbc[:], VEC, bc_ap(vec_d, VO["d_skip"], [[0, 128], [1, NH_S]]))
        P.dma("sp", sgbc, sgbc[:], VEC, bc_ap(vec_d, VO["ssm_norm"], [[0, 128], [1, SW]]))
        P.act(abc, abc[:], abc, abc[:], AF.Exp)
        P.ts("dve", abc, abc[:], abc, abc[:], -1.0, None, ALU.mult)
        P.dma("pool", Woa, Woa[:], WOA, woa_d[:, :, :])
        for j in range(12):
            P.dma("pool", Wos, Wos[:, j, :], WOS, wos_d[:, j, :])

        def loads(t):
            b = t % 2
            r0 = t * 128
            for k in range(4):
                P.dma("sp", uk[k], uk[k][:], U, u_d[r0 + k:r0 + k + 128, :])
            P.dma("sp", dtt[b], dtt[b][:], DT, dt_d[r0:r0 + 128, :])
            P.dma("sp", att[b], att[b][:], AT, attnT_d[:, :, r0:r0 + 128])
            P.dma("sp", xt[b], xt[b][:], X, x_d[r0:r0 + 128, :])

        def conv_pool(t):
            for k in range(4):
                P.tt("dve", tk[k], tk[k][:], uk[k], uk[k][:], cw, cw[:, k, :], ALU.mult)

        def conv_dve(t):
            P.tt("dve", tk[0], tk[0][:], tk[0], tk[0][:], tk[1], tk[1][:], ALU.add)
            P.tt("dve", tk[2], tk[2][:], tk[2], tk[2][:], tk[3], tk[3][:], ALU.add)
            P.tt("dve", tk[0], tk[0][:], tk[0], tk[0][:], tk[2], tk[2][:], ALU.add)
            P.tt("dve", tk[0], tk[0][:], tk[0], tk[0][:], cb, cb[:], ALU.add)
            xc = xcb[t % 2]
            P.act(xc, xc[:], tk[0], tk[0][:], AF.Silu)

        loads(0)
        conv_pool(0)
        conv_dve(0)
        for t in range(NT):
            b = t % 2
            r0 = t * 128
            xc = xcb[b]
            if t + 1 < NT:
                loads(t + 1)
            P.dma("sp", szt, szt[:], SZ, sz_d[r0:r0 + 128, :])
            m = sm[b]
            d_ = dtt[b]
            P.tt("dve", m, m[:, 0, :], d_, d_[:], abc, abc[:], ALU.mult)
            P.mm(psHi, psHi[:, 768:768 + NH_S], trif, trif[:], m, m[:, 0, :], True, True)
            P.mm(psHi, psHi[:, 800:800 + NH_S], onesf, onesf[:], m, m[:, 0, :], True, True)
            P.copy("dve", m, m[:, 1, :], psHi, psHi[:, 768:768 + NH_S])
            P.copy("dve", m, m[:, 2, :], psHi, psHi[:, 800:800 + NH_S])
            P.tt("dve", m, m[:, 3, :], m, m[:, 2, :], m, m[:, 1, :], ALU.subtract)
            P.act(m, m[:, 3, :], m, m[:, 3, :], AF.Exp)
            P.tt("dve", m, m[:, 3, :], m, m[:, 3, :], d_, d_[:], ALU.mult)
            P.act(m, m[:, 4, :], m, m[:, 2, :], AF.Exp)
            for j in range(4):
                P.tr(psTr, psTr[:, j, :], xc, xc[:, SW + j * 128:SW + (j + 1) * 128], identb, identb[:])
            P.copy("act", BCT, BCT[:], psTr, psTr[:])
            for g in range(2):
                P.mm(psSt, psSt[:, g * 128:128 + g * 128], BCT, BCT[:, g, :], BCT, BCT[:, 2 + g, :], True, True)
            P.copy("act", Gm, Gm[:], psSt, psSt[:, 0:256].rearrange("p (g l) -> p g l", g=2))
            for g in range(2):
                P.tt("dve", Gm, Gm[:, g, :], Gm, Gm[:, g, :], trif, trif[:], ALU.mult)

            def pre(q4):
                bc = psBC[(t * 6 + q4) % 2]
                for j in range(4):
                    h = 4 * q4 + j
                    P.mm(bc, bc[:, j, :], m, m[:, 0, h:h + 1].to_broadcast([128, 128]), trif, trif[:], True, True)

            def body_a(q4):
                i2 = (t * 6 + q4) % 2
                bc, dt4 = psBC[i2], dtmp[i2]
                for j in range(4):
                    h = 4 * q4 + j
                    P.act(dt4, dt4[:, j, :], bc, bc[:, j, :], AF.Exp, rd=[m], bias=m[:, 6, h:h + 1], scale=1.0)

            def body_b(q4):
                g = q4 // 3
                i2 = (t * 6 + q4) % 2
                dt4, mt = dtmp[i2], Mt[i2]
                for j in range(4):
                    P.stt("dve", mt, mt[:, j, :], dt4, dt4[:, j, :], 1.0, Gm, Gm[:, g, :], ALU.min, ALU.mult)
                for j in range(4):
                    h = 4 * q4 + j
                    P.mm(psY, psY[:, h * 64:(h + 1) * 64], mt, mt[:, j, :], xdt, xdt[:, h * 64:(h + 1) * 64], True, True)

            xs3 = xc[:, 0:SW].rearrange("p (h e) -> p h e", h=NH_S)
            P.ts("dve", m, m[:, 6, :], m, m[:, 1, :], -1.0, None, ALU.mult)
            P.tt("dve", xdt, xdt[:].rearrange("p (h e) -> p h e", h=NH_S), xc, xs3,
                 d_, d_[:].unsqueeze(2).to_broadcast([128, NH_S, 64]), ALU.mult)
            pre(0)
            body_a(0)
            pre(1)
            body_a(1)
            for q4 in range(6):
                if q4 + 2 < 6:
                    pre(q4 + 2)
                body_b(q4)
                if q4 + 2 < 6:
                    body_a(q4 + 2)
            P.tt("dve", tmpD, tmpD[:].rearrange("p (h e) -> p h e", h=NH_S), xc, xs3,
                 dbc, dbc[:].unsqueeze(2).to_broadcast([128, NH_S, 64]), ALU.mult)
            if t > 0:
                P.act(m, m[:, 5, :], m, m[:, 1, :], AF.Exp)
                for g in range(2):
                    P.mm(psSt, psSt[:, 0:512], BCT, BCT[:, 2 + g, :], Sb, Sb[:, g, 0:512], True, True)
                    P.mm(psSt, psSt[:, 512:768], BCT, BCT[:, 2 + g, :], Sb, Sb[:, g, 512:768], True, True)
                    y3 = ysb[:, g * 768:(g + 1) * 768].rearrange("p (h e) -> p h e", h=12)
                    P.tt("dve", ysb, y3, psSt, psSt[:, 0:768].rearrange("p (h e) -> p h e", h=12),
                         m, m[:, 5, 12 * g:12 * g + 12].unsqueeze(2).to_broadcast([128, 12, 64]), ALU.mult)
                    P.tt("dve", tmpD, tmpD[:, g * 768:(g + 1) * 768], tmpD, tmpD[:, g * 768:(g + 1) * 768],
                         ysb, ysb[:, g * 768:(g + 1) * 768], ALU.add)
            if t + 1 < NT:
                conv_pool(t + 1)
                conv_dve(t + 1)
            P.tt("dve", ysb, ysb[:], psY, psY[:], tmpD, tmpD[:], ALU.add)
            P.tt("dve", ysb, ysb[:], ysb, ysb[:], szt, szt[:], ALU.mult)
            for g in range(2):
                P.act(tmpD, tmpD[:, g * 768:(g + 1) * 768], ysb, ysb[:, g * 768:(g + 1) * 768], AF.Square,
                      wr=[st2], accum_out=st2[:, g:g + 1])
            P.rstd(st2, st2[:, 0:2], st2[:, 2:4], st2[:, 4:6], 1.0 / 768)
            for g in range(2):
                P.stt("dve", yb, yb[:, g * 768:(g + 1) * 768], ysb, ysb[:, g * 768:(g + 1) * 768], st2[:, 4 + g:5 + g],
                      sgbc, sgbc[:, g * 768:(g + 1) * 768], ALU.mult, ALU.mult, rd=[st2])
            P.tt("dve", xwb, xwb[:].rearrange("p (h e) -> p h e", h=NH_S), xc, xs3,
                 m, m[:, 3, :].unsqueeze(2).to_broadcast([128, NH_S, 64]), ALU.mult)
            for g in range(2):
                bs = xc[:, SW + g * 128:SW + (g + 1) * 128]
                P.mm(psSt, psSt[:, 0:512], xc, bs, xwb, xwb[:, g * 768:g * 768 + 512], True, True)
                P.mm(psSt, psSt[:, 512:768], xc, bs, xwb, xwb[:, g * 768 + 512:(g + 1) * 768], True, True)
                if t == 0:
                    P.copy("dve", S, S[:, g, :], psSt, psSt[:, 0:768])
                else:
                    sg3 = S[:, g, :].rearrange("p (h e) -> p h e", h=12)
                    P.tt("dve", S, sg3, S, sg3, m, m[:, 4, 12 * g:12 * g + 12].unsqueeze(2).to_broadcast([128, 12, 64]), ALU.mult)
                    P.tt("dve", S, S[:, g, :], S, S[:, g, :], psSt, psSt[:, 0:768], ALU.add)
            P.copy("act", Sb, Sb[:], S, S[:])
            for i in range(3):
                for j in range(4):
                    c = 4 * i + j
                    P.tr(psTr, psTr[:, j, :], yb, yb[:, c * 128:(c + 1) * 128], identb, identb[:])
                P.copy("act", yT, yT[:, 4 * i:4 * i + 4, :], psTr, psTr[:])
            a_ = att[b]
            for hf in range(2):
                o = psY[:, hf * 512:(hf + 1) * 512]
                for h in range(8):
                    P.mm(psY, o, a_, a_[0:64, h, :], Woa, Woa[0:64, h, hf * 512:(hf + 1) * 512], h == 0, False)
                for j in range(12):
                    P.mm(psY, o, yT, yT[:, j, :], Wos, Wos[:, j, hf * 512:(hf + 1) * 512], False, j == 11)
            P.tt("dve", xt[b], xt[b][:], xt[b], xt[b][:], psY, psY[:, 0:D_MODEL], ALU.add)
            P.dma("pool", OUT, out_d[r0:r0 + 128, :], xt[b], xt[b][:])
        P.barrier()
        P.emit()


def norm_T(P, xt_b, xt_ap, gbc, s, junk, hb, psT, hT, ident):
    P.act(junk, junk[:], xt_b, xt_ap, AF.Square, wr=[s], accum_out=s[:, 0:1])
    P.rstd(s, s[:, 0:1], s[:, 1:2], s[:, 2:3], 1.0 / D_MODEL)
    P.stt("dve", hb, hb[:], xt_b, xt_ap, s[:, 2:3], gbc, gbc[:], ALU.mult, ALU.mult, rd=[s])
    for kc in range(8):
        P.tr(psT, psT[:, kc, :], hb, hb[:, kc * 128:(kc + 1) * 128], ident, ident[:])
    P.copy("act", hT, hT[:], psT, psT[:])


def phase_d1(nc, P, SEQ, VEC, vec_d, CST, cst_d, WG, wg_d, WU, wu_d, WD, wd_d, OUT, out_d):
    NT = SEQ // 128
    with ExitStack() as ph:
        Wg = P.sb(ph, "Wg", [128, 8, D_FF], BF16)
        Wu = P.sb(ph, "Wu", [128, 8, D_FF], BF16)
        Wd = P.sb(ph, "Wd", [128, 22, D_MODEL], BF16)
        ident = P.sb(ph, "identD", [128, 128], BF16)
        gbc = P.sb(ph, "gffn", [128, D_MODEL], F32)
        xt = [P.sb(ph, "xtD%d" % i, [128, D_MODEL], F32) for i in range(3)]
        junk = P.sb(ph, "junkD", [128, D_MODEL], F32)
        ss = [P.sb(ph, "ssD%d" % i, [128, 8], F32) for i in range(2)]
        hb = P.sb(ph, "hbD", [128, D_MODEL], BF16)
        hT = [P.sb(ph, "hTD%d" % i, [128, 8, 128], BF16) for i in range(2)]
        sg = [P.sb(ph, "sgD%d" % i, [128, 512], F32) for i in range(2)]
        ab = P.sb(ph, "abD", [128, D_FF], BF16)
        aT = [P.sb(ph, "aTD%d" % i, [128, 22, 128], BF16) for i in range(2)]
        psT = P.ps(ph, "psTD", [128, 8, 128], BF16)
        psG = [P.ps(ph, "psGD%d" % i, [128, 512], F32) for i in range(2)]
        psU = [P.ps(ph, "psUD%d" % i, [128, 512], F32) for i in range(2)]
        psO = P.ps(ph, "psOD", [128, D_MODEL], F32)
        psA_ = P.ps(ph, "psAD", [128, 8, 128], BF16)
        psA = [psA_, Buf(psA_.t, "psAD_b", "ps")]
        for fc in range(22):
            P.dma("pool", Wg, Wg[:, :, fc * 128:(fc + 1) * 128], WG, wg_d[fc])
            P.dma("pool", Wu, Wu[:, :, fc * 128:(fc + 1) * 128], WU, wu_d[fc])
        for j in range(0, 22, 2):
            P.dma("pool", Wd, Wd[:, j:j + 2, :], WD, wd_d[:, j:j + 2, :])
        P.dma("pool", ident, ident[:], CST, cst_d[:, 0:128])
        P.dma("sp", gbc, gbc[:], VEC, bc_ap(vec_d, VO["ffn_norm"], [[0, 128], [1, D_MODEL]]))
        def nfront(t):
            b = t % 3
            s_ = ss[t % 2]
            P.act(junk, junk[:], xt[b], xt[b][:], AF.Square, wr=[s_], accum_out=s_[:, 0:1])
            P.rstd(s_, s_[:, 0:1], s_[:, 1:2], s_[:, 2:3], 1.0 / D_MODEL)
            P.stt("dve", hb, hb[:], xt[b], xt[b][:], s_[:, 2:3], gbc, gbc[:], ALU.mult, ALU.mult, rd=[s_])

        def nback(t):
            b = t % 2
            for kc in range(8):
                P.tr(psT, psT[:, kc, :], hb, hb[:, kc * 128:(kc + 1) * 128], ident, ident[:])
            P.copy("act", hT[b], hT[b][:], psT, psT[:])

        def ab_T(t, fc):
            n = 4 if fc < 5 else 2
            hh = (t * 6 + fc) % 2
            pa = psA[hh]
            for j in range(n):
                c = 4 * fc + j
                P.tr(pa, pa[:, 4 * hh + j, :], ab, ab[:, c * 128:(c + 1) * 128], ident, ident[:])
            P.copy("act", aT[t % 2], aT[t % 2][:, 4 * fc:4 * fc + n, :], pa, pa[:, 4 * hh:4 * hh + n, :])

        def down(t):
            a_ = aT[t % 2]
            x_ = xt[t % 3]
            for hf in range(2):
                for j in range(22):
                    P.mm(psO, psO[:, hf * 512:(hf + 1) * 512], a_, a_[:, j, :], Wd, Wd[:, j, hf * 512:(hf + 1) * 512], j == 0, j == 21)
            P.tt("dve", x_, x_[:], x_, x_[:], psO, psO[:], ALU.add)
            P.dma("pool", OUT, out_d[t * 128:(t + 1) * 128, :], x_, x_[:])

        P.dma("sp", xt[0], xt[0][:], OUT, out_d[0:128, :])
        if NT > 1:
            P.dma("sp", xt[1], xt[1][:], OUT, out_d[128:256, :])
        nfront(0)
        nback(0)
        cnt = 0
        for t in range(NT):
            b = t % 2
            for fc in range(6):
                f0 = fc * 512
                w = min(512, D_FF - f0)
                G_, U_, sg_ = psG[cnt % 2], psU[cnt % 2], sg[cnt % 2]
                cnt += 1
                for kc in range(8):
                    P.mm(G_, G_[:, 0:w], hT[b], hT[b][:, kc, :], Wg, Wg[:, kc, f0:f0 + w], kc == 0, kc == 7)
                for kc in range(8):
                    P.mm(U_, U_[:, 0:w], hT[b], hT[b][:, kc, :], Wu, Wu[:, kc, f0:f0 + w], kc == 0, kc == 7)
                P.act(sg_, sg_[:, 0:w], G_, G_[:, 0:w], AF.Silu)
                P.tt("dve", ab, ab[:, f0:f0 + w], sg_, sg_[:, 0:w], U_, U_[:, 0:w], ALU.mult)
                if fc == 0 and t > 0:
                    down(t - 1)
                if fc == 1 and t + 1 < NT:
                    if t + 2 < NT:
                        P.dma("sp", xt[(t + 2) % 3], xt[(t + 2) % 3][:], OUT, out_d[(t + 2) * 128:(t + 3) * 128, :])
                    nfront(t + 1)
                if fc >= 1:
                    ab_T(t, fc - 1)
                if fc == 4 and t + 1 < NT:
                    nback(t + 1)
            ab_T(t, 5)
        down(NT - 1)
        P.barrier()
        P.emit()


def phase_d2(nc, P, SEQ, VEC, vec_d, CST, cst_d, PIN, p_d, WPG, wpg_d, WPLE, wple_d, OUT, out_d):
    NT = SEQ // 128
    with ExitStack() as ph:
        Wpg = P.sb(ph, "Wpg", [128, 8, D_MODEL], BF16)
        Wpl = P.sb(ph, "Wpl", [128, 2, D_MODEL], BF16)
        ident = P.sb(ph, "identE", [128, 128], BF16)
        gbc = P.sb(ph, "gpg", [128, D_MODEL], F32)
        bbc = P.sb(ph, "bpg", [128, D_MODEL], F32)
        pgbc = P.sb(ph, "gple", [128, D_MODEL], F32)
        xt = [P.sb(ph, "xtE%d" % i, [128, D_MODEL], F32) for i in range(3)]
        pt = [P.sb(ph, "ptE%d" % i, [128, PLE], F32) for i in range(2)]
        pb = P.sb(ph, "pbE", [128, PLE], BF16)
        pT = [P.sb(ph, "pTE%d" % i, [128, 2, 128], BF16) for i in range(2)]
        junk = P.sb(ph, "junkE", [128, D_MODEL], F32)
        ss = [P.sb(ph, "ssE%d" % i, [128, 8], F32) for i in range(2)]
        hb = P.sb(ph, "hbE", [128, D_MODEL], BF16)
        hT = [P.sb(ph, "hTE%d" % i, [128, 8, 128], BF16) for i in range(2)]
        gt = P.sb(ph, "gtE", [128, D_MODEL], F32)
        en = P.sb(ph, "enE", [128, D_MODEL], F32)
        psT = P.ps(ph, "psTE", [128, 8, 128], BF16)
        psP = P.ps(ph, "psPE", [128, 4, 128], BF16)
        psGt = P.ps(ph, "psGt", [128, D_MODEL], F32)
        psE = P.ps(ph, "psE", [128, D_MODEL], F32)
        for kc in range(0, 8, 2):
            P.dma("pool", Wpg, Wpg[:, kc:kc + 2, :], WPG, wpg_d[:, kc:kc + 2, :])
        P.dma("pool", Wpl, Wpl[:], WPLE, wple_d[:, :, :])
        P.dma("pool", ident, ident[:], CST, cst_d[:, 0:128])
        P.dma("sp", gbc, gbc[:], VEC, bc_ap(vec_d, VO["ple_gate_norm"], [[0, 128], [1, D_MODEL]]))
        P.dma("sp", bbc, bbc[:], VEC, bc_ap(vec_d, VO["b_ple_gate"], [[0, 128], [1, D_MODEL]]))
        P.dma("sp", pgbc, pgbc[:], VEC, bc_ap(vec_d, VO["ple_norm"], [[0, 128], [1, D_MODEL]]))
        def load(t):
            r0 = t * 128
            P.dma("sp", xt[t % 3], xt[t % 3][:], OUT, out_d[r0:r0 + 128, :])
            P.dma("sp", pt[t % 2], pt[t % 2][:], PIN, p_d[r0:r0 + 128, :])

        def s1(t):
            x_, s_ = xt[t % 3], ss[t % 2]
            P.act(junk, junk[:], x_, x_[:], AF.Square, wr=[s_], accum_out=s_[:, 0:1])
            P.rstd(s_, s_[:, 0:1], s_[:, 1:2], s_[:, 2:3], 1.0 / D_MODEL)
            P.stt("dve", hb, hb[:], x_, x_[:], s_[:, 2:3], gbc, gbc[:], ALU.mult, ALU.mult, rd=[s_])
            P.copy("dve", pb, pb[:], pt[t % 2], pt[t % 2][:])

        def s2(t):
            b = t % 2
            for kc in range(8):
                P.tr(psT, psT[:, kc, :], hb, hb[:, kc * 128:(kc + 1) * 128], ident, ident[:])
            P.copy("act", hT[b], hT[b][:], psT, psT[:])
            for j in range(2):
                P.tr(psP, psP[:, j, :], pb, pb[:, j * 128:(j + 1) * 128], ident, ident[:])
            P.copy("act", pT[b], pT[b][:], psP, psP[:, 0:2, :])
            for hf in range(2):
                for kc in range(8):
                    P.mm(psGt, psGt[:, hf * 512:(hf + 1) * 512], hT[b], hT[b][:, kc, :], Wpg, Wpg[:, kc, hf * 512:(hf + 1) * 512], kc == 0, kc == 7)
            for hf in range(2):
                for j in range(2):
                    P.mm(psE, psE[:, hf * 512:(hf + 1) * 512], pT[b], pT[b][:, j, :], Wpl, Wpl[:, j, hf * 512:(hf + 1) * 512], j == 0, j == 1)

        def s3(t):
            x_, s_ = xt[t % 3], ss[t % 2]
            P.tt("dve", gt, gt[:], psGt, psGt[:], bbc, bbc[:], ALU.add)
            P.act(gt, gt[:], gt, gt[:], AF.Sigmoid)
            P.act(junk, junk[:], psE, psE[:], AF.Square, wr=[s_], accum_out=s_[:, 4:5])
            P.rstd(s_, s_[:, 4:5], s_[:, 5:6], s_[:, 6:7], 1.0 / D_MODEL)
            P.stt("dve", en, en[:], psE, psE[:], s_[:, 6:7], pgbc, pgbc[:], ALU.mult, ALU.mult, rd=[s_])
            P.tt("pool", en, en[:], en, en[:], gt, gt[:], ALU.mult)
            P.tt("dve", x_, x_[:], x_, x_[:], en, en[:], ALU.add)
            P.dma("pool", OUT, out_d[t * 128:(t + 1) * 128, :], x_, x_[:])

        load(0)
        if NT > 1:
            load(1)
        s1(0)
        s2(0)
        for t in range(NT):
            if t + 2 < NT:
                load(t + 2)
            if t + 1 < NT:
                s1(t + 1)
            s3(t)
            if t + 1 < NT:
                s2(t + 1)
        P.barrier()
        P.emit()


def make_consts():
    c = np.zeros((128, 1024), np.float32)
    i = np.arange(128)
    c[:, 0:128] = np.eye(128, dtype=np.float32)
    c[:, 128:256] = (i[:, None] <= i[None, :])
    c[:, 256:384] = (i[:, None] >= i[None, :])
    c[:, 384:512] = (i[:, None] <= i[None, :])
    return c


def pack_inputs(inp, SEQ=SEQ_FULL):
    f = lambda a: np.ascontiguousarray(np.asarray(a, dtype=np.float32))
    vec = np.zeros((1, NVEC), np.float32)
    for k, o in VO.items():
        a = f(inp[k][0]).reshape(-1)
        vec[0, o:o + a.size] = a
    shared = dict(
        w_in=f(f(inp["w_in"][0]).reshape(8, 128, IN_PROJ).transpose(1, 0, 2)),
        w_out_a=f(f(inp["w_out"][0])[0:AW].reshape(8, 64, D_MODEL).transpose(1, 0, 2)),
        w_out_s=f(f(inp["w_out"][0])[AW:].reshape(12, 128, D_MODEL).transpose(1, 0, 2)),
        w_g=f(f(inp["w_ffn_gate"][0]).reshape(8, 128, 22, 128).transpose(2, 1, 0, 3)),
        w_u=f(f(inp["w_ffn_up"][0]).reshape(8, 128, 22, 128).transpose(2, 1, 0, 3)),
        w_d=f(f(inp["w_ffn_down"][0]).reshape(22, 128, D_MODEL).transpose(1, 0, 2)),
        w_pg=f(f(inp["w_ple_gate"][0]).reshape(8, 128, D_MODEL).transpose(1, 0, 2)),
        w_ple=f(f(inp["w_ple"][0]).reshape(2, 128, D_MODEL).transpose(1, 0, 2)),
        vecs=vec,
        consts=make_consts(),
    )
    maps = []
    for b in range(inp["x"].shape[0]):
        m = dict(shared)
        m["x"] = f(inp["x"][b][:SEQ])
        m["p"] = f(inp["p"][0][b][:SEQ])
        maps.append(m)
    return maps


_NC_CACHE = {}


def kernel(**inputs):
    maps = pack_inputs(inputs)
    if "nc" not in _NC_CACHE:
        _NC_CACHE["nc"] = build()
    nc = _NC_CACHE["nc"]
    res = run_bass_kernel_spmd(nc, maps, core_ids=list(range(len(maps))))
    return np.stack([np.asarray(r["out"], dtype=np.float32) for r in res.results], axis=0)
```

```python
from contextlib import ExitStack
import numpy as np
import concourse.bass as bass
import concourse.mybir as mybir
from concourse.bass_utils import run_bass_kernel_spmd

F32 = mybir.dt.float32
BF16 = mybir.dt.bfloat16
AF = mybir.ActivationFunctionType
ALU = mybir.AluOpType
AX = mybir.AxisListType

ENG = ["pe", "act", "dve", "pool", "sp"]

D_MODEL = 1024
SEQ_FULL = 8192
NH_A = 8
HD = 64
AW = 512
NH_S = 24
SW = 1536
CONV_DIM = 2048
NSTATE = 128
IN_PROJ = 5144
D_FF = 2816
PLE = 256
EPS = 1e-6


class Buf:
    def __init__(self, t, name, space):
        self.t = t
        self.name = name
        self.space = space
        self.w = {}
        self.r = {}

    def __getitem__(self, k):
        return self.t[k]


class Prog:
    def __init__(self, nc, stack):
        self.nc = nc
        self.stack = stack
        self.q = {e: [] for e in ENG}
        self.emitted = {e: 0 for e in ENG}
        self.known = {e: {} for e in ENG}
        self.semval = {e: 0 for e in ENG}
        self.resolved = {e: {} for e in ENG}
        self.esem = {e: stack.enter_context(nc.semaphore("es_" + e)) for e in ENG}
        self.dsem = {}
        self.dcount = {}

    def sb(self, stack, name, shape, dtype):
        t = stack.enter_context(self.nc.sbuf_tensor(name, list(shape), dtype))
        return Buf(t, name, "sb")

    def ps(self, stack, name, shape, dtype=F32):
        t = stack.enter_context(self.nc.psum_tensor(name, list(shape), dtype))
        return Buf(t, name, "ps")

    def dr(self, t, name):
        return Buf(t, name, "dr")

    def _collect(self, e, reads, writes, skipkey=None):
        waits = {}

        def need(k, v):
            if k == skipkey:
                return
            if k == "pe" and e == "pe":
                return
            if self.known[e].get(k, -1) >= v:
                return
            if waits.get(k, -1) < v:
                waits[k] = v

        for b in reads:
            for k, v in b.w.items():
                need(k, v)
        for b in writes:
            for k, v in b.w.items():
                need(k, v)
            for k, v in b.r.items():
                need(k, v)
        for k, v in waits.items():
            self.known[e][k] = v
            if k in self.q:
                self.q[k][v]["inc"] = True
        return list(waits.items())

    def op(self, e, fn, reads=(), writes=()):
        idx = len(self.q[e])
        waits = self._collect(e, reads, writes)
        self.q[e].append(dict(fn=fn, waits=waits, inc=False, dkey=None))
        for b in reads:
            b.r[e] = idx
        for b in writes:
            b.w = {e: idx}
            b.r = {}
        return idx

    def dma(self, e, dst, dst_ap, src, src_ap, **kw):
        if dst.space != "dr":
            key = ("in", dst.name)
        elif src.space != "dr":
            key = ("out", src.name)
        else:
            key = ("dd", dst.name)
        waits = self._collect(e, [src], [dst], skipkey=key)
        cnt = self.dcount.get(key, 0) + 1
        self.dcount[key] = cnt
        val = 16 * cnt

        def fn(eng, dst_ap=dst_ap, src_ap=src_ap, kw=kw):
            return eng.dma_start(out=dst_ap, in_=src_ap, **kw)

        self.q[e].append(dict(fn=fn, waits=waits, inc=False, dkey=key))
        src.r[key] = val
        keep = {k: v for k, v in dst.w.items() if k == key}
        dst.w = keep
        dst.w[key] = val
        dst.r = {}

    def barrier(self):
        for e in ENG:
            waits = {}
            for k in ENG:
                if k == e or not self.q[k] or k == "sp":
                    continue
                v = -1
                for i in range(len(self.q[k]) - 1, -1, -1):
                    if self.q[k][i]["fn"] is not None and self.q[k][i]["dkey"] is None:
                        v = i
                        break
                if v < 0:
                    continue
                if self.known[e].get(k, -1) < v:
                    waits[k] = v
            for k, c in self.dcount.items():
                v = 16 * c
                if self.known[e].get(k, -1) < v:
                    waits[k] = v
            for k, v in waits.items():
                self.known[e][k] = v
                if k in self.q:
                    self.q[k][v]["inc"] = True
            self.q[e].append(dict(fn=None, waits=list(waits.items()), inc=False, dkey=None))

    def _sem(self, k):
        if k in self.esem:
            return self.esem[k]
        if k not in self.dsem:
            self.dsem[k] = self.stack.enter_context(
                self.nc.semaphore("ds%d" % len(self.dsem)))
        return self.dsem[k]

    def emit(self):
        nc = self.nc
        for e in ENG:
            for i in range(self.emitted[e], len(self.q[e])):
                r = self.q[e][i]
                if r["inc"] and r["fn"] is not None and r["dkey"] is None:
                    self.semval[e] += 1
                    self.resolved[e][i] = self.semval[e]
        for k in list(self.dcount):
            self._sem(k)

        def run(e, eng):
            recs = self.q[e]
            for i in range(self.emitted[e], len(recs)):
                rec = recs[i]
                for k, v in rec["waits"]:
                    if k in self.esem:
                        v = self.resolved[k][v]
                    eng.wait_ge(self._sem(k), v)
                if rec["fn"] is None:
                    continue
                ins = rec["fn"](eng)
                if rec["dkey"] is not None:
                    ins.then_inc(self._sem(rec["dkey"]), 16)
                elif rec["inc"]:
                    ins.then_inc(self.esem[e], 1)
            self.emitted[e] = len(recs)

        with nc.Block() as block:
            @block.tensor
            def _(eng):
                run("pe", eng)

            @block.scalar
            def _(eng):
                run("act", eng)

            @block.vector
            def _(eng):
                run("dve", eng)

            @block.gpsimd
            def _(eng):
                run("pool", eng)

            @block.sync
            def _(eng):
                run("sp", eng)

    def act(self, ob, o, ib, i, func, rd=(), wr=(), **kw):
        self.op("act", lambda e: e.activation(out=o, in_=i, func=func, **kw),
                [ib, *rd], [ob, *wr])

    def tt(self, eng, ob, o, ab, a, bb, b, op):
        self.op(eng, lambda e: e.tensor_tensor(out=o, in0=a, in1=b, op=op), [ab, bb], [ob])

    def ts(self, eng, ob, o, ab, a, s1, s2, op0, op1=None, rd=()):
        if op1 is None:
            self.op(eng, lambda e: e.tensor_scalar(out=o, in0=a, scalar1=s1, scalar2=None, op0=op0),
                    [ab, *rd], [ob])
        else:
            self.op(eng, lambda e: e.tensor_scalar(out=o, in0=a, scalar1=s1, scalar2=s2, op0=op0, op1=op1),
                    [ab, *rd], [ob])

    def stt(self, eng, ob, o, ab, a, s, bb, b, op0, op1, rd=(), wr=(), **kw):
        self.op(eng, lambda e: e.scalar_tensor_tensor(out=o, in0=a, scalar=s, in1=b, op0=op0, op1=op1, **kw),
                [ab, bb, *rd], [ob, *wr])

    def copy(self, eng, ob, o, ib, i):
        if eng == "act":
            self.op("act", lambda e: e.activation(out=o, in_=i, func=AF.Copy), [ib], [ob])
        else:
            self.op(eng, lambda e: e.tensor_copy(out=o, in_=i), [ib], [ob])

    def mm(self, ob, o, lb, l, rb, r, start, stop):
        self.op("pe", lambda e: e.matmul(o, lhsT=l, rhs=r, start=start, stop=stop), [lb, rb], [ob])

    def tr(self, ob, o, ib, i, idb, idap):
        self.op("pe", lambda e: e.transpose(o, i, idap), [ib, idb], [ob])

    def rstd(self, ssb, src, tmp, dst, inv_n):
        self.ts("dve", ssb, tmp, ssb, src, inv_n, EPS, ALU.mult, ALU.add)
        self.act(ssb, tmp, ssb, tmp, AF.Sqrt)
        self.op("dve", lambda e: e.reciprocal(out=dst, in_=tmp), [ssb], [ssb])


def bc_ap(handle, offset, dims):
    return bass.AP(tensor=handle, offset=offset, ap=[list(d) for d in dims])


def build(SEQ=SEQ_FULL, debug=False, phases="ABCD"):
    nc = bass.Bass("TRN2", target_bir_lowering=False)
    NT = SEQ // 128
    skind = "ExternalOutput" if debug else "Internal"

    def din(name, shape):
        return nc.dram_tensor(name, list(shape), F32, kind="ExternalInput")

    x_d = din("x", [SEQ, D_MODEL])
    p_d = din("p", [SEQ, PLE])
    win_d = din("w_in", [128, 8, IN_PROJ])
    wout_a_d = din("w_out_a", [64, 8, D_MODEL])
    wout_s_d = din("w_out_s", [128, 12, D_MODEL])
    wg_d = din("w_g", [22, 128, 8, 128])
    wu_d = din("w_u", [22, 128, 8, 128])
    wd_d = din("w_d", [128, 22, D_MODEL])
    wpg_d = din("w_pg", [128, 8, D_MODEL])
    wple_d = din("w_ple", [128, 2, D_MODEL])
    vec_d = din("vecs", [1, 17408])
    cst_d = din("consts", [128, 1024])
    out_d = nc.dram_tensor("out", [SEQ, D_MODEL], F32, kind="ExternalOutput")

    qT_d = nc.dram_tensor("qT_s", [128, 4, SEQ], BF16, kind=skind)
    kT_d = nc.dram_tensor("kT_s", [128, 4, SEQ], BF16, kind=skind)
    v_d = nc.dram_tensor("v_s", [SEQ, 8, 128], BF16, kind=skind)
    u_d = nc.dram_tensor("u_s", [SEQ + 3, CONV_DIM], BF16, kind=skind)
    sz_d = nc.dram_tensor("sz_s", [SEQ, SW], BF16, kind=skind)
    dt_d = nc.dram_tensor("dt_s", [SEQ, NH_S], F32, kind=skind)
    attnT_d = nc.dram_tensor("attnT_s", [64, 8, SEQ], BF16, kind=skind)

    with ExitStack() as st:
        P = Prog(nc, st)
        X = P.dr(x_d, "x")
        WIN = P.dr(win_d, "w_in")
        VEC = P.dr(vec_d, "vecs")
        CST = P.dr(cst_d, "consts")
        QT = P.dr(qT_d, "qT_s")
        KT = P.dr(kT_d, "kT_s")
        V = P.dr(v_d, "v_s")
        U = P.dr(u_d, "u_s")
        SZ = P.dr(sz_d, "sz_s")
        DT = P.dr(dt_d, "dt_s")

        AT = P.dr(attnT_d, "attnT_s")
        if "A" in phases:
            phase_a(nc, P, SEQ, X, x_d, WIN, win_d, VEC, vec_d, CST, cst_d,
                    QT, qT_d, KT, kT_d, V, v_d, U, u_d, SZ, sz_d, DT, dt_d)
        if "B" in phases:
            phase_b(nc, P, SEQ, CST, cst_d, QT, qT_d, KT, kT_d, V, v_d, AT, attnT_d)
        OUT = P.dr(out_d, "out")
        if "C" in phases:
            phase_c(nc, P, SEQ, X, x_d, VEC, vec_d, CST, cst_d, U, u_d, SZ, sz_d, DT, dt_d,
                    AT, attnT_d, P.dr(wout_a_d, "w_out_a"), wout_a_d, P.dr(wout_s_d, "w_out_s"), wout_s_d,
                    OUT, out_d)
        if "D" in phases:
            phase_d1(nc, P, SEQ, VEC, vec_d, CST, cst_d, P.dr(wg_d, "w_g"), wg_d, P.dr(wu_d, "w_u"), wu_d,
                     P.dr(wd_d, "w_d"), wd_d, OUT, out_d)
            phase_d2(nc, P, SEQ, VEC, vec_d, CST, cst_d, P.dr(p_d, "p"), p_d, P.dr(wpg_d, "w_pg"), wpg_d,
                     P.dr(wple_d, "w_ple"), wple_d, OUT, out_d)
    return nc


VO = dict(mix_norm=0, q_norm=1024, k_norm=1088, conv_w=1152, conv_b=1152 + 8192,
          dt_bias=11392, a_log=11416, d_skip=11440, ssm_norm=11464, ffn_norm=13000,
          ple_gate_norm=14024, b_ple_gate=15048)
VO["ple_norm"] = 16072
NVEC = 17408


def phase_a(nc, P, SEQ, X, x_d, WIN, win_d, VEC, vec_d, CST, cst_d,
            QT, qT_d, KT, kT_d, V, v_d, U, u_d, SZ, sz_d, DT, dt_d):
    NT = SEQ // 128
    with ExitStack() as ph:
        Win = P.sb(ph, "Win", [128, 8, IN_PROJ], BF16)
        ident = P.sb(ph, "identA", [128, 128], BF16)
        gmix = P.sb(ph, "gmix", [128, D_MODEL], F32)
        qg = P.sb(ph, "qg", [128, 8, 64], F32)
        kg = P.sb(ph, "kg", [128, 8, 64], F32)
        dtb = P.sb(ph, "dtb", [128, NH_S], F32)
        zer = P.sb(ph, "zer", [4, CONV_DIM], BF16)
        xt = [P.sb(ph, "xt%d" % i, [128, D_MODEL], F32) for i in range(2)]
        junk = P.sb(ph, "junkA", [128, D_MODEL], F32)
        ss = [P.sb(ph, "ssA%d" % i, [128, 8], F32) for i in range(2)]
        hb = P.sb(ph, "hb", [128, D_MODEL], BF16)
        hT = [P.sb(ph, "hT%d" % i, [128, 8, 128], BF16) for i in range(2)]
        tmpf = [P.sb(ph, "tmpf%d" % i, [128, 8, 64], F32) for i in range(2)]
        tmpg = [P.sb(ph, "tmpg%d" % i, [128, 8, 64], F32) for i in range(2)]
        ssh = [P.sb(ph, "ssh%d" % i, [128, 24], F32) for i in range(2)]
        qb = [P.sb(ph, "qb%d" % i, [128, 512], BF16) for i in range(2)]
        qst = [P.sb(ph, "qst%d" % i, [128, 4, 512], BF16) for i in range(2)]
        kst = [P.sb(ph, "kst%d" % i, [128, 4, 512], BF16) for i in range(2)]
        vb = [P.sb(ph, "vb%d" % i, [128, 8, 128], BF16) for i in range(2)]
        szb = [P.sb(ph, "szb%d" % i, [128, SW], BF16) for i in range(2)]
        ub = [P.sb(ph, "ub%d" % i, [128, CONV_DIM], BF16) for i in range(2)]
        dts = [P.sb(ph, "dts%d" % i, [128, 4, NH_S], F32) for i in range(2)]
        psT = P.ps(ph, "psT", [128, 8, 128], BF16)
        psQ = P.ps(ph, "psQ", [128, 4, 128], BF16)
        pj = [P.ps(ph, "pj%d" % i, [128, 512], F32) for i in range(4)]

        for kc in range(8):
            P.dma("pool", Win, Win[:, kc, :], WIN, win_d[:, kc, :])
        P.dma("pool", ident, ident[:], CST, cst_d[:, 0:128])
        P.dma("sp", gmix, gmix[:], VEC, bc_ap(vec_d, VO["mix_norm"], [[0, 128], [1, D_MODEL]]))
        P.dma("sp", qg, qg[:], VEC, bc_ap(vec_d, VO["q_norm"], [[0, 128], [0, 8], [1, 64]]))
        P.dma("sp", kg, kg[:], VEC, bc_ap(vec_d, VO["k_norm"], [[0, 128], [0, 8], [1, 64]]))
        P.dma("sp", dtb, dtb[:], VEC, bc_ap(vec_d, VO["dt_bias"], [[0, 128], [1, NH_S]]))
        P.ts("dve", qg, qg[:], qg, qg[:], HD ** -0.5, None, ALU.mult)
        P.op("dve", lambda e: e.memset(zer[:], 0.0), [], [zer])
        for vb_ in vb:
            P.op("dve", lambda e, vb_=vb_: e.memset(vb_[:], 1.0), [], [vb_])
        P.dma("pool", U, u_d[0:3, :], zer, zer[0:3, :])

        def norm_front(t):
            b = t % 2
            s_ = ss[b]
            P.act(junk, junk[:], xt[b], xt[b][:], AF.Square, wr=[s_], accum_out=s_[:, 0:1])
            P.rstd(s_, s_[:, 0:1], s_[:, 1:2], s_[:, 2:3], 1.0 / D_MODEL)
            P.stt("dve", hb, hb[:], xt[b], xt[b][:], s_[:, 2:3], gmix, gmix[:], ALU.mult, ALU.mult, rd=[s_])
            if t + 2 < NT:
                P.dma("sp", xt[b], xt[b][:], X, x_d[(t + 2) * 128:(t + 3) * 128, :])

        def norm_back(t):
            b = t % 2
            for kc in range(8):
                P.tr(psT, psT[:, kc, :], hb, hb[:, kc * 128:(kc + 1) * 128], ident, ident[:])
            P.copy("act", hT[b], hT[b][:], psT, psT[:])

        def qk_back(t, j):
            qbb = qb[j]
            stg = (qst if j == 0 else kst)[(t // 4) % 2]
            for pr in range(4):
                P.tr(psQ, psQ[:, pr, :], qbb, qbb[:, pr * 128:(pr + 1) * 128], ident, ident[:])
            tc0 = (t % 4) * 128
            P.copy("act", stg, stg[:, :, tc0:tc0 + 128], psQ, psQ[:])
            if t % 4 == 3 or t == NT - 1:
                t0 = (t // 4) * 512
                n = (t % 4 + 1) * 128
                D_, d_ = (QT, qT_d) if j == 0 else (KT, kT_d)
                P.dma("pool", D_, d_[:, :, t0:t0 + n], stg, stg[:, :, 0:n])

        P.dma("sp", xt[0], xt[0][:], X, x_d[0:128, :])
        if NT > 1:
            P.dma("sp", xt[1], xt[1][:], X, x_d[128:256, :])
        norm_front(0)
        norm_back(0)
        cnt = 0
        for t in range(NT):
            b = t % 2
            for j in range(11):
                c0 = j * 512
                w = min(512, IN_PROJ - c0)
                pb = pj[cnt % 4]
                cnt += 1
                for kc in range(8):
                    P.mm(pb, pb[:, 0:w], hT[b], hT[b][:, kc, :], Win, Win[:, kc, c0:c0 + w], kc == 0, kc == 7)
                if j < 2:
                    g = qg if j == 0 else kg
                    tf, tg, sh, qbb = tmpf[j], tmpg[j], ssh[j], qb[j]
                    pv = pb[:, :].rearrange("p (h e) -> p h e", h=8)
                    P.act(tf, tf[:], pb, pv, AF.Square)
                    P.op("dve", lambda e, sh=sh, tf=tf: e.tensor_reduce(out=sh[:, 0:8], in_=tf[:], axis=AX.X, op=ALU.add),
                         [tf], [sh])
                    P.rstd(sh, sh[:, 0:8], sh[:, 8:16], sh[:, 16:24], 1.0 / HD)
                    P.tt("dve", tg, tg[:], pb, pv, sh, sh[:, 16:24].unsqueeze(2).to_broadcast([128, 8, 64]), ALU.mult)
                    P.tt("pool", qbb, qbb[:].rearrange("p (h e) -> p h e", h=8), tg, tg[:], g, g[:], ALU.mult)
                elif j == 2:
                    P.copy("act", vb[b], vb[b][:, :, 0:64], pb, pb[:, :].rearrange("p (h e) -> p h e", h=8))
                    P.dma("pool", V, v_d[t * 128:(t + 1) * 128, :, :], vb[b], vb[b][:])
                    if t + 1 < NT:
                        norm_front(t + 1)
                elif j < 6:
                    jj = j - 3
                    P.act(szb[b], szb[b][:, jj * 512:(jj + 1) * 512], pb, pb[:], AF.Silu)
                    if jj == 2:
                        P.dma("pool", SZ, sz_d[t * 128:(t + 1) * 128, :], szb[b], szb[b][:])
                    if jj == 0:
                        qk_back(t, 0)
                    if jj == 2:
                        qk_back(t, 1)
                elif j < 10:
                    jj = j - 6
                    P.copy("dve", ub[b], ub[b][:, jj * 512:(jj + 1) * 512], pb, pb[:])
                    if jj == 3:
                        P.dma("pool", U, u_d[3 + t * 128:3 + (t + 1) * 128, :], ub[b], ub[b][:])
                    if jj == 1 and t + 1 < NT:
                        norm_back(t + 1)
                else:
                    d = dts[b]
                    P.tt("dve", d, d[:, 0, :], pb, pb[:, 0:NH_S], dtb, dtb[:], ALU.add)
                    P.stt("dve", d, d[:, 1, :], d, d[:, 0, :], -1.0, d, d[:, 0, :], ALU.mult, ALU.max)
                    P.act(d, d[:, 1, :], d, d[:, 1, :], AF.Exp, scale=-1.0)
                    P.act(d, d[:, 1, :], d, d[:, 1, :], AF.Ln, bias=1.0)
                    P.stt("dve", d, d[:, 2, :], d, d[:, 0, :], 0.0, d, d[:, 1, :], ALU.max, ALU.add)
                    P.dma("pool", DT, dt_d[t * 128:(t + 1) * 128, :], d, d[:, 2, :])
        P.barrier()
        P.emit()


PATTERNS = (1, 4, 16)
B_STOP = 99
C_STOP = 99
C_VAR = 0
HI_LIST = (0, 1)
MASK_ENG = 'pool'


def phase_b(nc, P, SEQ, CST, cst_d, QT, qT_d, KT, kT_d, V, v_d, AT, attnT_d):
    SBW = 2048
    NSB = SEQ // SBW
    with ExitStack() as ph:
        kTw = [P.sb(ph, "kTw%d" % i, [128, 4, SBW], BF16) for i in range(2)]
        qTw = [P.sb(ph, "qTw%d" % i, [128, 4, SBW], BF16) for i in range(2)]
        acc = P.sb(ph, "acc", [128, 8, SBW], F32)
        rd = P.sb(ph, "rd", [64, 8, 512], F32)
        ast = [P.sb(ph, "ast%d" % i, [64, 8, 512], BF16) for i in range(2)]
        vt = [P.sb(ph, "vt%d" % i, [128, 8, 128], BF16) for i in range(6)]
        pT = [P.sb(ph, "pT%d" % i, [128, 512], BF16) for i in range(3)]
        mask4 = P.sb(ph, "mask4", [128, 512], BF16)
        maskc = P.sb(ph, "maskc", [128, 256], BF16)
        psS = [P.ps(ph, "psS%d" % i, [128, 2, 512], F32) for i in range(2)]
        psN = [P.ps(ph, "psN%d" % i, [128, 2, 128], F32) for i in range(3)]

        P.dma("pool", mask4, mask4[:, 0:256], CST, cst_d[:, 256:512])
        P.dma("pool", mask4, mask4[:, 256:512], CST, cst_d[:, 256:512])
        P.dma("pool", maskc, maskc[:, 0:128], CST, cst_d[:, 384:512])
        P.dma("pool", maskc, maskc[:, 128:256], CST, cst_d[:, 384:512])

        vi = 0
        ui = 0
        for sb in range(NSB):
            T0 = sb * SBW
            kc_, qc_ = kTw[sb % 2], qTw[sb % 2]
            kp_ = kTw[(sb - 1) % 2]
            P.dma("sp", kc_, kc_[:], KT, kT_d[:, :, T0:T0 + SBW])
            P.dma("sp", qc_, qc_[:], QT, qT_d[:, :, T0:T0 + SBW])
            units = []
            for d in PATTERNS:
                span = 128 * d
                for b in range(SBW // span):
                    for r in range(d):
                        q0 = b * span + r
                        g0 = T0 + q0
                        has_prev = (T0 + b * span) >= span
                        for hp in range(4):
                            units.append((d, b, r, q0, g0, has_prev, hp))
            state = {}

            def front(u):
                nonlocal vi, ui
                d, b, r, q0, g0, has_prev, hp = units[u]
                span = 128 * d
                if hp == 0:
                    kbs = []
                    if has_prev:
                        vp = vt[vi % 6]
                        vi += 1
                        P.dma("sp", vp, vp[:], V, v_d[g0 - span:g0 - span + (127 * d + 1):d, :, :])
                        if b == 0:
                            kbs.append((kp_, SBW - span + r, vp))
                        else:
                            kbs.append((kc_, q0 - span, vp))
                    vc = vt[vi % 6]
                    vi += 1
                    P.dma("sp", vc, vc[:], V, v_d[g0:g0 + (127 * d + 1):d, :, :])
                    kbs.append((kc_, q0, vc))
                    state["kbs"] = kbs
                kbs = state["kbs"]
                nk = len(kbs)
                qs = slice(q0, q0 + 127 * d + 1, d)
                S = psS[ui % 2]
                pt = pT[ui % 3]
                N = psN[ui % 3]
                ui += 1
                W = 2 * nk * 128
                for hi in range(2):
                    pl = slice(64 * hi, 64 * hi + 64)
                    for ki, (kb, k0, _) in enumerate(kbs):
                        c = ki * 128
                        P.mm(S, S[:, hi, c:c + 128], kb, kb[pl, hp, k0:k0 + 127 * d + 1:d],
                             qc_, qc_[pl, hp, qs], True, True)
                P.act(pt, pt[:, 0:W].rearrange("p (h c) -> p h c", h=2), S, S[:, :, 0:nk * 128], AF.Exp)
                mk = mask4 if nk == 2 else maskc
                P.tt("dve", pt, pt[:, 0:W], pt, pt[:, 0:W], mk, mk[:, 0:W], ALU.mult)
                return (kbs, nk, qs, pt, N, hp, d == 1)

            def back(ctx):
                kbs, nk, qs, pt, N, hp, first = ctx
                for hi in range(2):
                    h = 2 * hp + hi
                    for ki, (kb, k0, vb_) in enumerate(kbs):
                        c = (hi * nk + ki) * 128
                        P.mm(N, N[:, hi, :], vb_, vb_[:, h, :], pt, pt[:, c:c + 128], ki == 0, ki == nk - 1)
                an = acc[:, 2 * hp:2 * hp + 2, qs]
                if first:
                    P.copy("dve", acc, an, N, N[:])
                else:
                    P.tt("dve", acc, an, acc, an, N, N[:], ALU.add)

            prev_ctx = None
            for u in range(len(units)):
                ctx = front(u)
                if prev_ctx is not None:
                    back(prev_ctx)
                prev_ctx = ctx
            back(prev_ctx)
            for c in range(SBW // 512):
                c0 = c * 512
                a_ = ast[c % 2]
                P.op("dve", lambda e, c0=c0: e.reciprocal(out=rd[:], in_=acc[64:128, :, c0:c0 + 512]), [acc], [rd])
                P.tt("dve", a_, a_[:], acc, acc[0:64, :, c0:c0 + 512], rd, rd[:], ALU.mult)
                P.dma("pool", AT, attnT_d[:, :, T0 + c0:T0 + c0 + 512], a_, a_[:])
        P.barrier()
        P.emit()


def phase_c(nc, P, SEQ, X, x_d, VEC, vec_d, CST, cst_d, U, u_d, SZ, sz_d, DT, dt_d,
            AT, attnT_d, WOA, woa_d, WOS, wos_d, OUT, out_d):
    NT = SEQ // 128
    with ExitStack() as ph:
        trif = P.sb(ph, "trif", [128, 128], F32)
        identb = P.sb(ph, "identC", [128, 128], BF16)
        onesf = P.sb(ph, "onesfC", [128, 128], F32)
        cw = P.sb(ph, "cw", [128, 4, CONV_DIM], F32)
        cb = P.sb(ph, "cb", [128, CONV_DIM], F32)
        abc = P.sb(ph, "abc", [128, NH_S], F32)
        dbc = P.sb(ph, "dbc", [128, NH_S], F32)
        sgbc = P.sb(ph, "sgbc", [128, SW], F32)
        Woa = P.sb(ph, "Woa", [64, 8, D_MODEL], BF16)
        Wos = P.sb(ph, "Wos", [128, 12, D_MODEL], BF16)
        uk = [P.sb(ph, "uk%d" % k, [128, CONV_DIM], BF16) for k in range(4)]
        tk = [P.sb(ph, "tk%d" % k, [128, CONV_DIM], F32) for k in range(4)]
        xcb = [P.sb(ph, "xcb%d" % i, [128, CONV_DIM], BF16) for i in range(2)]
        szt = P.sb(ph, "szt", [128, SW], BF16)
        dtt = [P.sb(ph, "dtt%d" % i, [128, NH_S], F32) for i in range(2)]
        att = [P.sb(ph, "att%d" % i, [64, 8, 128], BF16) for i in range(2)]
        xt = [P.sb(ph, "xtC%d" % i, [128, D_MODEL], F32) for i in range(2)]
        sm = [P.sb(ph, "smC%d" % i, [128, 8, NH_S], F32) for i in range(2)]
        st2 = P.sb(ph, "st2", [128, 8], F32)
        BCT = P.sb(ph, "BCT", [128, 4, 128], BF16)
        Gm = P.sb(ph, "Gm", [128, 2, 128], F32)
        dtmp = [P.sb(ph, "dtmp%d" % i, [128, 4, 128], F32) for i in range(2)]
        Mt = [P.sb(ph, "Mt%d" % i, [128, 4, 128], BF16) for i in range(2)]
        S = P.sb(ph, "Sst", [128, 2, 768], F32)
        Sb = P.sb(ph, "Sbf", [128, 2, 768], BF16)
        xwb = P.sb(ph, "xwb", [128, SW], BF16)
        xdt = P.sb(ph, "xdt", [128, SW], BF16)
        tmpD = P.sb(ph, "tmpD", [128, SW], F32)
        ysb = P.sb(ph, "ysb", [128, SW], F32)
        yb = P.sb(ph, "yb", [128, SW], BF16)
        yT = P.sb(ph, "yT", [128, 12, 128], BF16)
        psBC = [P.ps(ph, "psBC%d" % i, [128, 4, 128], F32) for i in range(2)]
        psY = P.ps(ph, "psY", [128, SW], F32)
        psSt = P.ps(ph, "psSt", [128, 1024], F32)
        psHi = Buf(psSt.t, "psSt_hi", "ps")
        psTr = P.ps(ph, "psTrC", [128, 4, 128], BF16)

        P.dma("sp", trif, trif[:], CST, cst_d[:, 128:256])
        P.dma("pool", identb, identb[:], CST, cst_d[:, 0:128])
        P.op("dve", lambda e: e.memset(onesf[:], 1.0), [], [onesf])
        P.dma("sp", cw, cw[:], VEC, bc_ap(vec_d, VO["conv_w"], [[0, 128], [CONV_DIM, 4], [1, CONV_DIM]]))
        P.dma("sp", cb, cb[:], VEC, bc_ap(vec_d, VO["conv_b"], [[0, 128], [1, CONV_DIM]]))
        P.dma("sp", abc, abc[:], VEC, bc_ap(vec_d, VO["a_log"], [[0, 128], [1, NH_S]]))
        P.dma("sp", dbc, dbc[:], VEC, bc_ap(vec_d, VO["d_skip"], [[0, 128], [1, NH_S]]))
        P.dma("sp", sgbc, sgbc[:], VEC, bc_ap(vec_d, VO["ssm_norm"], [[0, 128], [1, SW]]))
        P.act(abc, abc[:], abc, abc[:], AF.Exp)
        P.ts("dve", abc, abc[:], abc, abc[:], -1.0, None, ALU.mult)
        P.dma("pool", Woa, Woa[:], WOA, woa_d[:, :, :])
        for j in range(12):
            P.dma("pool", Wos, Wos[:, j, :], WOS, wos_d[:, j, :])

        def loads(t):
            b = t % 2
            r0 = t * 128
            for k in range(4):
                P.dma("sp", uk[k], uk[k][:], U, u_d[r0 + k:r0 + k + 128, :])
            P.dma("sp", dtt[b], dtt[b][:], DT, dt_d[r0:r0 + 128, :])
            P.dma("sp", att[b], att[b][:], AT, attnT_d[:, :, r0:r0 + 128])
            P.dma("sp", xt[b], xt[b][:], X, x_d[r0:r0 + 128, :])

        def conv_pool(t):
            for k in range(4):
                P.tt("dve", tk[k], tk[k][:], uk[k], uk[k][:], cw, cw[:, k, :], ALU.mult)

        def conv_dve(t):
            P.tt("dve", tk[0], tk[0][:], tk[0], tk[0][:], tk[1], tk[1][:], ALU.add)
            P.tt("dve", tk[2], tk[2][:], tk[2], tk[2][:], tk[3], tk[3][:], ALU.add)
            P.tt("dve", tk[0], tk[0][:], tk[0], tk[0][:], tk[2], tk[2][:], ALU.add)
            P.tt("dve", tk[0], tk[0][:], tk[0], tk[0][:], cb, cb[:], ALU.add)
            xc = xcb[t % 2]
            P.act(xc, xc[:], tk[0], tk[0][:], AF.Silu)

        loads(0)
        conv_pool(0)
        conv_dve(0)
        for t in range(NT):
            b = t % 2
            r0 = t * 128
            xc = xcb[b]
            if t + 1 < NT:
                loads(t + 1)
            P.dma("sp", szt, szt[:], SZ, sz_d[r0:r0 + 128, :])
            m = sm[b]
            d_ = dtt[b]
            P.tt("dve", m, m[:, 0, :], d_, d_[:], abc, abc[:], ALU.mult)
            P.mm(psHi, psHi[:, 768:768 + NH_S], trif, trif[:], m, m[:, 0, :], True, True)
            P.mm(psHi, psHi[:, 800:800 + NH_S], onesf, onesf[:], m, m[:, 0, :], True, True)
            P.copy("dve", m, m[:, 1, :], psHi, psHi[:, 768:768 + NH_S])
            P.copy("dve", m, m[:, 2, :], psHi, psHi[:, 800:800 + NH_S])
            P.tt("dve", m, m[:, 3, :], m, m[:, 2, :], m, m[:, 1, :], ALU.subtract)
            P.act(m, m[:, 3, :], m, m[:, 3, :], AF.Exp)
            P.tt("dve", m, m[:, 3, :], m, m[:, 3, :], d_, d_[:], ALU.mult)
            P.act(m, m[:, 4, :], m, m[:, 2, :], AF.Exp)
            for j in range(4):
                P.tr(psTr, psTr[:, j, :], xc, xc[:, SW + j * 128:SW + (j + 1) * 128], identb, identb[:])
            P.copy("act", BCT, BCT[:], psTr, psTr[:])
            for g in range(2):
                P.mm(psSt, psSt[:, g * 128:128 + g * 128], BCT, BCT[:, g, :], BCT, BCT[:, 2 + g, :], True, True)
            P.copy("act", Gm, Gm[:], psSt, psSt[:, 0:256].rearrange("p (g l) -> p g l", g=2))
            for g in range(2):
                P.tt("dve", Gm, Gm[:, g, :], Gm, Gm[:, g, :], trif, trif[:], ALU.mult)

            def pre(q4):
                bc = psBC[(t * 6 + q4) % 2]
                for j in range(4):
                    h = 4 * q4 + j
                    P.mm(bc, bc[:, j, :], m, m[:, 0, h:h + 1].to_broadcast([128, 128]), trif, trif[:], True, True)

            def body_a(q4):
                i2 = (t * 6 + q4) % 2
                bc, dt4 = psBC[i2], dtmp[i2]
                for j in range(4):
                    h = 4 * q4 + j
                    P.act(dt4, dt4[:, j, :], bc, bc[:, j, :], AF.Exp, rd=[m], bias=m[:, 6, h:h + 1], scale=1.0)

            def body_b(q4):
                g = q4 // 3
                i2 = (t * 6 + q4) % 2
                dt4, mt = dtmp[i2], Mt[i2]
                for j in range(4):
                    P.stt("dve", mt, mt[:, j, :], dt4, dt4[:, j, :], 1.0, Gm, Gm[:, g, :], ALU.min, ALU.mult)
                for j in range(4):
                    h = 4 * q4 + j
                    P.mm(psY, psY[:, h * 64:(h + 1) * 64], mt, mt[:, j, :], xdt, xdt[:, h * 64:(h + 1) * 64], True, True)

            xs3 = xc[:, 0:SW].rearrange("p (h e) -> p h e", h=NH_S)
            P.ts("dve", m, m[:, 6, :], m, m[:, 1, :], -1.0, None, ALU.mult)
            P.tt("dve", xdt, xdt[:].rearrange("p (h e) -> p h e", h=NH_S), xc, xs3,
                 d_, d_[:].unsqueeze(2).to_broadcast([128, NH_S, 64]), ALU.mult)
            pre(0)
            body_a(0)
            pre(1)
            body_a(1)
            for q4 in range(6):
                if q4 + 2 < 6:
                    pre(q4 + 2)
                body_b(q4)
                if q4 + 2 < 6:
                    body_a(q4 + 2)
            P.tt("dve", tmpD, tmpD[:].rearrange("p (h e) -> p h e", h=NH_S), xc, xs3,
                 dbc, dbc[:].unsqueeze(2).to_broadcast([128, NH_S, 64]), ALU.mult)
            if t > 0:
                P.act(m, m[:, 5, :], m, m[:, 1, :], AF.Exp)
                for g in range(2):
                    P.mm(psSt, psSt[:, 0:512], BCT, BCT[:, 2 + g, :], Sb, Sb[:, g, 0:512], True, True)
                    P.mm(psSt, psSt[:, 512:768], BCT, BCT[:, 2 + g, :], Sb, Sb[:, g, 512:768], True, True)
                    y3 = ysb[:, g * 768:(g + 1) * 768].rearrange("p (h e) -> p h e", h=12)
                    P.tt("dve", ysb, y3, psSt, psSt[:, 0:768].rearrange("p (h e) -> p h e", h=12),
                         m, m[:, 5, 12 * g:12 * g + 12].unsqueeze(2).to_broadcast([128, 12, 64]), ALU.mult)
                    P.tt("dve", tmpD, tmpD[:, g * 768:(g + 1) * 768], tmpD, tmpD[:, g * 768:(g + 1) * 768],
                         ysb, ysb[:, g * 768:(g + 1) * 768], ALU.add)
            if t + 1 < NT:
                conv_pool(t + 1)
                conv_dve(t + 1)
            P.tt("dve", ysb, ysb[:], psY, psY[:], tmpD, tmpD[:], ALU.add)
            P.tt("dve", ysb, ysb[:], ysb, ysb[:], szt, szt[:], ALU.mult)
            for g in range(2):
                P.act(tmpD, tmpD[:, g * 768:(g + 1) * 768], ysb, ysb[:, g * 768:(g + 1) * 768], AF.Square,
                      wr=[st2], accum_out=st2[:, g:g + 1])
            P.rstd(st2, st2[:, 0:2], st2[:, 2:4], st2[:, 4:6], 1.0 / 768)
            for g in range(2):
                P.stt("dve", yb, yb[:, g * 768:(g + 1) * 768], ysb, ysb[:, g * 768:(g + 1) * 768], st2[:, 4 + g:5 + g],
                      sgbc, sgbc[:, g * 768:(g + 1) * 768], ALU.mult, ALU.mult, rd=[st2])
            P.tt("dve", xwb, xwb[:].rearrange("p (h e) -> p h e", h=NH_S), xc, xs3,
                 m, m[:, 3, :].unsqueeze(2).to_broadcast([128, NH_S, 64]), ALU.mult)
            for g in range(2):
                bs = xc[:, SW + g * 128:SW + (g + 1) * 128]
                P.mm(psSt, psSt[:, 0:512], xc, bs, xwb, xwb[:, g * 768:g * 768 + 512], True, True)
                P.mm(psSt, psSt[:, 512:768], xc, bs, xwb, xwb[:, g * 768 + 512:(g + 1) * 768], True, True)
                if t == 0:
                    P.copy("dve", S, S[:, g, :], psSt, psSt[:, 0:768])
                else:
                    sg3 = S[:, g, :].rearrange("p (h e) -> p h e", h=12)
                    P.tt("dve", S, sg3, S, sg3, m, m[:, 4, 12 * g:12 * g + 12].unsqueeze(2).to_broadcast([128, 12, 64]), ALU.mult)
                    P.tt("dve", S, S[:, g, :], S, S[:, g, :], psSt, psSt[:, 0:768], ALU.add)
            P.copy("act", Sb, Sb[:], S, S[:])
            for i in range(3):
                for j in range(4):
                    c = 4 * i + j
                    P.tr(psTr, psTr[:, j, :], yb, yb[:, c * 128:(c + 1) * 128], identb, identb[:])
                P.copy("act", yT, yT[:, 4 * i:4 * i + 4, :], psTr, psTr[:])
            a_ = att[b]
            for hf in range(2):
                o = psY[:, hf * 512:(hf + 1) * 512]
                for h in range(8):
                    P.mm(psY, o, a_, a_[0:64, h, :], Woa, Woa[0:64, h, hf * 512:(hf + 1) * 512], h == 0, False)
                for j in range(12):
                    P.mm(psY, o, yT, yT[:, j, :], Wos, Wos[:, j, hf * 512:(hf + 1) * 512], False, j == 11)
            P.tt("dve", xt[b], xt[b][:], xt[b], xt[b][:], psY, psY[:, 0:D_MODEL], ALU.add)
            P.dma("pool", OUT, out_d[r0:r0 + 128, :], xt[b], xt[b][:])
        P.barrier()
        P.emit()


def norm_T(P, xt_b, xt_ap, gbc, s, junk, hb, psT, hT, ident):
    P.act(junk, junk[:], xt_b, xt_ap, AF.Square, wr=[s], accum_out=s[:, 0:1])
    P.rstd(s, s[:, 0:1], s[:, 1:2], s[:, 2:3], 1.0 / D_MODEL)
    P.stt("dve", hb, hb[:], xt_b, xt_ap, s[:, 2:3], gbc, gbc[:], ALU.mult, ALU.mult, rd=[s])
    for kc in range(8):
        P.tr(psT, psT[:, kc, :], hb, hb[:, kc * 128:(kc + 1) * 128], ident, ident[:])
    P.copy("act", hT, hT[:], psT, psT[:])


def phase_d1(nc, P, SEQ, VEC, vec_d, CST, cst_d, WG, wg_d, WU, wu_d, WD, wd_d, OUT, out_d):
    NT = SEQ // 128
    with ExitStack() as ph:
        Wg = P.sb(ph, "Wg", [128, 8, D_FF], BF16)
        Wu = P.sb(ph, "Wu", [128, 8, D_FF], BF16)
        Wd = P.sb(ph, "Wd", [128, 22, D_MODEL], BF16)
        ident = P.sb(ph, "identD", [128, 128], BF16)
        gbc = P.sb(ph, "gffn", [128, D_MODEL], F32)
        xt = [P.sb(ph, "xtD%d" % i, [128, D_MODEL], F32) for i in range(3)]
        junk = P.sb(ph, "junkD", [128, D_MODEL], F32)
        ss = [P.sb(ph, "ssD%d" % i, [128, 8], F32) for i in range(2)]
        hb = P.sb(ph, "hbD", [128, D_MODEL], BF16)
        hT = [P.sb(ph, "hTD%d" % i, [128, 8, 128], BF16) for i in range(2)]
        sg = [P.sb(ph, "sgD%d" % i, [128, 512], F32) for i in range(2)]
        ab = P.sb(ph, "abD", [128, D_FF], BF16)
        aT = [P.sb(ph, "aTD%d" % i, [128, 22, 128], BF16) for i in range(2)]
        psT = P.ps(ph, "psTD", [128, 8, 128], BF16)
        psG = [P.ps(ph, "psGD%d" % i, [128, 512], F32) for i in range(2)]
        psU = [P.ps(ph, "psUD%d" % i, [128, 512], F32) for i in range(2)]
        psO = P.ps(ph, "psOD", [128, D_MODEL], F32)
        psA_ = P.ps(ph, "psAD", [128, 8, 128], BF16)
        psA = [psA_, Buf(psA_.t, "psAD_b", "ps")]
        for fc in range(22):
            P.dma("pool", Wg, Wg[:, :, fc * 128:(fc + 1) * 128], WG, wg_d[fc])
            P.dma("pool", Wu, Wu[:, :, fc * 128:(fc + 1) * 128], WU, wu_d[fc])
        for j in range(0, 22, 2):
            P.dma("pool", Wd, Wd[:, j:j + 2, :], WD, wd_d[:, j:j + 2, :])
        P.dma("pool", ident, ident[:], CST, cst_d[:, 0:128])
        P.dma("sp", gbc, gbc[:], VEC, bc_ap(vec_d, VO["ffn_norm"], [[0, 128], [1, D_MODEL]]))
        def nfront(t):
            b = t % 3
            s_ = ss[t % 2]
            P.act(junk, junk[:], xt[b], xt[b][:], AF.Square, wr=[s_], accum_out=s_[:, 0:1])
            P.rstd(s_, s_[:, 0:1], s_[:, 1:2], s_[:, 2:3], 1.0 / D_MODEL)
            P.stt("dve", hb, hb[:], xt[b], xt[b][:], s_[:, 2:3], gbc, gbc[:], ALU.mult, ALU.mult, rd=[s_])

        def nback(t):
            b = t % 2
            for kc in range(8):
                P.tr(psT, psT[:, kc, :], hb, hb[:, kc * 128:(kc + 1) * 128], ident, ident[:])
            P.copy("act", hT[b], hT[b][:], psT, psT[:])

        def ab_T(t, fc):
            n = 4 if fc < 5 else 2
            hh = (t * 6 + fc) % 2
            pa = psA[hh]
            for j in range(n):
                c = 4 * fc + j
                P.tr(pa, pa[:, 4 * hh + j, :], ab, ab[:, c * 128:(c + 1) * 128], ident, ident[:])
            P.copy("act", aT[t % 2], aT[t % 2][:, 4 * fc:4 * fc + n, :], pa, pa[:, 4 * hh:4 * hh + n, :])

        def down(t):
            a_ = aT[t % 2]
            x_ = xt[t % 3]
            for hf in range(2):
                for j in range(22):
                    P.mm(psO, psO[:, hf * 512:(hf + 1) * 512], a_, a_[:, j, :], Wd, Wd[:, j, hf * 512:(hf + 1) * 512], j == 0, j == 21)
            P.tt("dve", x_, x_[:], x_, x_[:], psO, psO[:], ALU.add)
            P.dma("pool", OUT, out_d[t * 128:(t + 1) * 128, :], x_, x_[:])

        P.dma("sp", xt[0], xt[0][:], OUT, out_d[0:128, :])
        if NT > 1:
            P.dma("sp", xt[1], xt[1][:], OUT, out_d[128:256, :])
        nfront(0)
        nback(0)
        cnt = 0
        for t in range(NT):
            b = t % 2
            for fc in range(6):
                f0 = fc * 512
                w = min(512, D_FF - f0)
                G_, U_, sg_ = psG[cnt % 2], psU[cnt % 2], sg[cnt % 2]
                cnt += 1
                for kc in range(8):
                    P.mm(G_, G_[:, 0:w], hT[b], hT[b][:, kc, :], Wg, Wg[:, kc, f0:f0 + w], kc == 0, kc == 7)
                for kc in range(8):
                    P.mm(U_, U_[:, 0:w], hT[b], hT[b][:, kc, :], Wu, Wu[:, kc, f0:f0 + w], kc == 0, kc == 7)
                P.act(sg_, sg_[:, 0:w], G_, G_[:, 0:w], AF.Silu)
                P.tt("dve", ab, ab[:, f0:f0 + w], sg_, sg_[:, 0:w], U_, U_[:, 0:w], ALU.mult)
                if fc == 0 and t > 0:
                    down(t - 1)
                if fc == 1 and t + 1 < NT:
                    if t + 2 < NT:
                        P.dma("sp", xt[(t + 2) % 3], xt[(t + 2) % 3][:], OUT, out_d[(t + 2) * 128:(t + 3) * 128, :])
                    nfront(t + 1)
                if fc >= 1:
                    ab_T(t, fc - 1)
                if fc == 4 and t + 1 < NT:
                    nback(t + 1)
            ab_T(t, 5)
        down(NT - 1)
        P.barrier()
        P.emit()


def phase_d2(nc, P, SEQ, VEC, vec_d, CST, cst_d, PIN, p_d, WPG, wpg_d, WPLE, wple_d, OUT, out_d):
    NT = SEQ // 128
    with ExitStack() as ph:
        Wpg = P.sb(ph, "Wpg", [128, 8, D_MODEL], BF16)
        Wpl = P.sb(ph, "Wpl", [128, 2, D_MODEL], BF16)
        ident = P.sb(ph, "identE", [128, 128], BF16)
        gbc = P.sb(ph, "gpg", [128, D_MODEL], F32)
        bbc = P.sb(ph, "bpg", [128, D_MODEL], F32)
        pgbc = P.sb(ph, "gple", [128, D_MODEL], F32)
        xt = [P.sb(ph, "xtE%d" % i, [128, D_MODEL], F32) for i in range(3)]
        pt = [P.sb(ph, "ptE%d" % i, [128, PLE], F32) for i in range(2)]
        pb = P.sb(ph, "pbE", [128, PLE], BF16)
        pT = [P.sb(ph, "pTE%d" % i, [128, 2, 128], BF16) for i in range(2)]
        junk = P.sb(ph, "junkE", [128, D_MODEL], F32)
        ss = [P.sb(ph, "ssE%d" % i, [128, 8], F32) for i in range(2)]
        hb = P.sb(ph, "hbE", [128, D_MODEL], BF16)
        hT = [P.sb(ph, "hTE%d" % i, [128, 8, 128], BF16) for i in range(2)]
        gt = P.sb(ph, "gtE", [128, D_MODEL], F32)
        en = P.sb(ph, "enE", [128, D_MODEL], F32)
        psT = P.ps(ph, "psTE", [128, 8, 128], BF16)
        psP = P.ps(ph, "psPE", [128, 4, 128], BF16)
        psGt = P.ps(ph, "psGt", [128, D_MODEL], F32)
        psE = P.ps(ph, "psE", [128, D_MODEL], F32)
        for kc in range(0, 8, 2):
            P.dma("pool", Wpg, Wpg[:, kc:kc + 2, :], WPG, wpg_d[:, kc:kc + 2, :])
        P.dma("pool", Wpl, Wpl[:], WPLE, wple_d[:, :, :])
        P.dma("pool", ident, ident[:], CST, cst_d[:, 0:128])
        P.dma("sp", gbc, gbc[:], VEC, bc_ap(vec_d, VO["ple_gate_norm"], [[0, 128], [1, D_MODEL]]))
        P.dma("sp", bbc, bbc[:], VEC, bc_ap(vec_d, VO["b_ple_gate"], [[0, 128], [1, D_MODEL]]))
        P.dma("sp", pgbc, pgbc[:], VEC, bc_ap(vec_d, VO["ple_norm"], [[0, 128], [1, D_MODEL]]))
        def load(t):
            r0 = t * 128
            P.dma("sp", xt[t % 3], xt[t % 3][:], OUT, out_d[r0:r0 + 128, :])
            P.dma("sp", pt[t % 2], pt[t % 2][:], PIN, p_d[r0:r0 + 128, :])

        def s1(t):
            x_, s_ = xt[t % 3], ss[t % 2]
            P.act(junk, junk[:], x_, x_[:], AF.Square, wr=[s_], accum_out=s_[:, 0:1])
            P.rstd(s_, s_[:, 0:1], s_[:, 1:2], s_[:, 2:3], 1.0 / D_MODEL)
            P.stt("dve", hb, hb[:], x_, x_[:], s_[:, 2:3], gbc, gbc[:], ALU.mult, ALU.mult, rd=[s_])
            P.copy("dve", pb, pb[:], pt[t % 2], pt[t % 2][:])

        def s2(t):
            b = t % 2
            for kc in range(8):
                P.tr(psT, psT[:, kc, :], hb, hb[:, kc * 128:(kc + 1) * 128], ident, ident[:])
            P.copy("act", hT[b], hT[b][:], psT, psT[:])
            for j in range(2):
                P.tr(psP, psP[:, j, :], pb, pb[:, j * 128:(j + 1) * 128], ident, ident[:])
            P.copy("act", pT[b], pT[b][:], psP, psP[:, 0:2, :])
            for hf in range(2):
                for kc in range(8):
                    P.mm(psGt, psGt[:, hf * 512:(hf + 1) * 512], hT[b], hT[b][:, kc, :], Wpg, Wpg[:, kc, hf * 512:(hf + 1) * 512], kc == 0, kc == 7)
            for hf in range(2):
                for j in range(2):
                    P.mm(psE, psE[:, hf * 512:(hf + 1) * 512], pT[b], pT[b][:, j, :], Wpl, Wpl[:, j, hf * 512:(hf + 1) * 512], j == 0, j == 1)

        def s3(t):
            x_, s_ = xt[t % 3], ss[t % 2]
            P.tt("dve", gt, gt[:], psGt, psGt[:], bbc, bbc[:], ALU.add)
            P.act(gt, gt[:], gt, gt[:], AF.Sigmoid)
            P.act(junk, junk[:], psE, psE[:], AF.Square, wr=[s_], accum_out=s_[:, 4:5])
            P.rstd(s_, s_[:, 4:5], s_[:, 5:6], s_[:, 6:7], 1.0 / D_MODEL)
            P.stt("dve", en, en[:], psE, psE[:], s_[:, 6:7], pgbc, pgbc[:], ALU.mult, ALU.mult, rd=[s_])
            P.tt("pool", en, en[:], en, en[:], gt, gt[:], ALU.mult)
            P.tt("dve", x_, x_[:], x_, x_[:], en, en[:], ALU.add)
            P.dma("pool", OUT, out_d[t * 128:(t + 1) * 128, :], x_, x_[:])

        load(0)
        if NT > 1:
            load(1)
        s1(0)
        s2(0)
        for t in range(NT):
            if t + 2 < NT:
                load(t + 2)
            if t + 1 < NT:
                s1(t + 1)
            s3(t)
            if t + 1 < NT:
                s2(t + 1)
        P.barrier()
        P.emit()


def make_consts():
    c = np.zeros((128, 1024), np.float32)
    i = np.arange(128)
    c[:, 0:128] = np.eye(128, dtype=np.float32)
    c[:, 128:256] = (i[:, None] <= i[None, :])
    c[:, 256:384] = (i[:, None] >= i[None, :])
    c[:, 384:512] = (i[:, None] <= i[None, :])
    c[:, 512:640] = np.where(i[:, None] >= i[None, :], 0.0, -30000.0)
    c[:, 640:768] = np.where(i[:, None] <= i[None, :], 0.0, -30000.0)
    return c


def pack_inputs(inp, SEQ=SEQ_FULL):
    f = lambda a: np.ascontiguousarray(np.asarray(a, dtype=np.float32))
    vec = np.zeros((1, NVEC), np.float32)
    for k, o in VO.items():
        a = f(inp[k][0]).reshape(-1)
        vec[0, o:o + a.size] = a
    shared = dict(
        w_in=f(f(inp["w_in"][0]).reshape(8, 128, IN_PROJ).transpose(1, 0, 2)),
        w_out_a=f(f(inp["w_out"][0])[0:AW].reshape(8, 64, D_MODEL).transpose(1, 0, 2)),
        w_out_s=f(f(inp["w_out"][0])[AW:].reshape(12, 128, D_MODEL).transpose(1, 0, 2)),
        w_g=f(f(inp["w_ffn_gate"][0]).reshape(8, 128, 22, 128).transpose(2, 1, 0, 3)),
        w_u=f(f(inp["w_ffn_up"][0]).reshape(8, 128, 22, 128).transpose(2, 1, 0, 3)),
        w_d=f(f(inp["w_ffn_down"][0]).reshape(22, 128, D_MODEL).transpose(1, 0, 2)),
        w_pg=f(f(inp["w_ple_gate"][0]).reshape(8, 128, D_MODEL).transpose(1, 0, 2)),
        w_ple=f(f(inp["w_ple"][0]).reshape(2, 128, D_MODEL).transpose(1, 0, 2)),
        vecs=vec,
        consts=make_consts(),
    )
    maps = []
    for b in range(inp["x"].shape[0]):
        m = dict(shared)
        m["x"] = f(inp["x"][b][:SEQ])
        m["p"] = f(inp["p"][0][b][:SEQ])
        maps.append(m)
    return maps


_NC_CACHE = {}


def kernel(**inputs):
    maps = pack_inputs(inputs)
    if "nc" not in _NC_CACHE:
        _NC_CACHE["nc"] = build()
    nc = _NC_CACHE["nc"]
    res = run_bass_kernel_spmd(nc, maps, core_ids=list(range(len(maps))))
    return np.stack([np.asarray(r["out"], dtype=np.float32) for r in res.results], axis=0)
```

```python
from contextlib import ExitStack
import numpy as np
import concourse.bass as bass
import concourse.mybir as mybir
from concourse.bass_utils import run_bass_kernel_spmd

F32 = mybir.dt.float32
BF16 = mybir.dt.bfloat16
AF = mybir.ActivationFunctionType
ALU = mybir.AluOpType
AX = mybir.AxisListType

ENG = ["pe", "act", "dve", "pool", "sp"]

D_MODEL = 1024
SEQ_FULL = 8192
NH_A = 8
HD = 64
AW = 512
NH_S = 24
SW = 1536
CONV_DIM = 2048
NSTATE = 128
IN_PROJ = 5144
D_FF = 2816
PLE = 256
EPS = 1e-6


class Buf:
    def __init__(self, t, name, space):
        self.t = t
        self.name = name
        self.space = space
        self.w = {}
        self.r = {}

    def __getitem__(self, k):
        return self.t[k]


class Prog:
    def __init__(self, nc, stack):
        self.nc = nc
        self.stack = stack
        self.q = {e: [] for e in ENG}
        self.emitted = {e: 0 for e in ENG}
        self.known = {e: {} for e in ENG}
        self.semval = {e: 0 for e in ENG}
        self.resolved = {e: {} for e in ENG}
        self.esem = {e: stack.enter_context(nc.semaphore("es_" + e)) for e in ENG}
        self.dsem = {}
        self.dcount = {}

    def sb(self, stack, name, shape, dtype):
        t = stack.enter_context(self.nc.sbuf_tensor(name, list(shape), dtype))
        return Buf(t, name, "sb")

    def ps(self, stack, name, shape, dtype=F32):
        t = stack.enter_context(self.nc.psum_tensor(name, list(shape), dtype))
        return Buf(t, name, "ps")

    def dr(self, t, name):
        return Buf(t, name, "dr")

    def _collect(self, e, reads, writes, skipkey=None):
        waits = {}

        def need(k, v):
            if k == skipkey:
                return
            if k == "pe" and e == "pe":
                return
            if self.known[e].get(k, -1) >= v:
                return
            if waits.get(k, -1) < v:
                waits[k] = v

        for b in reads:
            for k, v in b.w.items():
                need(k, v)
        for b in writes:
            for k, v in b.w.items():
                need(k, v)
            for k, v in b.r.items():
                need(k, v)
        for k, v in waits.items():
            self.known[e][k] = v
            if k in self.q:
                self.q[k][v]["inc"] = True
        return list(waits.items())

    def op(self, e, fn, reads=(), writes=()):
        idx = len(self.q[e])
        waits = self._collect(e, reads, writes)
        self.q[e].append(dict(fn=fn, waits=waits, inc=False, dkey=None))
        for b in reads:
            b.r[e] = idx
        for b in writes:
            b.w = {e: idx}
            b.r = {}
        return idx

    def dma(self, e, dst, dst_ap, src, src_ap, **kw):
        if dst.space != "dr":
            key = ("in", dst.name)
        elif src.space != "dr":
            key = ("out", src.name)
        else:
            key = ("dd", dst.name)
        waits = self._collect(e, [src], [dst], skipkey=key)
        cnt = self.dcount.get(key, 0) + 1
        self.dcount[key] = cnt
        val = 16 * cnt

        def fn(eng, dst_ap=dst_ap, src_ap=src_ap, kw=kw):
            return eng.dma_start(out=dst_ap, in_=src_ap, **kw)

        self.q[e].append(dict(fn=fn, waits=waits, inc=False, dkey=key))
        src.r[key] = val
        keep = {k: v for k, v in dst.w.items() if k == key}
        dst.w = keep
        dst.w[key] = val
        dst.r = {}

    def barrier(self):
        for e in ENG:
            waits = {}
            for k in ENG:
                if k == e or not self.q[k] or k == "sp":
                    continue
                v = -1
                for i in range(len(self.q[k]) - 1, -1, -1):
                    if self.q[k][i]["fn"] is not None and self.q[k][i]["dkey"] is None:
                        v = i
                        break
                if v < 0:
                    continue
                if self.known[e].get(k, -1) < v:
                    waits[k] = v
            for k, c in self.dcount.items():
                v = 16 * c
                if self.known[e].get(k, -1) < v:
                    waits[k] = v
            for k, v in waits.items():
                self.known[e][k] = v
                if k in self.q:
                    self.q[k][v]["inc"] = True
            self.q[e].append(dict(fn=None, waits=list(waits.items()), inc=False, dkey=None))

    def _sem(self, k):
        if k in self.esem:
            return self.esem[k]
        if k not in self.dsem:
            self.dsem[k] = self.stack.enter_context(
                self.nc.semaphore("ds%d" % len(self.dsem)))
        return self.dsem[k]

    def emit(self):
        nc = self.nc
        for e in ENG:
            for i in range(self.emitted[e], len(self.q[e])):
                r = self.q[e][i]
                if r["inc"] and r["fn"] is not None and r["dkey"] is None:
                    self.semval[e] += 1
                    self.resolved[e][i] = self.semval[e]
        for k in list(self.dcount):
            self._sem(k)

        def run(e, eng):
            recs = self.q[e]
            for i in range(self.emitted[e], len(recs)):
                rec = recs[i]
                for k, v in rec["waits"]:
                    if k in self.esem:
                        v = self.resolved[k][v]
                    eng.wait_ge(self._sem(k), v)
                if rec["fn"] is None:
                    continue
                ins = rec["fn"](eng)
                if rec["dkey"] is not None:
                    ins.then_inc(self._sem(rec["dkey"]), 16)
                elif rec["inc"]:
                    ins.then_inc(self.esem[e], 1)
            self.emitted[e] = len(recs)

        with nc.Block() as block:
            @block.tensor
            def _(eng):
                run("pe", eng)

            @block.scalar
            def _(eng):
                run("act", eng)

            @block.vector
            def _(eng):
                run("dve", eng)

            @block.gpsimd
            def _(eng):
                run("pool", eng)

            @block.sync
            def _(eng):
                run("sp", eng)

    def act(self, ob, o, ib, i, func, rd=(), wr=(), **kw):
        self.op("act", lambda e: e.activation(out=o, in_=i, func=func, **kw),
                [ib, *rd], [ob, *wr])

    def tt(self, eng, ob, o, ab, a, bb, b, op):
        self.op(eng, lambda e: e.tensor_tensor(out=o, in0=a, in1=b, op=op), [ab, bb], [ob])

    def ts(self, eng, ob, o, ab, a, s1, s2, op0, op1=None, rd=()):
        if op1 is None:
            self.op(eng, lambda e: e.tensor_scalar(out=o, in0=a, scalar1=s1, scalar2=None, op0=op0),
                    [ab, *rd], [ob])
        else:
            self.op(eng, lambda e: e.tensor_scalar(out=o, in0=a, scalar1=s1, scalar2=s2, op0=op0, op1=op1),
                    [ab, *rd], [ob])

    def stt(self, eng, ob, o, ab, a, s, bb, b, op0, op1, rd=(), wr=(), **kw):
        self.op(eng, lambda e: e.scalar_tensor_tensor(out=o, in0=a, scalar=s, in1=b, op0=op0, op1=op1, **kw),
                [ab, bb, *rd], [ob, *wr])

    def copy(self, eng, ob, o, ib, i):
        if eng == "act":
            self.op("act", lambda e: e.activation(out=o, in_=i, func=AF.Copy), [ib], [ob])
        else:
            self.op(eng, lambda e: e.tensor_copy(out=o, in_=i), [ib], [ob])

    def mm(self, ob, o, lb, l, rb, r, start, stop):
        self.op("pe", lambda e: e.matmul(o, lhsT=l, rhs=r, start=start, stop=stop), [lb, rb], [ob])

    def tr(self, ob, o, ib, i, idb, idap):
        self.op("pe", lambda e: e.transpose(o, i, idap), [ib, idb], [ob])

    def rstd(self, ssb, src, tmp, dst, inv_n):
        self.ts("dve", ssb, tmp, ssb, src, inv_n, EPS, ALU.mult, ALU.add)
        self.act(ssb, tmp, ssb, tmp, AF.Sqrt)
        self.op("dve", lambda e: e.reciprocal(out=dst, in_=tmp), [ssb], [ssb])


def bc_ap(handle, offset, dims):
    return bass.AP(tensor=handle, offset=offset, ap=[list(d) for d in dims])


def build(SEQ=SEQ_FULL, debug=False, phases="ABCD"):
    nc = bass.Bass("TRN2", target_bir_lowering=False)
    NT = SEQ // 128
    skind = "ExternalOutput" if debug else "Internal"

    def din(name, shape):
        return nc.dram_tensor(name, list(shape), F32, kind="ExternalInput")

    x_d = din("x", [SEQ, D_MODEL])
    p_d = din("p", [SEQ, PLE])
    win_d = din("w_in", [128, 8, IN_PROJ])
    wout_a_d = din("w_out_a", [64, 8, D_MODEL])
    wout_s_d = din("w_out_s", [128, 12, D_MODEL])
    wg_d = din("w_g", [22, 128, 8, 128])
    wu_d = din("w_u", [22, 128, 8, 128])
    wd_d = din("w_d", [128, 22, D_MODEL])
    wpg_d = din("w_pg", [128, 8, D_MODEL])
    wple_d = din("w_ple", [128, 2, D_MODEL])
    vec_d = din("vecs", [1, 17408])
    cst_d = din("consts", [128, 1024])
    out_d = nc.dram_tensor("out", [SEQ, D_MODEL], F32, kind="ExternalOutput")

    qT_d = nc.dram_tensor("qT_s", [128, 4, SEQ], BF16, kind=skind)
    kT_d = nc.dram_tensor("kT_s", [128, 4, SEQ], BF16, kind=skind)
    v_d = nc.dram_tensor("v_s", [SEQ, 8, 128], BF16, kind=skind)
    u_d = nc.dram_tensor("u_s", [SEQ + 3, CONV_DIM], BF16, kind=skind)
    sz_d = nc.dram_tensor("sz_s", [SEQ, SW], BF16, kind=skind)
    dt_d = nc.dram_tensor("dt_s", [SEQ, NH_S], F32, kind=skind)
    attnT_d = nc.dram_tensor("attnT_s", [64, 8, SEQ], BF16, kind=skind)

    with ExitStack() as st:
        P = Prog(nc, st)
        X = P.dr(x_d, "x")
        WIN = P.dr(win_d, "w_in")
        VEC = P.dr(vec_d, "vecs")
        CST = P.dr(cst_d, "consts")
        QT = P.dr(qT_d, "qT_s")
        KT = P.dr(kT_d, "kT_s")
        V = P.dr(v_d, "v_s")
        U = P.dr(u_d, "u_s")
        SZ = P.dr(sz_d, "sz_s")
        DT = P.dr(dt_d, "dt_s")

        AT = P.dr(attnT_d, "attnT_s")
        if "A" in phases:
            phase_a(nc, P, SEQ, X, x_d, WIN, win_d, VEC, vec_d, CST, cst_d,
                    QT, qT_d, KT, kT_d, V, v_d, U, u_d, SZ, sz_d, DT, dt_d)
        if "B" in phases:
            phase_b(nc, P, SEQ, CST, cst_d, QT, qT_d, KT, kT_d, V, v_d, AT, attnT_d)
        OUT = P.dr(out_d, "out")
        if "C" in phases:
            phase_c(nc, P, SEQ, X, x_d, VEC, vec_d, CST, cst_d, U, u_d, SZ, sz_d, DT, dt_d,
                    AT, attnT_d, P.dr(wout_a_d, "w_out_a"), wout_a_d, P.dr(wout_s_d, "w_out_s"), wout_s_d,
                    OUT, out_d)
        if "D" in phases:
            phase_d1(nc, P, SEQ, VEC, vec_d, CST, cst_d, P.dr(wg_d, "w_g"), wg_d, P.dr(wu_d, "w_u"), wu_d,
                     P.dr(wd_d, "w_d"), wd_d, OUT, out_d)
            phase_d2(nc, P, SEQ, VEC, vec_d, CST, cst_d, P.dr(p_d, "p"), p_d, P.dr(wpg_d, "w_pg"), wpg_d,
                     P.dr(wple_d, "w_ple"), wple_d, OUT, out_d)
    return nc


VO = dict(mix_norm=0, q_norm=1024, k_norm=1088, conv_w=1152, conv_b=1152 + 8192,
          dt_bias=11392, a_log=11416, d_skip=11440, ssm_norm=11464, ffn_norm=13000,
          ple_gate_norm=14024, b_ple_gate=15048)
VO["ple_norm"] = 16072
NVEC = 17408


def phase_a(nc, P, SEQ, X, x_d, WIN, win_d, VEC, vec_d, CST, cst_d,
            QT, qT_d, KT, kT_d, V, v_d, U, u_d, SZ, sz_d, DT, dt_d):
    NT = SEQ // 128
    with ExitStack() as ph:
        Win = P.sb(ph, "Win", [128, 8, IN_PROJ], BF16)
        ident = P.sb(ph, "identA", [128, 128], BF16)
        gmix = P.sb(ph, "gmix", [128, D_MODEL], F32)
        qg = P.sb(ph, "qg", [128, 8, 64], F32)
        kg = P.sb(ph, "kg", [128, 8, 64], F32)
        dtb = P.sb(ph, "dtb", [128, NH_S], F32)
        zer = P.sb(ph, "zer", [4, CONV_DIM], BF16)
        xt = [P.sb(ph, "xt%d" % i, [128, D_MODEL], F32) for i in range(2)]
        junk = P.sb(ph, "junkA", [128, D_MODEL], F32)
        ss = [P.sb(ph, "ssA%d" % i, [128, 8], F32) for i in range(2)]
        hb = P.sb(ph, "hb", [128, D_MODEL], BF16)
        hT = [P.sb(ph, "hT%d" % i, [128, 8, 128], BF16) for i in range(2)]
        tmpf = [P.sb(ph, "tmpf%d" % i, [128, 8, 64], F32) for i in range(2)]
        tmpg = [P.sb(ph, "tmpg%d" % i, [128, 8, 64], F32) for i in range(2)]
        ssh = [P.sb(ph, "ssh%d" % i, [128, 24], F32) for i in range(2)]
        qb = [P.sb(ph, "qb%d" % i, [128, 512], BF16) for i in range(2)]
        qst = [P.sb(ph, "qst%d" % i, [128, 4, 512], BF16) for i in range(2)]
        kst = [P.sb(ph, "kst%d" % i, [128, 4, 512], BF16) for i in range(2)]
        vb = [P.sb(ph, "vb%d" % i, [128, 8, 128], BF16) for i in range(2)]
        szb = [P.sb(ph, "szb%d" % i, [128, SW], BF16) for i in range(2)]
        ub = [P.sb(ph, "ub%d" % i, [128, CONV_DIM], BF16) for i in range(2)]
        dts = [P.sb(ph, "dts%d" % i, [128, 4, NH_S], F32) for i in range(2)]
        psT = P.ps(ph, "psT", [128, 8, 128], BF16)
        psQ = P.ps(ph, "psQ", [128, 4, 128], BF16)
        pj = [P.ps(ph, "pj%d" % i, [128, 512], F32) for i in range(4)]

        for kc in range(8):
            P.dma("pool", Win, Win[:, kc, :], WIN, win_d[:, kc, :])
        P.dma("pool", ident, ident[:], CST, cst_d[:, 0:128])
        P.dma("sp", gmix, gmix[:], VEC, bc_ap(vec_d, VO["mix_norm"], [[0, 128], [1, D_MODEL]]))
        P.dma("sp", qg, qg[:], VEC, bc_ap(vec_d, VO["q_norm"], [[0, 128], [0, 8], [1, 64]]))
        P.dma("sp", kg, kg[:], VEC, bc_ap(vec_d, VO["k_norm"], [[0, 128], [0, 8], [1, 64]]))
        P.dma("sp", dtb, dtb[:], VEC, bc_ap(vec_d, VO["dt_bias"], [[0, 128], [1, NH_S]]))
        P.ts("dve", qg, qg[:], qg, qg[:], HD ** -0.5, None, ALU.mult)
        P.op("dve", lambda e: e.memset(zer[:], 0.0), [], [zer])
        for vb_ in vb:
            P.op("dve", lambda e, vb_=vb_: e.memset(vb_[:], 1.0), [], [vb_])
        P.dma("pool", U, u_d[0:3, :], zer, zer[0:3, :])

        def norm_front(t):
            b = t % 2
            s_ = ss[b]
            P.act(junk, junk[:], xt[b], xt[b][:], AF.Square, wr=[s_], accum_out=s_[:, 0:1])
            P.rstd(s_, s_[:, 0:1], s_[:, 1:2], s_[:, 2:3], 1.0 / D_MODEL)
            P.stt("dve", hb, hb[:], xt[b], xt[b][:], s_[:, 2:3], gmix, gmix[:], ALU.mult, ALU.mult, rd=[s_])
            if t + 2 < NT:
                P.dma("sp", xt[b], xt[b][:], X, x_d[(t + 2) * 128:(t + 3) * 128, :])

        def norm_back(t):
            b = t % 2
            for kc in range(8):
                P.tr(psT, psT[:, kc, :], hb, hb[:, kc * 128:(kc + 1) * 128], ident, ident[:])
            P.copy("act", hT[b], hT[b][:], psT, psT[:])

        def qk_back(t, j):
            qbb = qb[j]
            stg = (qst if j == 0 else kst)[(t // 4) % 2]
            for pr in range(4):
                P.tr(psQ, psQ[:, pr, :], qbb, qbb[:, pr * 128:(pr + 1) * 128], ident, ident[:])
            tc0 = (t % 4) * 128
            P.copy("act", stg, stg[:, :, tc0:tc0 + 128], psQ, psQ[:])
            if t % 4 == 3 or t == NT - 1:
                t0 = (t // 4) * 512
                n = (t % 4 + 1) * 128
                D_, d_ = (QT, qT_d) if j == 0 else (KT, kT_d)
                P.dma("pool", D_, d_[:, :, t0:t0 + n], stg, stg[:, :, 0:n])

        P.dma("sp", xt[0], xt[0][:], X, x_d[0:128, :])
        if NT > 1:
            P.dma("sp", xt[1], xt[1][:], X, x_d[128:256, :])
        norm_front(0)
        norm_back(0)
        cnt = 0
        for t in range(NT):
            b = t % 2
            for j in range(11):
                c0 = j * 512
                w = min(512, IN_PROJ - c0)
                pb = pj[cnt % 4]
                cnt += 1
                for kc in range(8):
                    P.mm(pb, pb[:, 0:w], hT[b], hT[b][:, kc, :], Win, Win[:, kc, c0:c0 + w], kc == 0, kc == 7)
                if j < 2:
                    g = qg if j == 0 else kg
                    tf, tg, sh, qbb = tmpf[j], tmpg[j], ssh[j], qb[j]
                    pv = pb[:, :].rearrange("p (h e) -> p h e", h=8)
                    P.act(tf, tf[:], pb, pv, AF.Square)
                    P.op("dve", lambda e, sh=sh, tf=tf: e.tensor_reduce(out=sh[:, 0:8], in_=tf[:], axis=AX.X, op=ALU.add),
                         [tf], [sh])
                    P.rstd(sh, sh[:, 0:8], sh[:, 8:16], sh[:, 16:24], 1.0 / HD)
                    P.tt("dve", tg, tg[:], pb, pv, sh, sh[:, 16:24].unsqueeze(2).to_broadcast([128, 8, 64]), ALU.mult)
                    P.tt("pool", qbb, qbb[:].rearrange("p (h e) -> p h e", h=8), tg, tg[:], g, g[:], ALU.mult)
                elif j == 2:
                    P.copy("act", vb[b], vb[b][:, :, 0:64], pb, pb[:, :].rearrange("p (h e) -> p h e", h=8))
                    P.dma("pool", V, v_d[t * 128:(t + 1) * 128, :, :], vb[b], vb[b][:])
                    if t + 1 < NT:
                        norm_front(t + 1)
                elif j < 6:
                    jj = j - 3
                    P.act(szb[b], szb[b][:, jj * 512:(jj + 1) * 512], pb, pb[:], AF.Silu)
                    if jj == 2:
                        P.dma("pool", SZ, sz_d[t * 128:(t + 1) * 128, :], szb[b], szb[b][:])
                    if jj == 0:
                        qk_back(t, 0)
                    if jj == 2:
                        qk_back(t, 1)
                elif j < 10:
                    jj = j - 6
                    P.copy("dve", ub[b], ub[b][:, jj * 512:(jj + 1) * 512], pb, pb[:])
                    if jj == 3:
                        P.dma("pool", U, u_d[3 + t * 128:3 + (t + 1) * 128, :], ub[b], ub[b][:])
                    if jj == 1 and t + 1 < NT:
                        norm_back(t + 1)
                else:
                    d = dts[b]
                    P.tt("dve", d, d[:, 0, :], pb, pb[:, 0:NH_S], dtb, dtb[:], ALU.add)
                    P.stt("dve", d, d[:, 1, :], d, d[:, 0, :], -1.0, d, d[:, 0, :], ALU.mult, ALU.max)
                    P.act(d, d[:, 1, :], d, d[:, 1, :], AF.Exp, scale=-1.0)
                    P.act(d, d[:, 1, :], d, d[:, 1, :], AF.Ln, bias=1.0)
                    P.stt("dve", d, d[:, 2, :], d, d[:, 0, :], 0.0, d, d[:, 1, :], ALU.max, ALU.add)
                    P.dma("pool", DT, dt_d[t * 128:(t + 1) * 128, :], d, d[:, 2, :])
        P.barrier()
        P.emit()


PATTERNS = (1, 4, 16)
B_STOP = 99
C_STOP = 99
C_VAR = 0
HI_LIST = (0, 1)
MASK_ENG = 'pool'


def phase_b(nc, P, SEQ, CST, cst_d, QT, qT_d, KT, kT_d, V, v_d, AT, attnT_d):
    SBW = 2048
    NSB = SEQ // SBW
    with ExitStack() as ph:
        kTw = [P.sb(ph, "kTw%d" % i, [128, 4, SBW], BF16) for i in range(2)]
        qTw = [P.sb(ph, "qTw%d" % i, [128, 4, SBW], BF16) for i in range(2)]
        acc = P.sb(ph, "acc", [128, 8, SBW], F32)
        rd = P.sb(ph, "rd", [64, 8, 512], F32)
        ast = [P.sb(ph, "ast%d" % i, [64, 8, 512], BF16) for i in range(2)]
        vt = [P.sb(ph, "vt%d" % i, [128, 8, 128], BF16) for i in range(6)]
        pT = [P.sb(ph, "pT%d" % i, [128, 512], BF16) for i in range(3)]
        mask4 = P.sb(ph, "mask4", [128, 512], BF16)
        maskc = P.sb(ph, "maskc", [128, 256], BF16)
        psS = [P.ps(ph, "psS%d" % i, [128, 2, 512], F32) for i in range(2)]
        psN = [P.ps(ph, "psN%d" % i, [128, 2, 128], F32) for i in range(3)]

        P.dma("pool", mask4, mask4[:, 0:256], CST, cst_d[:, 256:512])
        P.dma("pool", mask4, mask4[:, 256:512], CST, cst_d[:, 256:512])
        P.dma("pool", maskc, maskc[:, 0:128], CST, cst_d[:, 384:512])
        P.dma("pool", maskc, maskc[:, 128:256], CST, cst_d[:, 384:512])

        vi = 0
        ui = 0
        for sb in range(NSB):
            T0 = sb * SBW
            kc_, qc_ = kTw[sb % 2], qTw[sb % 2]
            kp_ = kTw[(sb - 1) % 2]
            P.dma("sp", kc_, kc_[:], KT, kT_d[:, :, T0:T0 + SBW])
            P.dma("sp", qc_, qc_[:], QT, qT_d[:, :, T0:T0 + SBW])
            units = []
            for d in PATTERNS:
                span = 128 * d
                for b in range(SBW // span):
                    for r in range(d):
                        q0 = b * span + r
                        g0 = T0 + q0
                        has_prev = (T0 + b * span) >= span
                        for hp in range(4):
                            units.append((d, b, r, q0, g0, has_prev, hp))
            state = {}

            def front(u):
                nonlocal vi, ui
                d, b, r, q0, g0, has_prev, hp = units[u]
                span = 128 * d
                if hp == 0:
                    kbs = []
                    if has_prev:
                        vp = vt[vi % 6]
                        vi += 1
                        P.dma("sp", vp, vp[:], V, v_d[g0 - span:g0 - span + (127 * d + 1):d, :, :])
                        if b == 0:
                            kbs.append((kp_, SBW - span + r, vp))
                        else:
                            kbs.append((kc_, q0 - span, vp))
                    vc = vt[vi % 6]
                    vi += 1
                    P.dma("sp", vc, vc[:], V, v_d[g0:g0 + (127 * d + 1):d, :, :])
                    kbs.append((kc_, q0, vc))
                    state["kbs"] = kbs
                kbs = state["kbs"]
                nk = len(kbs)
                qs = slice(q0, q0 + 127 * d + 1, d)
                S = psS[ui % 2]
                pt = pT[ui % 3]
                N = psN[ui % 3]
                ui += 1
                W = 2 * nk * 128
                for hi in range(2):
                    pl = slice(64 * hi, 64 * hi + 64)
                    for ki, (kb, k0, _) in enumerate(kbs):
                        c = ki * 128
                        P.mm(S, S[:, hi, c:c + 128], kb, kb[pl, hp, k0:k0 + 127 * d + 1:d],
                             qc_, qc_[pl, hp, qs], True, True)
                P.act(pt, pt[:, 0:W].rearrange("p (h c) -> p h c", h=2), S, S[:, :, 0:nk * 128], AF.Exp)
                mk = mask4 if nk == 2 else maskc
                P.tt("dve", pt, pt[:, 0:W], pt, pt[:, 0:W], mk, mk[:, 0:W], ALU.mult)
                return (kbs, nk, qs, pt, N, hp, d == 1)

            def back(ctx):
                kbs, nk, qs, pt, N, hp, first = ctx
                for hi in range(2):
                    h = 2 * hp + hi
                    for ki, (kb, k0, vb_) in enumerate(kbs):
                        c = (hi * nk + ki) * 128
                        P.mm(N, N[:, hi, :], vb_, vb_[:, h, :], pt, pt[:, c:c + 128], ki == 0, ki == nk - 1)
                an = acc[:, 2 * hp:2 * hp + 2, qs]
                if first:
                    P.copy("dve", acc, an, N, N[:])
                else:
                    P.tt("dve", acc, an, acc, an, N, N[:], ALU.add)

            prev_ctx = None
            for u in range(len(units)):
                ctx = front(u)
                if prev_ctx is not None:
                    back(prev_ctx)
                prev_ctx = ctx
            back(prev_ctx)
            for c in range(SBW // 512):
                c0 = c * 512
                a_ = ast[c % 2]
                P.op("dve", lambda e, c0=c0: e.reciprocal(out=rd[:], in_=acc[64:128, :, c0:c0 + 512]), [acc], [rd])
                P.tt("dve", a_, a_[:], acc, acc[0:64, :, c0:c0 + 512], rd, rd[:], ALU.mult)
                P.dma("pool", AT, attnT_d[:, :, T0 + c0:T0 + c0 + 512], a_, a_[:])
        P.barrier()
        P.emit()


def phase_c(nc, P, SEQ, X, x_d, VEC, vec_d, CST, cst_d, U, u_d, SZ, sz_d, DT, dt_d,
            AT, attnT_d, WOA, woa_d, WOS, wos_d, OUT, out_d):
    NT = SEQ // 128
    with ExitStack() as ph:
        trif = P.sb(ph, "trif", [128, 128], F32)
        identb = P.sb(ph, "identC", [128, 128], BF16)
        onesf = P.sb(ph, "onesfC", [128, 128], F32)
        cw = P.sb(ph, "cw", [128, 4, CONV_DIM], F32)
        cb = P.sb(ph, "cb", [128, CONV_DIM], F32)
        abc = P.sb(ph, "abc", [128, NH_S], F32)
        dbc = P.sb(ph, "dbc", [128, NH_S], F32)
        sgbc = P.sb(ph, "sgbc", [128, SW], F32)
        Woa = P.sb(ph, "Woa", [64, 8, D_MODEL], BF16)
        Wos = P.sb(ph, "Wos", [128, 12, D_MODEL], BF16)
        uk = [P.sb(ph, "uk%d" % k, [128, CONV_DIM], BF16) for k in range(4)]
        tk = [P.sb(ph, "tk%d" % k, [128, CONV_DIM], F32) for k in range(4)]
        xcb = [P.sb(ph, "xcb%d" % i, [128, CONV_DIM], BF16) for i in range(2)]
        szt = P.sb(ph, "szt", [128, SW], BF16)
        dtt = [P.sb(ph, "dtt%d" % i, [128, NH_S], F32) for i in range(2)]
        att = [P.sb(ph, "att%d" % i, [64, 8, 128], BF16) for i in range(2)]
        xt = [P.sb(ph, "xtC%d" % i, [128, D_MODEL], F32) for i in range(2)]
        sm = [P.sb(ph, "smC%d" % i, [128, 8, NH_S], F32) for i in range(2)]
        st2 = P.sb(ph, "st2", [128, 8], F32)
        BCT = P.sb(ph, "BCT", [128, 4, 128], BF16)
        Gm = P.sb(ph, "Gm", [128, 2, 128], F32)
        dtmp = [P.sb(ph, "dtmp%d" % i, [128, 4, 128], F32) for i in range(2)]
        A4 = [P.sb(ph, "A4%d" % i, [128, 4, 128], F32) for i in range(2)]
        upf = P.sb(ph, "upf", [128, 128], F32)
        Mt = [P.sb(ph, "Mt%d" % i, [128, 4, 128], BF16) for i in range(2)]
        S = P.sb(ph, "Sst", [128, 2, 768], F32)
        Sb = P.sb(ph, "Sbf", [128, 2, 768], BF16)
        xwb = P.sb(ph, "xwb", [128, SW], BF16)
        xdt = P.sb(ph, "xdt", [128, SW], BF16)
        tmpD = P.sb(ph, "tmpD", [128, SW], F32)
        ysb = P.sb(ph, "ysb", [128, SW], F32)
        yb = P.sb(ph, "yb", [128, SW], BF16)
        yT = P.sb(ph, "yT", [128, 12, 128], BF16)
        psBC = [P.ps(ph, "psBC%d" % i, [128, 4, 128], F32) for i in range(2)]
        psY = P.ps(ph, "psY", [128, SW], F32)
        psSt = P.ps(ph, "psSt", [128, 1024], F32)
        psHi = Buf(psSt.t, "psSt_hi", "ps")
        psTr = P.ps(ph, "psTrC", [128, 4, 128], BF16)

        P.dma("sp", trif, trif[:], CST, cst_d[:, 128:256])
        P.dma("sp", upf, upf[:], CST, cst_d[:, 768:896])
        P.dma("pool", identb, identb[:], CST, cst_d[:, 0:128])
        P.op("dve", lambda e: e.memset(onesf[:], 1.0), [], [onesf])
        P.dma("sp", cw, cw[:], VEC, bc_ap(vec_d, VO["conv_w"], [[0, 128], [CONV_DIM, 4], [1, CONV_DIM]]))
        P.dma("sp", cb, cb[:], VEC, bc_ap(vec_d, VO["conv_b"], [[0, 128], [1, CONV_DIM]]))
        P.dma("sp", abc, abc[:], VEC, bc_ap(vec_d, VO["a_log"], [[0, 128], [1, NH_S]]))
        P.dma("sp", dbc, dbc[:], VEC, bc_ap(vec_d, VO["d_skip"], [[0, 128], [1, NH_S]]))
        P.dma("sp", sgbc, sgbc[:], VEC, bc_ap(vec_d, VO["ssm_norm"], [[0, 128], [1, SW]]))
        P.act(abc, abc[:], abc, abc[:], AF.Exp)
        P.ts("dve", abc, abc[:], abc, abc[:], -1.0, None, ALU.mult)
        P.dma("pool", Woa, Woa[:], WOA, woa_d[:, :, :])
        for j in range(12):
            P.dma("pool", Wos, Wos[:, j, :], WOS, wos_d[:, j, :])

        def loads(t):
            b = t % 2
            r0 = t * 128
            for k in range(4):
                P.dma("sp", uk[k], uk[k][:], U, u_d[r0 + k:r0 + k + 128, :])
            P.dma("sp", dtt[b], dtt[b][:], DT, dt_d[r0:r0 + 128, :])
            P.dma("sp", att[b], att[b][:], AT, attnT_d[:, :, r0:r0 + 128])
            P.dma("sp", xt[b], xt[b][:], X, x_d[r0:r0 + 128, :])

        def conv_pool(t):
            for k in range(4):
                P.tt("dve", tk[k], tk[k][:], uk[k], uk[k][:], cw, cw[:, k, :], ALU.mult)

        def conv_dve(t):
            P.tt("dve", tk[0], tk[0][:], tk[0], tk[0][:], tk[1], tk[1][:], ALU.add)
            P.tt("dve", tk[2], tk[2][:], tk[2], tk[2][:], tk[3], tk[3][:], ALU.add)
            P.tt("dve", tk[0], tk[0][:], tk[0], tk[0][:], tk[2], tk[2][:], ALU.add)
            P.tt("dve", tk[0], tk[0][:], tk[0], tk[0][:], cb, cb[:], ALU.add)
            xc = xcb[t % 2]
            P.act(xc, xc[:], tk[0], tk[0][:], AF.Silu)

        loads(0)
        conv_pool(0)
        conv_dve(0)
        for t in range(NT):
            b = t % 2
            r0 = t * 128
            xc = xcb[b]
            if t + 1 < NT:
                loads(t + 1)
            P.dma("sp", szt, szt[:], SZ, sz_d[r0:r0 + 128, :])
            m = sm[b]
            d_ = dtt[b]
            P.tt("dve", m, m[:, 0, :], d_, d_[:], abc, abc[:], ALU.mult)
            if t + 1 < NT:
                conv_pool(t + 1)
            P.mm(psHi, psHi[:, 768:768 + NH_S], trif, trif[:], m, m[:, 0, :], True, True)
            P.mm(psHi, psHi[:, 800:800 + NH_S], onesf, onesf[:], m, m[:, 0, :], True, True)
            P.copy("dve", m, m[:, 1, :], psHi, psHi[:, 768:768 + NH_S])
            P.copy("dve", m, m[:, 2, :], psHi, psHi[:, 800:800 + NH_S])
            P.tt("dve", m, m[:, 3, :], m, m[:, 2, :], m, m[:, 1, :], ALU.subtract)
            P.act(m, m[:, 3, :], m, m[:, 3, :], AF.Exp)
            P.tt("dve", m, m[:, 3, :], m, m[:, 3, :], d_, d_[:], ALU.mult)
            P.act(m, m[:, 4, :], m, m[:, 2, :], AF.Exp)
            for j in range(4):
                P.tr(psTr, psTr[:, j, :], xc, xc[:, SW + j * 128:SW + (j + 1) * 128], identb, identb[:])
            P.copy("act", BCT, BCT[:], psTr, psTr[:])
            for g in range(2):
                P.mm(psSt, psSt[:, g * 128:128 + g * 128], BCT, BCT[:, g, :], BCT, BCT[:, 2 + g, :], True, True)
            P.copy("act", Gm, Gm[:], psSt, psSt[:, 0:256].rearrange("p (g l) -> p g l", g=2))
            for g in range(2):
                P.tt("dve", Gm, Gm[:, g, :], Gm, Gm[:, g, :], trif, trif[:], ALU.mult)

            def pre(q4):
                i2 = (t * 6 + q4) % 2
                bc, a4 = psBC[i2], A4[i2]
                P.tt("dve", a4, a4[:], upf, upf[:].unsqueeze(1).to_broadcast([128, 4, 128]),
                     m, m[:, 0, 4 * q4:4 * q4 + 4].unsqueeze(2).to_broadcast([128, 4, 128]), ALU.mult)
                for j in range(4):
                    P.mm(bc, bc[:, j, :], a4, a4[:, j, :], trif, trif[:], True, True)

            def body_a(q4):
                i2 = (t * 6 + q4) % 2
                bc, dt4 = psBC[i2], dtmp[i2]
                P.act(dt4, dt4[:], bc, bc[:], AF.Exp)

            def body_b(q4):
                g = q4 // 3
                i2 = (t * 6 + q4) % 2
                dt4, mt = dtmp[i2], Mt[i2]
                for j in range(4):
                    P.stt("dve", mt, mt[:, j, :], dt4, dt4[:, j, :], 1.0, Gm, Gm[:, g, :], ALU.min, ALU.mult)
                for j in range(4):
                    h = 4 * q4 + j
                    P.mm(psY, psY[:, h * 64:(h + 1) * 64], mt, mt[:, j, :], xdt, xdt[:, h * 64:(h + 1) * 64], True, True)

            xs3 = xc[:, 0:SW].rearrange("p (h e) -> p h e", h=NH_S)
            P.ts("dve", m, m[:, 6, :], m, m[:, 1, :], -1.0, None, ALU.mult)
            P.tt("dve", xdt, xdt[:].rearrange("p (h e) -> p h e", h=NH_S), xc, xs3,
                 d_, d_[:].unsqueeze(2).to_broadcast([128, NH_S, 64]), ALU.mult)
            pre(0)
            body_a(0)
            pre(1)
            body_a(1)
            for q4 in range(6):
                if q4 + 2 < 6:
                    pre(q4 + 2)
                body_b(q4)
                if q4 + 2 < 6:
                    body_a(q4 + 2)
            P.tt("dve", tmpD, tmpD[:].rearrange("p (h e) -> p h e", h=NH_S), xc, xs3,
                 dbc, dbc[:].unsqueeze(2).to_broadcast([128, NH_S, 64]), ALU.mult)
            if t > 0:
                P.act(m, m[:, 5, :], m, m[:, 1, :], AF.Exp)
                for g in range(2):
                    P.mm(psSt, psSt[:, 0:512], BCT, BCT[:, 2 + g, :], Sb, Sb[:, g, 0:512], True, True)
                    P.mm(psSt, psSt[:, 512:768], BCT, BCT[:, 2 + g, :], Sb, Sb[:, g, 512:768], True, True)
                    y3 = ysb[:, g * 768:(g + 1) * 768].rearrange("p (h e) -> p h e", h=12)
                    P.tt("dve", ysb, y3, psSt, psSt[:, 0:768].rearrange("p (h e) -> p h e", h=12),
                         m, m[:, 5, 12 * g:12 * g + 12].unsqueeze(2).to_broadcast([128, 12, 64]), ALU.mult)
                    P.tt("dve", tmpD, tmpD[:, g * 768:(g + 1) * 768], tmpD, tmpD[:, g * 768:(g + 1) * 768],
                         ysb, ysb[:, g * 768:(g + 1) * 768], ALU.add)
            P.tt("dve", ysb, ysb[:], psY, psY[:], tmpD, tmpD[:], ALU.add)
            P.tt("dve", ysb, ysb[:], ysb, ysb[:], szt, szt[:], ALU.mult)
            for g in range(2):
                P.act(tmpD, tmpD[:, g * 768:(g + 1) * 768], ysb, ysb[:, g * 768:(g + 1) * 768], AF.Square,
                      wr=[st2], accum_out=st2[:, g:g + 1])
            P.rstd(st2, st2[:, 0:2], st2[:, 2:4], st2[:, 4:6], 1.0 / 768)
            for g in range(2):
                P.stt("dve", yb, yb[:, g * 768:(g + 1) * 768], ysb, ysb[:, g * 768:(g + 1) * 768], st2[:, 4 + g:5 + g],
                      sgbc, sgbc[:, g * 768:(g + 1) * 768], ALU.mult, ALU.mult, rd=[st2])
            P.tt("dve", xwb, xwb[:].rearrange("p (h e) -> p h e", h=NH_S), xc, xs3,
                 m, m[:, 3, :].unsqueeze(2).to_broadcast([128, NH_S, 64]), ALU.mult)
            for g in range(2):
                bs = xc[:, SW + g * 128:SW + (g + 1) * 128]
                P.mm(psSt, psSt[:, 0:512], xc, bs, xwb, xwb[:, g * 768:g * 768 + 512], True, True)
                P.mm(psSt, psSt[:, 512:768], xc, bs, xwb, xwb[:, g * 768 + 512:(g + 1) * 768], True, True)
                if t == 0:
                    P.copy("dve", S, S[:, g, :], psSt, psSt[:, 0:768])
                else:
                    sg3 = S[:, g, :].rearrange("p (h e) -> p h e", h=12)
                    P.tt("dve", S, sg3, S, sg3, m, m[:, 4, 12 * g:12 * g + 12].unsqueeze(2).to_broadcast([128, 12, 64]), ALU.mult)
                    P.tt("dve", S, S[:, g, :], S, S[:, g, :], psSt, psSt[:, 0:768], ALU.add)
            P.copy("act", Sb, Sb[:], S, S[:])
            for i in range(3):
                for j in range(4):
                    c = 4 * i + j
                    P.tr(psTr, psTr[:, j, :], yb, yb[:, c * 128:(c + 1) * 128], identb, identb[:])
                P.copy("act", yT, yT[:, 4 * i:4 * i + 4, :], psTr, psTr[:])
            a_ = att[b]
            for hf in range(2):
                o = psY[:, hf * 512:(hf + 1) * 512]
                for h in range(8):
                    P.mm(psY, o, a_, a_[0:64, h, :], Woa, Woa[0:64, h, hf * 512:(hf + 1) * 512], h == 0, False)
                for j in range(12):
                    P.mm(psY, o, yT, yT[:, j, :], Wos, Wos[:, j, hf * 512:(hf + 1) * 512], False, j == 11)
            if t + 1 < NT:
                conv_dve(t + 1)
            P.tt("dve", xt[b], xt[b][:], xt[b], xt[b][:], psY, psY[:, 0:D_MODEL], ALU.add)
            P.dma("pool", OUT, out_d[r0:r0 + 128, :], xt[b], xt[b][:])
        P.barrier()
        P.emit()


def norm_T(P, xt_b, xt_ap, gbc, s, junk, hb, psT, hT, ident):
    P.act(junk, junk[:], xt_b, xt_ap, AF.Square, wr=[s], accum_out=s[:, 0:1])
    P.rstd(s, s[:, 0:1], s[:, 1:2], s[:, 2:3], 1.0 / D_MODEL)
    P.stt("dve", hb, hb[:], xt_b, xt_ap, s[:, 2:3], gbc, gbc[:], ALU.mult, ALU.mult, rd=[s])
    for kc in range(8):
        P.tr(psT, psT[:, kc, :], hb, hb[:, kc * 128:(kc + 1) * 128], ident, ident[:])
    P.copy("act", hT, hT[:], psT, psT[:])


def phase_d1(nc, P, SEQ, VEC, vec_d, CST, cst_d, WG, wg_d, WU, wu_d, WD, wd_d, OUT, out_d):
    NT = SEQ // 128
    with ExitStack() as ph:
        Wg = P.sb(ph, "Wg", [128, 8, D_FF], BF16)
        Wu = P.sb(ph, "Wu", [128, 8, D_FF], BF16)
        Wd = P.sb(ph, "Wd", [128, 22, D_MODEL], BF16)
        ident = P.sb(ph, "identD", [128, 128], BF16)
        gbc = P.sb(ph, "gffn", [128, D_MODEL], F32)
        xt = [P.sb(ph, "xtD%d" % i, [128, D_MODEL], F32) for i in range(3)]
        junk = P.sb(ph, "junkD", [128, D_MODEL], F32)
        ss = [P.sb(ph, "ssD%d" % i, [128, 8], F32) for i in range(2)]
        hb = P.sb(ph, "hbD", [128, D_MODEL], BF16)
        hT = [P.sb(ph, "hTD%d" % i, [128, 8, 128], BF16) for i in range(2)]
        sg = [P.sb(ph, "sgD%d" % i, [128, 512], F32) for i in range(2)]
        ab = P.sb(ph, "abD", [128, D_FF], BF16)
        aT = [P.sb(ph, "aTD%d" % i, [128, 22, 128], BF16) for i in range(2)]
        psT = P.ps(ph, "psTD", [128, 8, 128], BF16)
        psG = [P.ps(ph, "psGD%d" % i, [128, 512], F32) for i in range(2)]
        psU = [P.ps(ph, "psUD%d" % i, [128, 512], F32) for i in range(2)]
        psO = P.ps(ph, "psOD", [128, D_MODEL], F32)
        psA_ = P.ps(ph, "psAD", [128, 8, 128], BF16)
        psA = [psA_, Buf(psA_.t, "psAD_b", "ps")]
        for fc in range(22):
            P.dma("pool", Wg, Wg[:, :, fc * 128:(fc + 1) * 128], WG, wg_d[fc])
            P.dma("pool", Wu, Wu[:, :, fc * 128:(fc + 1) * 128], WU, wu_d[fc])
        for j in range(0, 22, 2):
            P.dma("pool", Wd, Wd[:, j:j + 2, :], WD, wd_d[:, j:j + 2, :])
        P.dma("pool", ident, ident[:], CST, cst_d[:, 0:128])
        P.dma("sp", gbc, gbc[:], VEC, bc_ap(vec_d, VO["ffn_norm"], [[0, 128], [1, D_MODEL]]))
        def nfront(t):
            b = t % 3
            s_ = ss[t % 2]
            P.act(junk, junk[:], xt[b], xt[b][:], AF.Square, wr=[s_], accum_out=s_[:, 0:1])
            P.rstd(s_, s_[:, 0:1], s_[:, 1:2], s_[:, 2:3], 1.0 / D_MODEL)
            P.stt("dve", hb, hb[:], xt[b], xt[b][:], s_[:, 2:3], gbc, gbc[:], ALU.mult, ALU.mult, rd=[s_])

        def nback(t):
            b = t % 2
            for kc in range(8):
                P.tr(psT, psT[:, kc, :], hb, hb[:, kc * 128:(kc + 1) * 128], ident, ident[:])
            P.copy("act", hT[b], hT[b][:], psT, psT[:])

        def ab_T(t, fc):
            n = 4 if fc < 5 else 2
            hh = (t * 6 + fc) % 2
            pa = psA[hh]
            for j in range(n):
                c = 4 * fc + j
                P.tr(pa, pa[:, 4 * hh + j, :], ab, ab[:, c * 128:(c + 1) * 128], ident, ident[:])
            P.copy("act", aT[t % 2], aT[t % 2][:, 4 * fc:4 * fc + n, :], pa, pa[:, 4 * hh:4 * hh + n, :])

        def down(t):
            a_ = aT[t % 2]
            x_ = xt[t % 3]
            for hf in range(2):
                for j in range(22):
                    P.mm(psO, psO[:, hf * 512:(hf + 1) * 512], a_, a_[:, j, :], Wd, Wd[:, j, hf * 512:(hf + 1) * 512], j == 0, j == 21)
            P.tt("dve", x_, x_[:], x_, x_[:], psO, psO[:], ALU.add)
            P.dma("pool", OUT, out_d[t * 128:(t + 1) * 128, :], x_, x_[:])

        P.dma("sp", xt[0], xt[0][:], OUT, out_d[0:128, :])
        if NT > 1:
            P.dma("sp", xt[1], xt[1][:], OUT, out_d[128:256, :])
        nfront(0)
        nback(0)
        cnt = 0
        for t in range(NT):
            b = t % 2
            for fc in range(6):
                f0 = fc * 512
                w = min(512, D_FF - f0)
                G_, U_, sg_ = psG[cnt % 2], psU[cnt % 2], sg[cnt % 2]
                cnt += 1
                for kc in range(8):
                    P.mm(G_, G_[:, 0:w], hT[b], hT[b][:, kc, :], Wg, Wg[:, kc, f0:f0 + w], kc == 0, kc == 7)
                for kc in range(8):
                    P.mm(U_, U_[:, 0:w], hT[b], hT[b][:, kc, :], Wu, Wu[:, kc, f0:f0 + w], kc == 0, kc == 7)
                P.act(sg_, sg_[:, 0:w], G_, G_[:, 0:w], AF.Silu)
                P.tt("dve", ab, ab[:, f0:f0 + w], sg_, sg_[:, 0:w], U_, U_[:, 0:w], ALU.mult)
                if fc == 0 and t > 0:
                    down(t - 1)
                if fc == 1 and t + 1 < NT:
                    if t + 2 < NT:
                        P.dma("sp", xt[(t + 2) % 3], xt[(t + 2) % 3][:], OUT, out_d[(t + 2) * 128:(t + 3) * 128, :])
                    nfront(t + 1)
                if fc >= 1:
                    ab_T(t, fc - 1)
                if fc == 4 and t + 1 < NT:
                    nback(t + 1)
            ab_T(t, 5)
        down(NT - 1)
        P.barrier()
        P.emit()


def phase_d2(nc, P, SEQ, VEC, vec_d, CST, cst_d, PIN, p_d, WPG, wpg_d, WPLE, wple_d, OUT, out_d):
    NT = SEQ // 128
    with ExitStack() as ph:
        Wpg = P.sb(ph, "Wpg", [128, 8, D_MODEL], BF16)
        Wpl = P.sb(ph, "Wpl", [128, 2, D_MODEL], BF16)
        ident = P.sb(ph, "identE", [128, 128], BF16)
        gbc = P.sb(ph, "gpg", [128, D_MODEL], F32)
        bbc = P.sb(ph, "bpg", [128, D_MODEL], F32)
        pgbc = P.sb(ph, "gple", [128, D_MODEL], F32)
        xt = [P.sb(ph, "xtE%d" % i, [128, D_MODEL], F32) for i in range(3)]
        pt = [P.sb(ph, "ptE%d" % i, [128, PLE], F32) for i in range(2)]
        pb = P.sb(ph, "pbE", [128, PLE], BF16)
        pT = [P.sb(ph, "pTE%d" % i, [128, 2, 128], BF16) for i in range(2)]
        junk = P.sb(ph, "junkE", [128, D_MODEL], F32)
        ss = [P.sb(ph, "ssE%d" % i, [128, 8], F32) for i in range(2)]
        hb = P.sb(ph, "hbE", [128, D_MODEL], BF16)
        hT = [P.sb(ph, "hTE%d" % i, [128, 8, 128], BF16) for i in range(2)]
        gt = P.sb(ph, "gtE", [128, D_MODEL], F32)
        en = P.sb(ph, "enE", [128, D_MODEL], F32)
        psT = P.ps(ph, "psTE", [128, 8, 128], BF16)
        psP = P.ps(ph, "psPE", [128, 4, 128], BF16)
        psGt = P.ps(ph, "psGt", [128, D_MODEL], F32)
        psE = P.ps(ph, "psE", [128, D_MODEL], F32)
        for kc in range(0, 8, 2):
            P.dma("pool", Wpg, Wpg[:, kc:kc + 2, :], WPG, wpg_d[:, kc:kc + 2, :])
        P.dma("pool", Wpl, Wpl[:], WPLE, wple_d[:, :, :])
        P.dma("pool", ident, ident[:], CST, cst_d[:, 0:128])
        P.dma("sp", gbc, gbc[:], VEC, bc_ap(vec_d, VO["ple_gate_norm"], [[0, 128], [1, D_MODEL]]))
        P.dma("sp", bbc, bbc[:], VEC, bc_ap(vec_d, VO["b_ple_gate"], [[0, 128], [1, D_MODEL]]))
        P.dma("sp", pgbc, pgbc[:], VEC, bc_ap(vec_d, VO["ple_norm"], [[0, 128], [1, D_MODEL]]))
        def load(t):
            r0 = t * 128
            P.dma("sp", xt[t % 3], xt[t % 3][:], OUT, out_d[r0:r0 + 128, :])
            P.dma("sp", pt[t % 2], pt[t % 2][:], PIN, p_d[r0:r0 + 128, :])

        def s1(t):
            x_, s_ = xt[t % 3], ss[t % 2]
            P.act(junk, junk[:], x_, x_[:], AF.Square, wr=[s_], accum_out=s_[:, 0:1])
            P.rstd(s_, s_[:, 0:1], s_[:, 1:2], s_[:, 2:3], 1.0 / D_MODEL)
            P.stt("dve", hb, hb[:], x_, x_[:], s_[:, 2:3], gbc, gbc[:], ALU.mult, ALU.mult, rd=[s_])
            P.copy("dve", pb, pb[:], pt[t % 2], pt[t % 2][:])

        def s2(t):
            b = t % 2
            for kc in range(8):
                P.tr(psT, psT[:, kc, :], hb, hb[:, kc * 128:(kc + 1) * 128], ident, ident[:])
            P.copy("act", hT[b], hT[b][:], psT, psT[:])
            for j in range(2):
                P.tr(psP, psP[:, j, :], pb, pb[:, j * 128:(j + 1) * 128], ident, ident[:])
            P.copy("act", pT[b], pT[b][:], psP, psP[:, 0:2, :])
            for hf in range(2):
                for kc in range(8):
                    P.mm(psGt, psGt[:, hf * 512:(hf + 1) * 512], hT[b], hT[b][:, kc, :], Wpg, Wpg[:, kc, hf * 512:(hf + 1) * 512], kc == 0, kc == 7)
            for hf in range(2):
                for j in range(2):
                    P.mm(psE, psE[:, hf * 512:(hf + 1) * 512], pT[b], pT[b][:, j, :], Wpl, Wpl[:, j, hf * 512:(hf + 1) * 512], j == 0, j == 1)

        def s3(t):
            x_, s_ = xt[t % 3], ss[t % 2]
            P.tt("dve", gt, gt[:], psGt, psGt[:], bbc, bbc[:], ALU.add)
            P.act(gt, gt[:], gt, gt[:], AF.Sigmoid)
            P.act(junk, junk[:], psE, psE[:], AF.Square, wr=[s_], accum_out=s_[:, 4:5])
            P.rstd(s_, s_[:, 4:5], s_[:, 5:6], s_[:, 6:7], 1.0 / D_MODEL)
            P.stt("dve", en, en[:], psE, psE[:], s_[:, 6:7], pgbc, pgbc[:], ALU.mult, ALU.mult, rd=[s_])
            P.tt("pool", en, en[:], en, en[:], gt, gt[:], ALU.mult)
            P.tt("dve", x_, x_[:], x_, x_[:], en, en[:], ALU.add)
            P.dma("pool", OUT, out_d[t * 128:(t + 1) * 128, :], x_, x_[:])

        load(0)
        if NT > 1:
            load(1)
        s1(0)
        s2(0)
        for t in range(NT):
            if t + 2 < NT:
                load(t + 2)
            if t + 1 < NT:
                s1(t + 1)
            s3(t)
            if t + 1 < NT:
                s2(t + 1)
        P.barrier()
        P.emit()


def make_consts():
    c = np.zeros((128, 1024), np.float32)
    i = np.arange(128)
    c[:, 0:128] = np.eye(128, dtype=np.float32)
    c[:, 128:256] = (i[:, None] <= i[None, :])
    c[:, 256:384] = (i[:, None] >= i[None, :])
    c[:, 384:512] = (i[:, None] <= i[None, :])
    c[:, 512:640] = np.where(i[:, None] >= i[None, :], 0.0, -30000.0)
    c[:, 640:768] = np.where(i[:, None] <= i[None, :], 0.0, -30000.0)
    c[:, 768:896] = (i[:, None] > i[None, :])
    return c


def pack_inputs(inp, SEQ=SEQ_FULL):
    f = lambda a: np.ascontiguousarray(np.asarray(a, dtype=np.float32))
    vec = np.zeros((1, NVEC), np.float32)
    for k, o in VO.items():
        a = f(inp[k][0]).reshape(-1)
        vec[0, o:o + a.size] = a
    shared = dict(
        w_in=f(f(inp["w_in"][0]).reshape(8, 128, IN_PROJ).transpose(1, 0, 2)),
        w_out_a=f(f(inp["w_out"][0])[0:AW].reshape(8, 64, D_MODEL).transpose(1, 0, 2)),
        w_out_s=f(f(inp["w_out"][0])[AW:].reshape(12, 128, D_MODEL).transpose(1, 0, 2)),
        w_g=f(f(inp["w_ffn_gate"][0]).reshape(8, 128, 22, 128).transpose(2, 1, 0, 3)),
        w_u=f(f(inp["w_ffn_up"][0]).reshape(8, 128, 22, 128).transpose(2, 1, 0, 3)),
        w_d=f(f(inp["w_ffn_down"][0]).reshape(22, 128, D_MODEL).transpose(1, 0, 2)),
        w_pg=f(f(inp["w_ple_gate"][0]).reshape(8, 128, D_MODEL).transpose(1, 0, 2)),
        w_ple=f(f(inp["w_ple"][0]).reshape(2, 128, D_MODEL).transpose(1, 0, 2)),
        vecs=vec,
        consts=make_consts(),
    )
    maps = []
    for b in range(inp["x"].shape[0]):
        m = dict(shared)
        m["x"] = f(inp["x"][b][:SEQ])
        m["p"] = f(inp["p"][0][b][:SEQ])
        maps.append(m)
    return maps


_NC_CACHE = {}


def kernel(**inputs):
    maps = pack_inputs(inputs)
    if "nc" not in _NC_CACHE:
        _NC_CACHE["nc"] = build()
    nc = _NC_CACHE["nc"]
    res = run_bass_kernel_spmd(nc, maps, core_ids=list(range(len(maps))))
    return np.stack([np.asarray(r["out"], dtype=np.float32) for r in res.results], axis=0)
```

```python
from contextlib import ExitStack
import numpy as np
import concourse.bass as bass
import concourse.mybir as mybir
from concourse.bass_utils import run_bass_kernel_spmd

F32 = mybir.dt.float32
BF16 = mybir.dt.bfloat16
AF = mybir.ActivationFunctionType
ALU = mybir.AluOpType
AX = mybir.AxisListType

ENG = ["pe", "act", "dve", "pool", "sp"]

D_MODEL = 1024
SEQ_FULL = 8192
NH_A = 8
HD = 64
AW = 512
NH_S = 24
SW = 1536
CONV_DIM = 2048
NSTATE = 128
IN_PROJ = 5144
D_FF = 2816
PLE = 256
EPS = 1e-6


class Buf:
    def __init__(self, t, name, space):
        self.t = t
        self.name = name
        self.space = space
        self.w = {}
        self.r = {}

    def __getitem__(self, k):
        return self.t[k]


class Prog:
    def __init__(self, nc, stack):
        self.nc = nc
        self.stack = stack
        self.q = {e: [] for e in ENG}
        self.emitted = {e: 0 for e in ENG}
        self.known = {e: {} for e in ENG}
        self.semval = {e: 0 for e in ENG}
        self.resolved = {e: {} for e in ENG}
        self.esem = {e: stack.enter_context(nc.semaphore("es_" + e)) for e in ENG}
        self.dsem = {}
        self.dcount = {}

    def sb(self, stack, name, shape, dtype):
        t = stack.enter_context(self.nc.sbuf_tensor(name, list(shape), dtype))
        return Buf(t, name, "sb")

    def ps(self, stack, name, shape, dtype=F32):
        t = stack.enter_context(self.nc.psum_tensor(name, list(shape), dtype))
        return Buf(t, name, "ps")

    def dr(self, t, name):
        return Buf(t, name, "dr")

    def _collect(self, e, reads, writes, skipkey=None):
        waits = {}

        def need(k, v):
            if k == skipkey:
                return
            if k == "pe" and e == "pe":
                return
            if self.known[e].get(k, -1) >= v:
                return
            if waits.get(k, -1) < v:
                waits[k] = v

        for b in reads:
            for k, v in b.w.items():
                need(k, v)
        for b in writes:
            for k, v in b.w.items():
                need(k, v)
            for k, v in b.r.items():
                need(k, v)
        for k, v in waits.items():
            self.known[e][k] = v
            if k in self.q:
                self.q[k][v]["inc"] = True
        return list(waits.items())

    def op(self, e, fn, reads=(), writes=()):
        idx = len(self.q[e])
        waits = self._collect(e, reads, writes)
        self.q[e].append(dict(fn=fn, waits=waits, inc=False, dkey=None))
        for b in reads:
            b.r[e] = idx
        for b in writes:
            b.w = {e: idx}
            b.r = {}
        return idx

    def dma(self, e, dst, dst_ap, src, src_ap, **kw):
        if dst.space != "dr":
            key = ("in", dst.name)
        elif src.space != "dr":
            key = ("out", src.name)
        else:
            key = ("dd", dst.name)
        waits = self._collect(e, [src], [dst], skipkey=key)
        cnt = self.dcount.get(key, 0) + 1
        self.dcount[key] = cnt
        val = 16 * cnt

        def fn(eng, dst_ap=dst_ap, src_ap=src_ap, kw=kw):
            return eng.dma_start(out=dst_ap, in_=src_ap, **kw)

        self.q[e].append(dict(fn=fn, waits=waits, inc=False, dkey=key))
        src.r[key] = val
        keep = {k: v for k, v in dst.w.items() if k == key}
        dst.w = keep
        dst.w[key] = val
        dst.r = {}

    def barrier(self):
        for e in ENG:
            waits = {}
            for k in ENG:
                if k == e or not self.q[k] or k == "sp":
                    continue
                v = -1
                for i in range(len(self.q[k]) - 1, -1, -1):
                    if self.q[k][i]["fn"] is not None and self.q[k][i]["dkey"] is None:
                        v = i
                        break
                if v < 0:
                    continue
                if self.known[e].get(k, -1) < v:
                    waits[k] = v
            for k, c in self.dcount.items():
                v = 16 * c
                if self.known[e].get(k, -1) < v:
                    waits[k] = v
            for k, v in waits.items():
                self.known[e][k] = v
                if k in self.q:
                    self.q[k][v]["inc"] = True
            self.q[e].append(dict(fn=None, waits=list(waits.items()), inc=False, dkey=None))

    def _sem(self, k):
        if k in self.esem:
            return self.esem[k]
        if k not in self.dsem:
            self.dsem[k] = self.stack.enter_context(
                self.nc.semaphore("ds%d" % len(self.dsem)))
        return self.dsem[k]

    def emit(self):
        nc = self.nc
        for e in ENG:
            for i in range(self.emitted[e], len(self.q[e])):
                r = self.q[e][i]
                if r["inc"] and r["fn"] is not None and r["dkey"] is None:
                    self.semval[e] += 1
                    self.resolved[e][i] = self.semval[e]
        for k in list(self.dcount):
            self._sem(k)

        def run(e, eng):
            recs = self.q[e]
            for i in range(self.emitted[e], len(recs)):
                rec = recs[i]
                for k, v in rec["waits"]:
                    if k in self.esem:
                        v = self.resolved[k][v]
                    eng.wait_ge(self._sem(k), v)
                if rec["fn"] is None:
                    continue
                ins = rec["fn"](eng)
                if rec["dkey"] is not None:
                    ins.then_inc(self._sem(rec["dkey"]), 16)
                elif rec["inc"]:
                    ins.then_inc(self.esem[e], 1)
            self.emitted[e] = len(recs)

        with nc.Block() as block:
            @block.tensor
            def _(eng):
                run("pe", eng)

            @block.scalar
            def _(eng):
                run("act", eng)

            @block.vector
            def _(eng):
                run("dve", eng)

            @block.gpsimd
            def _(eng):
                run("pool", eng)

            @block.sync
            def _(eng):
                run("sp", eng)

    def act(self, ob, o, ib, i, func, rd=(), wr=(), **kw):
        self.op("act", lambda e: e.activation(out=o, in_=i, func=func, **kw),
                [ib, *rd], [ob, *wr])

    def tt(self, eng, ob, o, ab, a, bb, b, op):
        self.op(eng, lambda e: e.tensor_tensor(out=o, in0=a, in1=b, op=op), [ab, bb], [ob])

    def ts(self, eng, ob, o, ab, a, s1, s2, op0, op1=None, rd=()):
        if op1 is None:
            self.op(eng, lambda e: e.tensor_scalar(out=o, in0=a, scalar1=s1, scalar2=None, op0=op0),
                    [ab, *rd], [ob])
        else:
            self.op(eng, lambda e: e.tensor_scalar(out=o, in0=a, scalar1=s1, scalar2=s2, op0=op0, op1=op1),
                    [ab, *rd], [ob])

    def stt(self, eng, ob, o, ab, a, s, bb, b, op0, op1, rd=(), wr=(), **kw):
        self.op(eng, lambda e: e.scalar_tensor_tensor(out=o, in0=a, scalar=s, in1=b, op0=op0, op1=op1, **kw),
                [ab, bb, *rd], [ob, *wr])

    def copy(self, eng, ob, o, ib, i):
        if eng == "act":
            self.op("act", lambda e: e.activation(out=o, in_=i, func=AF.Copy), [ib], [ob])
        else:
            self.op(eng, lambda e: e.tensor_copy(out=o, in_=i), [ib], [ob])

    def mm(self, ob, o, lb, l, rb, r, start, stop):
        self.op("pe", lambda e: e.matmul(o, lhsT=l, rhs=r, start=start, stop=stop), [lb, rb], [ob])

    def tr(self, ob, o, ib, i, idb, idap):
        self.op("pe", lambda e: e.transpose(o, i, idap), [ib, idb], [ob])

    def rstd(self, ssb, src, tmp, dst, inv_n):
        self.ts("dve", ssb, tmp, ssb, src, inv_n, EPS, ALU.mult, ALU.add)
        self.act(ssb, tmp, ssb, tmp, AF.Sqrt)
        self.op("dve", lambda e: e.reciprocal(out=dst, in_=tmp), [ssb], [ssb])


def bc_ap(handle, offset, dims):
    return bass.AP(tensor=handle, offset=offset, ap=[list(d) for d in dims])


def build(SEQ=SEQ_FULL, debug=False, phases="ABCD"):
    nc = bass.Bass("TRN2", target_bir_lowering=False)
    NT = SEQ // 128
    skind = "ExternalOutput" if debug else "Internal"

    def din(name, shape):
        return nc.dram_tensor(name, list(shape), F32, kind="ExternalInput")

    x_d = din("x", [SEQ, D_MODEL])
    p_d = din("p", [SEQ, PLE])
    win_d = din("w_in", [128, 8, IN_PROJ])
    wout_a_d = din("w_out_a", [64, 8, D_MODEL])
    wout_s_d = din("w_out_s", [128, 12, D_MODEL])
    wg_d = din("w_g", [22, 128, 8, 128])
    wu_d = din("w_u", [22, 128, 8, 128])
    wd_d = din("w_d", [128, 22, D_MODEL])
    wpg_d = din("w_pg", [128, 8, D_MODEL])
    wple_d = din("w_ple", [128, 2, D_MODEL])
    vec_d = din("vecs", [1, 17408])
    cst_d = din("consts", [128, 1024])
    out_d = nc.dram_tensor("out", [SEQ, D_MODEL], F32, kind="ExternalOutput")

    qT_d = nc.dram_tensor("qT_s", [128, 4, SEQ], BF16, kind=skind)
    kT_d = nc.dram_tensor("kT_s", [128, 4, SEQ], BF16, kind=skind)
    v_d = nc.dram_tensor("v_s", [SEQ, 8, 128], BF16, kind=skind)
    u_d = nc.dram_tensor("u_s", [SEQ + 3, CONV_DIM], BF16, kind=skind)
    sz_d = nc.dram_tensor("sz_s", [SEQ, SW], BF16, kind=skind)
    dt_d = nc.dram_tensor("dt_s", [SEQ, NH_S], F32, kind=skind)
    attnT_d = nc.dram_tensor("attnT_s", [64, 8, SEQ], BF16, kind=skind)
    xc_d = nc.dram_tensor("xc_s", [SEQ, CONV_DIM], BF16, kind=skind)

    with ExitStack() as st:
        P = Prog(nc, st)
        X = P.dr(x_d, "x")
        WIN = P.dr(win_d, "w_in")
        VEC = P.dr(vec_d, "vecs")
        CST = P.dr(cst_d, "consts")
        QT = P.dr(qT_d, "qT_s")
        KT = P.dr(kT_d, "kT_s")
        V = P.dr(v_d, "v_s")
        U = P.dr(u_d, "u_s")
        SZ = P.dr(sz_d, "sz_s")
        DT = P.dr(dt_d, "dt_s")

        AT = P.dr(attnT_d, "attnT_s")
        XC = P.dr(xc_d, "xc_s")
        if "A" in phases:
            phase_a(nc, P, SEQ, X, x_d, WIN, win_d, VEC, vec_d, CST, cst_d,
                    QT, qT_d, KT, kT_d, V, v_d, U, u_d, SZ, sz_d, DT, dt_d, XC, xc_d)
        if "B" in phases:
            phase_b(nc, P, SEQ, CST, cst_d, QT, qT_d, KT, kT_d, V, v_d, AT, attnT_d)
        OUT = P.dr(out_d, "out")
        if "C" in phases:
            phase_c(nc, P, SEQ, X, x_d, VEC, vec_d, CST, cst_d, XC, xc_d, SZ, sz_d, DT, dt_d,
                    AT, attnT_d, P.dr(wout_a_d, "w_out_a"), wout_a_d, P.dr(wout_s_d, "w_out_s"), wout_s_d,
                    OUT, out_d)
        if "D" in phases:
            phase_d1(nc, P, SEQ, VEC, vec_d, CST, cst_d, P.dr(wg_d, "w_g"), wg_d, P.dr(wu_d, "w_u"), wu_d,
                     P.dr(wd_d, "w_d"), wd_d, OUT, out_d)
            phase_d2(nc, P, SEQ, VEC, vec_d, CST, cst_d, P.dr(p_d, "p"), p_d, P.dr(wpg_d, "w_pg"), wpg_d,
                     P.dr(wple_d, "w_ple"), wple_d, OUT, out_d)
    return nc


VO = dict(mix_norm=0, q_norm=1024, k_norm=1088, conv_w=1152, conv_b=1152 + 8192,
          dt_bias=11392, a_log=11416, d_skip=11440, ssm_norm=11464, ffn_norm=13000,
          ple_gate_norm=14024, b_ple_gate=15048)
VO["ple_norm"] = 16072
NVEC = 17408


def phase_a(nc, P, SEQ, X, x_d, WIN, win_d, VEC, vec_d, CST, cst_d,
            QT, qT_d, KT, kT_d, V, v_d, U, u_d, SZ, sz_d, DT, dt_d, XC, xc_d):
    NT = SEQ // 128
    with ExitStack() as ph:
        Win = P.sb(ph, "Win", [128, 8, IN_PROJ], BF16)
        ident = P.sb(ph, "identA", [128, 128], BF16)
        gmix = P.sb(ph, "gmix", [128, D_MODEL], F32)
        qg = P.sb(ph, "qg", [128, 8, 64], F32)
        kg = P.sb(ph, "kg", [128, 8, 64], F32)
        dtb = P.sb(ph, "dtb", [128, NH_S], F32)
        xt = [P.sb(ph, "xt%d" % i, [128, D_MODEL], F32) for i in range(2)]
        junk = P.sb(ph, "junkA", [128, D_MODEL], F32)
        ss = [P.sb(ph, "ssA%d" % i, [128, 8], F32) for i in range(2)]
        hb = P.sb(ph, "hb", [128, D_MODEL], BF16)
        hT = [P.sb(ph, "hT%d" % i, [128, 8, 128], BF16) for i in range(2)]
        tmpf = [P.sb(ph, "tmpf%d" % i, [128, 8, 64], F32) for i in range(2)]
        tmpg = [P.sb(ph, "tmpg%d" % i, [128, 8, 64], F32) for i in range(2)]
        ssh = [P.sb(ph, "ssh%d" % i, [128, 24], F32) for i in range(2)]
        qb = [P.sb(ph, "qb%d" % i, [128, 512], BF16) for i in range(2)]
        qst = [P.sb(ph, "qst%d" % i, [128, 4, 512], BF16) for i in range(2)]
        kst = [P.sb(ph, "kst%d" % i, [128, 4, 512], BF16) for i in range(2)]
        vb = [P.sb(ph, "vb%d" % i, [128, 8, 128], BF16) for i in range(2)]
        szb = [P.sb(ph, "szb%d" % i, [128, SW], BF16) for i in range(2)]
        ub = [P.sb(ph, "ub%d" % i, [128, CONV_DIM], BF16) for i in range(2)]
        dts = [P.sb(ph, "dts%d" % i, [128, 4, NH_S], F32) for i in range(2)]
        psT = P.ps(ph, "psT", [128, 8, 128], BF16)
        psQ = P.ps(ph, "psQ", [128, 4, 128], BF16)
        pj = [P.ps(ph, "pj%d" % i, [128, 512], F32) for i in range(4)]
        cwb = P.sb(ph, "cwb", [128, 4, CONV_DIM], BF16)
        cbb = P.sb(ph, "cbb", [128, CONV_DIM], BF16)
        uk = [P.sb(ph, "uk%d" % k, [128, CONV_DIM], BF16) for k in range(4)]
        pk = [P.sb(ph, "pk%d" % k, [128, CONV_DIM], BF16) for k in range(4)]
        xcs = P.sb(ph, "xcs", [128, CONV_DIM], BF16)
        psC = P.ps(ph, "psC", [128, 1024], F32)
        P.dma("pool", cwb, cwb[:], VEC, bc_ap(vec_d, VO["conv_w"], [[0, 128], [CONV_DIM, 4], [1, CONV_DIM]]))
        P.dma("pool", cbb, cbb[:], VEC, bc_ap(vec_d, VO["conv_b"], [[0, 128], [1, CONV_DIM]]))

        def conv_load(tc):
            for k in range(4):
                P.dma("sp", uk[k], uk[k][:], U, u_d[tc * 128 + k:tc * 128 + k + 128, :])

        def conv_prod(tc):
            for k in range(4):
                P.tt("dve", pk[k], pk[k][:], uk[k], uk[k][:], cwb, cwb[:, k, :], ALU.mult)

        def conv_half(tc, hf):
            for c in range(2):
                cs = slice(hf * 1024 + c * 512, hf * 1024 + (c + 1) * 512)
                o = psC[:, c * 512:(c + 1) * 512]
                for k in range(4):
                    P.mm(psC, o, ident, ident[:], pk[k], pk[k][:, cs], k == 0, False)
                P.mm(psC, o, ident, ident[:], cbb, cbb[:, cs], False, True)
            P.act(xcs, xcs[:, hf * 1024:(hf + 1) * 1024], psC, psC[:], AF.Silu)
            if hf == 1:
                P.dma("pool", XC, xc_d[tc * 128:(tc + 1) * 128, :], xcs, xcs[:])

        for kc in range(8):
            P.dma("pool", Win, Win[:, kc, :], WIN, win_d[:, kc, :])
        P.dma("pool", ident, ident[:], CST, cst_d[:, 0:128])
        P.dma("sp", gmix, gmix[:], VEC, bc_ap(vec_d, VO["mix_norm"], [[0, 128], [1, D_MODEL]]))
        P.dma("sp", qg, qg[:], VEC, bc_ap(vec_d, VO["q_norm"], [[0, 128], [0, 8], [1, 64]]))
        P.dma("sp", kg, kg[:], VEC, bc_ap(vec_d, VO["k_norm"], [[0, 128], [0, 8], [1, 64]]))
        P.dma("sp", dtb, dtb[:], VEC, bc_ap(vec_d, VO["dt_bias"], [[0, 128], [1, NH_S]]))
        P.ts("dve", qg, qg[:], qg, qg[:], HD ** -0.5, None, ALU.mult)
        zer = pk[0]
        P.op("dve", lambda e: e.memset(zer[0:4, :], 0.0), [], [zer])
        for vb_ in vb:
            P.op("dve", lambda e, vb_=vb_: e.memset(vb_[:], 1.0), [], [vb_])
        P.dma("pool", U, u_d[0:3, :], zer, zer[0:3, :])

        def norm_front(t):
            b = t % 2
            s_ = ss[b]
            P.act(junk, junk[:], xt[b], xt[b][:], AF.Square, wr=[s_], accum_out=s_[:, 0:1])
            P.rstd(s_, s_[:, 0:1], s_[:, 1:2], s_[:, 2:3], 1.0 / D_MODEL)
            P.stt("dve", hb, hb[:], xt[b], xt[b][:], s_[:, 2:3], gmix, gmix[:], ALU.mult, ALU.mult, rd=[s_])
            if t + 2 < NT:
                P.dma("sp", xt[b], xt[b][:], X, x_d[(t + 2) * 128:(t + 3) * 128, :])

        def norm_back(t):
            b = t % 2
            for kc in range(8):
                P.tr(psT, psT[:, kc, :], hb, hb[:, kc * 128:(kc + 1) * 128], ident, ident[:])
            P.copy("act", hT[b], hT[b][:], psT, psT[:])

        def qk_back(t, j):
            qbb = qb[j]
            stg = (qst if j == 0 else kst)[(t // 4) % 2]
            for pr in range(4):
                P.tr(psQ, psQ[:, pr, :], qbb, qbb[:, pr * 128:(pr + 1) * 128], ident, ident[:])
            tc0 = (t % 4) * 128
            P.copy("act", stg, stg[:, :, tc0:tc0 + 128], psQ, psQ[:])
            if t % 4 == 3 or t == NT - 1:
                t0 = (t // 4) * 512
                n = (t % 4 + 1) * 128
                D_, d_ = (QT, qT_d) if j == 0 else (KT, kT_d)
                P.dma("pool", D_, d_[:, :, t0:t0 + n], stg, stg[:, :, 0:n])

        P.dma("sp", xt[0], xt[0][:], X, x_d[0:128, :])
        if NT > 1:
            P.dma("sp", xt[1], xt[1][:], X, x_d[128:256, :])
        norm_front(0)
        norm_back(0)
        cnt = 0
        for t in range(NT):
            b = t % 2
            for j in range(11):
                c0 = j * 512
                w = min(512, IN_PROJ - c0)
                pb = pj[cnt % 4]
                cnt += 1
                for kc in range(8):
                    P.mm(pb, pb[:, 0:w], hT[b], hT[b][:, kc, :], Win, Win[:, kc, c0:c0 + w], kc == 0, kc == 7)
                if j < 2:
                    g = qg if j == 0 else kg
                    tf, tg, sh, qbb = tmpf[j], tmpg[j], ssh[j], qb[j]
                    pv = pb[:, :].rearrange("p (h e) -> p h e", h=8)
                    P.act(tf, tf[:], pb, pv, AF.Square)
                    P.op("dve", lambda e, sh=sh, tf=tf: e.tensor_reduce(out=sh[:, 0:8], in_=tf[:], axis=AX.X, op=ALU.add),
                         [tf], [sh])
                    P.rstd(sh, sh[:, 0:8], sh[:, 8:16], sh[:, 16:24], 1.0 / HD)
                    P.tt("dve", tg, tg[:], pb, pv, sh, sh[:, 16:24].unsqueeze(2).to_broadcast([128, 8, 64]), ALU.mult)
                    P.tt("pool", qbb, qbb[:].rearrange("p (h e) -> p h e", h=8), tg, tg[:], g, g[:], ALU.mult)
                elif j == 2:
                    P.copy("act", vb[b], vb[b][:, :, 0:64], pb, pb[:, :].rearrange("p (h e) -> p h e", h=8))
                    P.dma("pool", V, v_d[t * 128:(t + 1) * 128, :, :], vb[b], vb[b][:])
                    if t + 1 < NT:
                        norm_front(t + 1)
                elif j < 6:
                    jj = j - 3
                    P.act(szb[b], szb[b][:, jj * 512:(jj + 1) * 512], pb, pb[:], AF.Silu)
                    if jj == 2:
                        P.dma("pool", SZ, sz_d[t * 128:(t + 1) * 128, :], szb[b], szb[b][:])
                    if jj == 0:
                        qk_back(t, 0)
                        if t > 0:
                            conv_prod(t - 1)
                    if jj == 2:
                        qk_back(t, 1)
                elif j < 10:
                    jj = j - 6
                    P.copy("dve", ub[b], ub[b][:, jj * 512:(jj + 1) * 512], pb, pb[:])
                    if jj == 3:
                        P.dma("pool", U, u_d[3 + t * 128:3 + (t + 1) * 128, :], ub[b], ub[b][:])
                    if jj == 1 and t + 1 < NT:
                        norm_back(t + 1)
                    if jj == 0 and t > 0:
                        conv_half(t - 1, 0)
                    if jj == 2 and t > 0:
                        conv_half(t - 1, 1)
                else:
                    d = dts[b]
                    P.tt("dve", d, d[:, 0, :], pb, pb[:, 0:NH_S], dtb, dtb[:], ALU.add)
                    P.stt("dve", d, d[:, 1, :], d, d[:, 0, :], -1.0, d, d[:, 0, :], ALU.mult, ALU.max)
                    P.act(d, d[:, 1, :], d, d[:, 1, :], AF.Exp, scale=-1.0)
                    P.act(d, d[:, 1, :], d, d[:, 1, :], AF.Ln, bias=1.0)
                    P.stt("dve", d, d[:, 2, :], d, d[:, 0, :], 0.0, d, d[:, 1, :], ALU.max, ALU.add)
                    P.dma("pool", DT, dt_d[t * 128:(t + 1) * 128, :], d, d[:, 2, :])
                    conv_load(t)
        conv_prod(NT - 1)
        conv_half(NT - 1, 0)
        conv_half(NT - 1, 1)
        P.barrier()
        P.emit()


PATTERNS = (1, 4, 16)
B_STOP = 99
C_STOP = 99
C_VAR = 0
HI_LIST = (0, 1)
MASK_ENG = 'pool'


def phase_b(nc, P, SEQ, CST, cst_d, QT, qT_d, KT, kT_d, V, v_d, AT, attnT_d):
    SBW = 2048
    NSB = SEQ // SBW
    with ExitStack() as ph:
        kTw = [P.sb(ph, "kTw%d" % i, [128, 4, SBW], BF16) for i in range(2)]
        qTw = [P.sb(ph, "qTw%d" % i, [128, 4, SBW], BF16) for i in range(2)]
        acc = P.sb(ph, "acc", [128, 8, SBW], F32)
        rd = P.sb(ph, "rd", [64, 8, 512], F32)
        ast = [P.sb(ph, "ast%d" % i, [64, 8, 512], BF16) for i in range(2)]
        vt = [P.sb(ph, "vt%d" % i, [128, 8, 128], BF16) for i in range(6)]
        pT = [P.sb(ph, "pT%d" % i, [128, 512], BF16) for i in range(3)]
        mask4 = P.sb(ph, "mask4", [128, 512], BF16)
        maskc = P.sb(ph, "maskc", [128, 256], BF16)
        psS = [P.ps(ph, "psS%d" % i, [128, 2, 512], F32) for i in range(2)]
        psN = [P.ps(ph, "psN%d" % i, [128, 2, 128], F32) for i in range(3)]

        P.dma("pool", mask4, mask4[:, 0:256], CST, cst_d[:, 256:512])
        P.dma("pool", mask4, mask4[:, 256:512], CST, cst_d[:, 256:512])
        P.dma("pool", maskc, maskc[:, 0:128], CST, cst_d[:, 384:512])
        P.dma("pool", maskc, maskc[:, 128:256], CST, cst_d[:, 384:512])

        vi = 0
        ui = 0
        for sb in range(NSB):
            T0 = sb * SBW
            kc_, qc_ = kTw[sb % 2], qTw[sb % 2]
            kp_ = kTw[(sb - 1) % 2]
            P.dma("sp", kc_, kc_[:], KT, kT_d[:, :, T0:T0 + SBW])
            P.dma("sp", qc_, qc_[:], QT, qT_d[:, :, T0:T0 + SBW])
            units = []
            for d in PATTERNS:
                span = 128 * d
                for b in range(SBW // span):
                    for r in range(d):
                        q0 = b * span + r
                        g0 = T0 + q0
                        has_prev = (T0 + b * span) >= span
                        for hp in range(4):
                            units.append((d, b, r, q0, g0, has_prev, hp))
            state = {}

            def front(u):
                nonlocal vi, ui
                d, b, r, q0, g0, has_prev, hp = units[u]
                span = 128 * d
                if hp == 0:
                    kbs = []
                    if has_prev:
                        vp = vt[vi % 6]
                        vi += 1
                        P.dma("sp", vp, vp[:], V, v_d[g0 - span:g0 - span + (127 * d + 1):d, :, :])
                        if b == 0:
                            kbs.append((kp_, SBW - span + r, vp))
                        else:
                            kbs.append((kc_, q0 - span, vp))
                    vc = vt[vi % 6]
                    vi += 1
                    P.dma("sp", vc, vc[:], V, v_d[g0:g0 + (127 * d + 1):d, :, :])
                    kbs.append((kc_, q0, vc))
                    state["kbs"] = kbs
                kbs = state["kbs"]
                nk = len(kbs)
                qs = slice(q0, q0 + 127 * d + 1, d)
                S = psS[ui % 2]
                pt = pT[ui % 3]
                N = psN[ui % 3]
                ui += 1
                W = 2 * nk * 128
                for hi in range(2):
                    pl = slice(64 * hi, 64 * hi + 64)
                    for ki, (kb, k0, _) in enumerate(kbs):
                        c = ki * 128
                        P.mm(S, S[:, hi, c:c + 128], kb, kb[pl, hp, k0:k0 + 127 * d + 1:d],
                             qc_, qc_[pl, hp, qs], True, True)
                P.act(pt, pt[:, 0:W].rearrange("p (h c) -> p h c", h=2), S, S[:, :, 0:nk * 128], AF.Exp)
                mk = mask4 if nk == 2 else maskc
                P.tt("dve", pt, pt[:, 0:W], pt, pt[:, 0:W], mk, mk[:, 0:W], ALU.mult)
                return (kbs, nk, qs, pt, N, hp, d == 1)

            def back(ctx):
                kbs, nk, qs, pt, N, hp, first = ctx
                for hi in range(2):
                    h = 2 * hp + hi
                    for ki, (kb, k0, vb_) in enumerate(kbs):
                        c = (hi * nk + ki) * 128
                        P.mm(N, N[:, hi, :], vb_, vb_[:, h, :], pt, pt[:, c:c + 128], ki == 0, ki == nk - 1)
                an = acc[:, 2 * hp:2 * hp + 2, qs]
                if first:
                    P.copy("dve", acc, an, N, N[:])
                else:
                    P.tt("dve", acc, an, acc, an, N, N[:], ALU.add)

            prev_ctx = None
            for u in range(len(units)):
                ctx = front(u)
                if prev_ctx is not None:
                    back(prev_ctx)
                prev_ctx = ctx
            back(prev_ctx)
            for c in range(SBW // 512):
                c0 = c * 512
                a_ = ast[c % 2]
                P.op("dve", lambda e, c0=c0: e.reciprocal(out=rd[:], in_=acc[64:128, :, c0:c0 + 512]), [acc], [rd])
                P.tt("dve", a_, a_[:], acc, acc[0:64, :, c0:c0 + 512], rd, rd[:], ALU.mult)
                P.dma("pool", AT, attnT_d[:, :, T0 + c0:T0 + c0 + 512], a_, a_[:])
        P.barrier()
        P.emit()


def phase_a2(nc, P, SEQ, VEC, vec_d, CST, cst_d, U, u_d, XC, xc_d):
    NT = SEQ // 128
    with ExitStack() as ph:
        identb = P.sb(ph, "identA2", [128, 128], BF16)
        cwb = P.sb(ph, "cwb", [128, 4, CONV_DIM], BF16)
        cbb = P.sb(ph, "cbb", [128, CONV_DIM], BF16)
        uk = [[P.sb(ph, "uk%d_%d" % (k, i), [128, CONV_DIM], BF16) for i in range(2)] for k in range(4)]
        pk = [[P.sb(ph, "pk%d_%d" % (k, i), [128, CONV_DIM], BF16) for i in range(2)] for k in range(4)]
        xcs = [P.sb(ph, "xcs%d" % i, [128, CONV_DIM], BF16) for i in range(2)]
        psC = [P.ps(ph, "psC%d" % i, [128, CONV_DIM], F32) for i in range(2)]
        P.dma("pool", identb, identb[:], CST, cst_d[:, 0:128])
        P.dma("pool", cwb, cwb[:], VEC, bc_ap(vec_d, VO["conv_w"], [[0, 128], [CONV_DIM, 4], [1, CONV_DIM]]))
        P.dma("pool", cbb, cbb[:], VEC, bc_ap(vec_d, VO["conv_b"], [[0, 128], [1, CONV_DIM]]))

        def loads(t):
            for k in range(4):
                P.dma("sp", uk[k][t % 2], uk[k][t % 2][:], U, u_d[t * 128 + k:t * 128 + k + 128, :])

        loads(0)
        for t in range(NT):
            b = t % 2
            if t + 1 < NT:
                loads(t + 1)
            for k in range(4):
                P.tt("dve", pk[k][b], pk[k][b][:], uk[k][b], uk[k][b][:], cwb, cwb[:, k, :], ALU.mult)
            pc = psC[b]
            for c in range(4):
                cs = slice(c * 512, (c + 1) * 512)
                for k in range(4):
                    P.mm(pc, pc[:, cs], identb, identb[:], pk[k][b], pk[k][b][:, cs], k == 0, False)
                P.mm(pc, pc[:, cs], identb, identb[:], cbb, cbb[:, cs], False, True)
            for c in range(4):
                cs = slice(c * 512, (c + 1) * 512)
                P.act(xcs[b], xcs[b][:, cs], pc, pc[:, cs], AF.Silu)
            P.dma("pool", XC, xc_d[t * 128:(t + 1) * 128, :], xcs[b], xcs[b][:])
        P.barrier()
        P.emit()


def phase_c(nc, P, SEQ, X, x_d, VEC, vec_d, CST, cst_d, XC, xc_d, SZ, sz_d, DT, dt_d,
            AT, attnT_d, WOA, woa_d, WOS, wos_d, OUT, out_d):
    NT = SEQ // 128
    with ExitStack() as ph:
        trif = P.sb(ph, "trif", [128, 128], F32)
        identb = P.sb(ph, "identC", [128, 128], BF16)
        onesf = P.sb(ph, "onesfC", [128, 128], F32)
        abc = P.sb(ph, "abc", [128, NH_S], F32)
        dbc = P.sb(ph, "dbc", [128, NH_S], F32)
        sgbc = P.sb(ph, "sgbc", [128, SW], F32)
        Woa = P.sb(ph, "Woa", [64, 8, D_MODEL], BF16)
        Wos = P.sb(ph, "Wos", [128, 12, D_MODEL], BF16)
        xcb = [P.sb(ph, "xcb%d" % i, [128, CONV_DIM], BF16) for i in range(2)]
        szt = P.sb(ph, "szt", [128, SW], BF16)
        dtt = [P.sb(ph, "dtt%d" % i, [128, NH_S], F32) for i in range(2)]
        att = [P.sb(ph, "att%d" % i, [64, 8, 128], BF16) for i in range(2)]
        xt = [P.sb(ph, "xtC%d" % i, [128, D_MODEL], F32) for i in range(2)]
        sm = [P.sb(ph, "smC%d" % i, [128, 8, NH_S], F32) for i in range(2)]
        st2 = P.sb(ph, "st2", [128, 8], F32)
        BCT = P.sb(ph, "BCT", [128, 4, 128], BF16)
        Gm = P.sb(ph, "Gm", [128, 2, 128], F32)
        dtmp = [P.sb(ph, "dtmp%d" % i, [128, 4, 128], F32) for i in range(2)]
        A4 = [P.sb(ph, "A4%d" % i, [128, 4, 128], F32) for i in range(2)]
        upf = P.sb(ph, "upf", [128, 128], F32)
        Mt = [P.sb(ph, "Mt%d" % i, [128, 4, 128], BF16) for i in range(2)]
        S = P.sb(ph, "Sst", [128, 2, 768], F32)
        Sb = P.sb(ph, "Sbf", [128, 2, 768], BF16)
        xwb = P.sb(ph, "xwb", [128, SW], BF16)
        xdt = P.sb(ph, "xdt", [128, SW], BF16)
        tmpD = P.sb(ph, "tmpD", [128, SW], F32)
        ysb = P.sb(ph, "ysb", [128, SW], F32)
        yb = P.sb(ph, "yb", [128, SW], BF16)
        yT = P.sb(ph, "yT", [128, 12, 128], BF16)
        psBC = [P.ps(ph, "psBC%d" % i, [128, 4, 128], F32) for i in range(2)]
        psY = P.ps(ph, "psY", [128, SW], F32)
        psSt = P.ps(ph, "psSt", [128, 1024], F32)
        psHi = Buf(psSt.t, "psSt_hi", "ps")
        psTr = P.ps(ph, "psTrC", [128, 4, 128], BF16)

        P.dma("sp", trif, trif[:], CST, cst_d[:, 128:256])
        P.dma("sp", upf, upf[:], CST, cst_d[:, 768:896])
        P.dma("pool", identb, identb[:], CST, cst_d[:, 0:128])
        P.op("dve", lambda e: e.memset(onesf[:], 1.0), [], [onesf])
        P.dma("sp", abc, abc[:], VEC, bc_ap(vec_d, VO["a_log"], [[0, 128], [1, NH_S]]))
        P.dma("sp", dbc, dbc[:], VEC, bc_ap(vec_d, VO["d_skip"], [[0, 128], [1, NH_S]]))
        P.dma("sp", sgbc, sgbc[:], VEC, bc_ap(vec_d, VO["ssm_norm"], [[0, 128], [1, SW]]))
        P.act(abc, abc[:], abc, abc[:], AF.Exp)
        P.ts("dve", abc, abc[:], abc, abc[:], -1.0, None, ALU.mult)
        P.dma("pool", Woa, Woa[:], WOA, woa_d[:, :, :])
        for j in range(12):
            P.dma("pool", Wos, Wos[:, j, :], WOS, wos_d[:, j, :])

        def loads(t):
            b = t % 2
            r0 = t * 128
            P.dma("sp", xcb[b], xcb[b][:], XC, xc_d[r0:r0 + 128, :])
            P.dma("sp", dtt[b], dtt[b][:], DT, dt_d[r0:r0 + 128, :])
            P.dma("sp", att[b], att[b][:], AT, attnT_d[:, :, r0:r0 + 128])
            P.dma("sp", xt[b], xt[b][:], X, x_d[r0:r0 + 128, :])

        loads(0)
        for t in range(NT):
            b = t % 2
            r0 = t * 128
            xc = xcb[b]
            if t + 1 < NT:
                loads(t + 1)
            P.dma("sp", szt, szt[:], SZ, sz_d[r0:r0 + 128, :])
            m = sm[b]
            d_ = dtt[b]
            P.tt("dve", m, m[:, 0, :], d_, d_[:], abc, abc[:], ALU.mult)
            P.mm(psHi, psHi[:, 768:768 + NH_S], trif, trif[:], m, m[:, 0, :], True, True)
            P.mm(psHi, psHi[:, 800:800 + NH_S], onesf, onesf[:], m, m[:, 0, :], True, True)
            P.copy("dve", m, m[:, 1, :], psHi, psHi[:, 768:768 + NH_S])
            P.copy("dve", m, m[:, 2, :], psHi, psHi[:, 800:800 + NH_S])
            P.tt("dve", m, m[:, 3, :], m, m[:, 2, :], m, m[:, 1, :], ALU.subtract)
            P.act(m, m[:, 3, :], m, m[:, 3, :], AF.Exp)
            P.tt("dve", m, m[:, 3, :], m, m[:, 3, :], d_, d_[:], ALU.mult)
            P.act(m, m[:, 4, :], m, m[:, 2, :], AF.Exp)
            for j in range(4):
                P.tr(psTr, psTr[:, j, :], xc, xc[:, SW + j * 128:SW + (j + 1) * 128], identb, identb[:])
            P.copy("act", BCT, BCT[:], psTr, psTr[:])
            for g in range(2):
                P.mm(psSt, psSt[:, g * 128:128 + g * 128], BCT, BCT[:, g, :], BCT, BCT[:, 2 + g, :], True, True)
            P.copy("act", Gm, Gm[:], psSt, psSt[:, 0:256].rearrange("p (g l) -> p g l", g=2))
            for g in range(2):
                P.tt("dve", Gm, Gm[:, g, :], Gm, Gm[:, g, :], trif, trif[:], ALU.mult)

            def pre(q4):
                i2 = (t * 6 + q4) % 2
                bc, a4 = psBC[i2], A4[i2]
                P.tt("dve", a4, a4[:], upf, upf[:].unsqueeze(1).to_broadcast([128, 4, 128]),
                     m, m[:, 0, 4 * q4:4 * q4 + 4].unsqueeze(2).to_broadcast([128, 4, 128]), ALU.mult)
                for j in range(4):
                    P.mm(bc, bc[:, j, :], a4, a4[:, j, :], trif, trif[:], True, True)

            def body_a(q4):
                i2 = (t * 6 + q4) % 2
                bc, dt4 = psBC[i2], dtmp[i2]
                P.act(dt4, dt4[:], bc, bc[:], AF.Exp)

            def body_b(q4):
                g = q4 // 3
                i2 = (t * 6 + q4) % 2
                dt4, mt = dtmp[i2], Mt[i2]
                for j in range(4):
                    P.stt("dve", mt, mt[:, j, :], dt4, dt4[:, j, :], 1.0, Gm, Gm[:, g, :], ALU.min, ALU.mult)
                for j in range(4):
                    h = 4 * q4 + j
                    P.mm(psY, psY[:, h * 64:(h + 1) * 64], mt, mt[:, j, :], xdt, xdt[:, h * 64:(h + 1) * 64], True, True)

            xs3 = xc[:, 0:SW].rearrange("p (h e) -> p h e", h=NH_S)
            P.ts("dve", m, m[:, 6, :], m, m[:, 1, :], -1.0, None, ALU.mult)
            P.tt("dve", xdt, xdt[:].rearrange("p (h e) -> p h e", h=NH_S), xc, xs3,
                 d_, d_[:].unsqueeze(2).to_broadcast([128, NH_S, 64]), ALU.mult)
            pre(0)
            body_a(0)
            pre(1)
            body_a(1)
            for q4 in range(6):
                if q4 + 2 < 6:
                    pre(q4 + 2)
                body_b(q4)
                if q4 + 2 < 6:
                    body_a(q4 + 2)
            P.tt("dve", tmpD, tmpD[:].rearrange("p (h e) -> p h e", h=NH_S), xc, xs3,
                 dbc, dbc[:].unsqueeze(2).to_broadcast([128, NH_S, 64]), ALU.mult)
            if t > 0:
                P.act(m, m[:, 5, :], m, m[:, 1, :], AF.Exp)
                for g in range(2):
                    P.mm(psSt, psSt[:, 0:512], BCT, BCT[:, 2 + g, :], Sb, Sb[:, g, 0:512], True, True)
                    P.mm(psSt, psSt[:, 512:768], BCT, BCT[:, 2 + g, :], Sb, Sb[:, g, 512:768], True, True)
                    y3 = ysb[:, g * 768:(g + 1) * 768].rearrange("p (h e) -> p h e", h=12)
                    P.tt("dve", ysb, y3, psSt, psSt[:, 0:768].rearrange("p (h e) -> p h e", h=12),
                         m, m[:, 5, 12 * g:12 * g + 12].unsqueeze(2).to_broadcast([128, 12, 64]), ALU.mult)
                    P.tt("dve", tmpD, tmpD[:, g * 768:(g + 1) * 768], tmpD, tmpD[:, g * 768:(g + 1) * 768],
                         ysb, ysb[:, g * 768:(g + 1) * 768], ALU.add)
            P.tt("dve", ysb, ysb[:], psY, psY[:], tmpD, tmpD[:], ALU.add)
            P.tt("dve", ysb, ysb[:], ysb, ysb[:], szt, szt[:], ALU.mult)
            for g in range(2):
                P.act(tmpD, tmpD[:, g * 768:(g + 1) * 768], ysb, ysb[:, g * 768:(g + 1) * 768], AF.Square,
                      wr=[st2], accum_out=st2[:, g:g + 1])
            P.rstd(st2, st2[:, 0:2], st2[:, 2:4], st2[:, 4:6], 1.0 / 768)
            for g in range(2):
                P.stt("dve", yb, yb[:, g * 768:(g + 1) * 768], ysb, ysb[:, g * 768:(g + 1) * 768], st2[:, 4 + g:5 + g],
                      sgbc, sgbc[:, g * 768:(g + 1) * 768], ALU.mult, ALU.mult, rd=[st2])
            P.tt("dve", xwb, xwb[:].rearrange("p (h e) -> p h e", h=NH_S), xc, xs3,
                 m, m[:, 3, :].unsqueeze(2).to_broadcast([128, NH_S, 64]), ALU.mult)
            for g in range(2):
                bs = xc[:, SW + g * 128:SW + (g + 1) * 128]
                P.mm(psSt, psSt[:, 0:512], xc, bs, xwb, xwb[:, g * 768:g * 768 + 512], True, True)
                P.mm(psSt, psSt[:, 512:768], xc, bs, xwb, xwb[:, g * 768 + 512:(g + 1) * 768], True, True)
                if t == 0:
                    P.copy("dve", S, S[:, g, :], psSt, psSt[:, 0:768])
                else:
                    sg3 = S[:, g, :].rearrange("p (h e) -> p h e", h=12)
                    P.tt("dve", S, sg3, S, sg3, m, m[:, 4, 12 * g:12 * g + 12].unsqueeze(2).to_broadcast([128, 12, 64]), ALU.mult)
                    P.tt("dve", S, S[:, g, :], S, S[:, g, :], psSt, psSt[:, 0:768], ALU.add)
            P.copy("act", Sb, Sb[:], S, S[:])
            for i in range(3):
                for j in range(4):
                    c = 4 * i + j
                    P.tr(psTr, psTr[:, j, :], yb, yb[:, c * 128:(c + 1) * 128], identb, identb[:])
                P.copy("act", yT, yT[:, 4 * i:4 * i + 4, :], psTr, psTr[:])
            a_ = att[b]
            for hf in range(2):
                o = psY[:, hf * 512:(hf + 1) * 512]
                for h in range(8):
                    P.mm(psY, o, a_, a_[0:64, h, :], Woa, Woa[0:64, h, hf * 512:(hf + 1) * 512], h == 0, False)
                for j in range(12):
                    P.mm(psY, o, yT, yT[:, j, :], Wos, Wos[:, j, hf * 512:(hf + 1) * 512], False, j == 11)
            P.tt("dve", xt[b], xt[b][:], xt[b], xt[b][:], psY, psY[:, 0:D_MODEL], ALU.add)
            P.dma("pool", OUT, out_d[r0:r0 + 128, :], xt[b], xt[b][:])
        P.barrier()
        P.emit()


def norm_T(P, xt_b, xt_ap, gbc, s, junk, hb, psT, hT, ident):
    P.act(junk, junk[:], xt_b, xt_ap, AF.Square, wr=[s], accum_out=s[:, 0:1])
    P.rstd(s, s[:, 0:1], s[:, 1:2], s[:, 2:3], 1.0 / D_MODEL)
    P.stt("dve", hb, hb[:], xt_b, xt_ap, s[:, 2:3], gbc, gbc[:], ALU.mult, ALU.mult, rd=[s])
    for kc in range(8):
        P.tr(psT, psT[:, kc, :], hb, hb[:, kc * 128:(kc + 1) * 128], ident, ident[:])
    P.copy("act", hT, hT[:], psT, psT[:])


def phase_d1(nc, P, SEQ, VEC, vec_d, CST, cst_d, WG, wg_d, WU, wu_d, WD, wd_d, OUT, out_d):
    NT = SEQ // 128
    with ExitStack() as ph:
        Wg = P.sb(ph, "Wg", [128, 8, D_FF], BF16)
        Wu = P.sb(ph, "Wu", [128, 8, D_FF], BF16)
        Wd = P.sb(ph, "Wd", [128, 22, D_MODEL], BF16)
        ident = P.sb(ph, "identD", [128, 128], BF16)
        gbc = P.sb(ph, "gffn", [128, D_MODEL], F32)
        xt = [P.sb(ph, "xtD%d" % i, [128, D_MODEL], F32) for i in range(3)]
        junk = P.sb(ph, "junkD", [128, D_MODEL], F32)
        ss = [P.sb(ph, "ssD%d" % i, [128, 8], F32) for i in range(2)]
        hb = P.sb(ph, "hbD", [128, D_MODEL], BF16)
        hT = [P.sb(ph, "hTD%d" % i, [128, 8, 128], BF16) for i in range(2)]
        sg = [P.sb(ph, "sgD%d" % i, [128, 512], F32) for i in range(2)]
        ab = P.sb(ph, "abD", [128, D_FF], BF16)
        aT = [P.sb(ph, "aTD%d" % i, [128, 22, 128], BF16) for i in range(2)]
        psT = P.ps(ph, "psTD", [128, 8, 128], BF16)
        psG = [P.ps(ph, "psGD%d" % i, [128, 512], F32) for i in range(2)]
        psU = [P.ps(ph, "psUD%d" % i, [128, 512], F32) for i in range(2)]
        psO = P.ps(ph, "psOD", [128, D_MODEL], F32)
        psA_ = P.ps(ph, "psAD", [128, 8, 128], BF16)
        psA = [psA_, Buf(psA_.t, "psAD_b", "ps")]
        for fc in range(22):
            P.dma("pool", Wg, Wg[:, :, fc * 128:(fc + 1) * 128], WG, wg_d[fc])
            P.dma("pool", Wu, Wu[:, :, fc * 128:(fc + 1) * 128], WU, wu_d[fc])
        for j in range(0, 22, 2):
            P.dma("pool", Wd, Wd[:, j:j + 2, :], WD, wd_d[:, j:j + 2, :])
        P.dma("pool", ident, ident[:], CST, cst_d[:, 0:128])
        P.dma("sp", gbc, gbc[:], VEC, bc_ap(vec_d, VO["ffn_norm"], [[0, 128], [1, D_MODEL]]))
        def nfront(t):
            b = t % 3
            s_ = ss[t % 2]
            P.act(junk, junk[:], xt[b], xt[b][:], AF.Square, wr=[s_], accum_out=s_[:, 0:1])
            P.rstd(s_, s_[:, 0:1], s_[:, 1:2], s_[:, 2:3], 1.0 / D_MODEL)
            P.stt("dve", hb, hb[:], xt[b], xt[b][:], s_[:, 2:3], gbc, gbc[:], ALU.mult, ALU.mult, rd=[s_])

        def nback(t):
            b = t % 2
            for kc in range(8):
                P.tr(psT, psT[:, kc, :], hb, hb[:, kc * 128:(kc + 1) * 128], ident, ident[:])
            P.copy("act", hT[b], hT[b][:], psT, psT[:])

        def ab_T(t, fc):
            n = 4 if fc < 5 else 2
            hh = (t * 6 + fc) % 2
            pa = psA[hh]
            for j in range(n):
                c = 4 * fc + j
                P.tr(pa, pa[:, 4 * hh + j, :], ab, ab[:, c * 128:(c + 1) * 128], ident, ident[:])
            P.copy("act", aT[t % 2], aT[t % 2][:, 4 * fc:4 * fc + n, :], pa, pa[:, 4 * hh:4 * hh + n, :])

        def down(t):
            a_ = aT[t % 2]
            x_ = xt[t % 3]
            for hf in range(2):
                for j in range(22):
                    P.mm(psO, psO[:, hf * 512:(hf + 1) * 512], a_, a_[:, j, :], Wd, Wd[:, j, hf * 512:(hf + 1) * 512], j == 0, j == 21)
            P.tt("dve", x_, x_[:], x_, x_[:], psO, psO[:], ALU.add)
            P.dma("pool", OUT, out_d[t * 128:(t + 1) * 128, :], x_, x_[:])

        P.dma("sp", xt[0], xt[0][:], OUT, out_d[0:128, :])
        if NT > 1:
            P.dma("sp", xt[1], xt[1][:], OUT, out_d[128:256, :])
        nfront(0)
        nback(0)
        cnt = 0
        for t in range(NT):
            b = t % 2
            for fc in range(6):
                f0 = fc * 512
                w = min(512, D_FF - f0)
                G_, U_, sg_ = psG[cnt % 2], psU[cnt % 2], sg[cnt % 2]
                cnt += 1
                for kc in range(8):
                    P.mm(G_, G_[:, 0:w], hT[b], hT[b][:, kc, :], Wg, Wg[:, kc, f0:f0 + w], kc == 0, kc == 7)
                for kc in range(8):
                    P.mm(U_, U_[:, 0:w], hT[b], hT[b][:, kc, :], Wu, Wu[:, kc, f0:f0 + w], kc == 0, kc == 7)
                P.act(sg_, sg_[:, 0:w], G_, G_[:, 0:w], AF.Silu)
                P.tt("dve", ab, ab[:, f0:f0 + w], sg_, sg_[:, 0:w], U_, U_[:, 0:w], ALU.mult)
                if fc == 0 and t > 0:
                    down(t - 1)
                if fc == 1 and t + 1 < NT:
                    if t + 2 < NT:
                        P.dma("sp", xt[(t + 2) % 3], xt[(t + 2) % 3][:], OUT, out_d[(t + 2) * 128:(t + 3) * 128, :])
                    nfront(t + 1)
                if fc >= 1:
                    ab_T(t, fc - 1)
                if fc == 4 and t + 1 < NT:
                    nback(t + 1)
            ab_T(t, 5)
        down(NT - 1)
        P.barrier()
        P.emit()


def phase_d2(nc, P, SEQ, VEC, vec_d, CST, cst_d, PIN, p_d, WPG, wpg_d, WPLE, wple_d, OUT, out_d):
    NT = SEQ // 128
    with ExitStack() as ph:
        Wpg = P.sb(ph, "Wpg", [128, 8, D_MODEL], BF16)
        Wpl = P.sb(ph, "Wpl", [128, 2, D_MODEL], BF16)
        ident = P.sb(ph, "identE", [128, 128], BF16)
        gbc = P.sb(ph, "gpg", [128, D_MODEL], F32)
        bbc = P.sb(ph, "bpg", [128, D_MODEL], F32)
        pgbc = P.sb(ph, "gple", [128, D_MODEL], F32)
        xt = [P.sb(ph, "xtE%d" % i, [128, D_MODEL], F32) for i in range(3)]
        pt = [P.sb(ph, "ptE%d" % i, [128, PLE], F32) for i in range(2)]
        pb = P.sb(ph, "pbE", [128, PLE], BF16)
        pT = [P.sb(ph, "pTE%d" % i, [128, 2, 128], BF16) for i in range(2)]
        junk = P.sb(ph, "junkE", [128, D_MODEL], F32)
        ss = [P.sb(ph, "ssE%d" % i, [128, 8], F32) for i in range(2)]
        hb = P.sb(ph, "hbE", [128, D_MODEL], BF16)
        hT = [P.sb(ph, "hTE%d" % i, [128, 8, 128], BF16) for i in range(2)]
        gt = P.sb(ph, "gtE", [128, D_MODEL], F32)
        en = P.sb(ph, "enE", [128, D_MODEL], F32)
        psT = P.ps(ph, "psTE", [128, 8, 128], BF16)
        psP = P.ps(ph, "psPE", [128, 4, 128], BF16)
        psGt = P.ps(ph, "psGt", [128, D_MODEL], F32)
        psE = P.ps(ph, "psE", [128, D_MODEL], F32)
        for kc in range(0, 8, 2):
            P.dma("pool", Wpg, Wpg[:, kc:kc + 2, :], WPG, wpg_d[:, kc:kc + 2, :])
        P.dma("pool", Wpl, Wpl[:], WPLE, wple_d[:, :, :])
        P.dma("pool", ident, ident[:], CST, cst_d[:, 0:128])
        P.dma("sp", gbc, gbc[:], VEC, bc_ap(vec_d, VO["ple_gate_norm"], [[0, 128], [1, D_MODEL]]))
        P.dma("sp", bbc, bbc[:], VEC, bc_ap(vec_d, VO["b_ple_gate"], [[0, 128], [1, D_MODEL]]))
        P.dma("sp", pgbc, pgbc[:], VEC, bc_ap(vec_d, VO["ple_norm"], [[0, 128], [1, D_MODEL]]))
        def load(t):
            r0 = t * 128
            P.dma("sp", xt[t % 3], xt[t % 3][:], OUT, out_d[r0:r0 + 128, :])
            P.dma("sp", pt[t % 2], pt[t % 2][:], PIN, p_d[r0:r0 + 128, :])

        def s1(t):
            x_, s_ = xt[t % 3], ss[t % 2]
            P.act(junk, junk[:], x_, x_[:], AF.Square, wr=[s_], accum_out=s_[:, 0:1])
            P.rstd(s_, s_[:, 0:1], s_[:, 1:2], s_[:, 2:3], 1.0 / D_MODEL)
            P.stt("dve", hb, hb[:], x_, x_[:], s_[:, 2:3], gbc, gbc[:], ALU.mult, ALU.mult, rd=[s_])
            P.copy("dve", pb, pb[:], pt[t % 2], pt[t % 2][:])

        def s2(t):
            b = t % 2
            for kc in range(8):
                P.tr(psT, psT[:, kc, :], hb, hb[:, kc * 128:(kc + 1) * 128], ident, ident[:])
            P.copy("act", hT[b], hT[b][:], psT, psT[:])
            for j in range(2):
                P.tr(psP, psP[:, j, :], pb, pb[:, j * 128:(j + 1) * 128], ident, ident[:])
            P.copy("act", pT[b], pT[b][:], psP, psP[:, 0:2, :])
            for hf in range(2):
                for kc in range(8):
                    P.mm(psGt, psGt[:, hf * 512:(hf + 1) * 512], hT[b], hT[b][:, kc, :], Wpg, Wpg[:, kc, hf * 512:(hf + 1) * 512], kc == 0, kc == 7)
            for hf in range(2):
                for j in range(2):
                    P.mm(psE, psE[:, hf * 512:(hf + 1) * 512], pT[b], pT[b][:, j, :], Wpl, Wpl[:, j, hf * 512:(hf + 1) * 512], j == 0, j == 1)

        def s3(t):
            x_, s_ = xt[t % 3], ss[t % 2]
            P.tt("dve", gt, gt[:], psGt, psGt[:], bbc, bbc[:], ALU.add)
            P.act(gt, gt[:], gt, gt[:], AF.Sigmoid)
            P.act(junk, junk[:], psE, psE[:], AF.Square, wr=[s_], accum_out=s_[:, 4:5])
            P.rstd(s_, s_[:, 4:5], s_[:, 5:6], s_[:, 6:7], 1.0 / D_MODEL)
            P.stt("dve", en, en[:], psE, psE[:], s_[:, 6:7], pgbc, pgbc[:], ALU.mult, ALU.mult, rd=[s_])
            P.tt("pool", en, en[:], en, en[:], gt, gt[:], ALU.mult)
            P.tt("dve", x_, x_[:], x_, x_[:], en, en[:], ALU.add)
            P.dma("pool", OUT, out_d[t * 128:(t + 1) * 128, :], x_, x_[:])

        load(0)
        if NT > 1:
            load(1)
        s1(0)
        s2(0)
        for t in range(NT):
            if t + 2 < NT:
                load(t + 2)
            if t + 1 < NT:
                s1(t + 1)
            s3(t)
            if t + 1 < NT:
                s2(t + 1)
        P.barrier()
        P.emit()


def make_consts():
    c = np.zeros((128, 1024), np.float32)
    i = np.arange(128)
    c[:, 0:128] = np.eye(128, dtype=np.float32)
    c[:, 128:256] = (i[:, None] <= i[None, :])
    c[:, 256:384] = (i[:, None] >= i[None, :])
    c[:, 384:512] = (i[:, None] <= i[None, :])
    c[:, 512:640] = np.where(i[:, None] >= i[None, :], 0.0, -30000.0)
    c[:, 640:768] = np.where(i[:, None] <= i[None, :], 0.0, -30000.0)
    c[:, 768:896] = (i[:, None] > i[None, :])
    return c


def pack_inputs(inp, SEQ=SEQ_FULL):
    f = lambda a: np.ascontiguousarray(np.asarray(a, dtype=np.float32))
    vec = np.zeros((1, NVEC), np.float32)
    for k, o in VO.items():
        a = f(inp[k][0]).reshape(-1)
        vec[0, o:o + a.size] = a
    shared = dict(
        w_in=f(f(inp["w_in"][0]).reshape(8, 128, IN_PROJ).transpose(1, 0, 2)),
        w_out_a=f(f(inp["w_out"][0])[0:AW].reshape(8, 64, D_MODEL).transpose(1, 0, 2)),
        w_out_s=f(f(inp["w_out"][0])[AW:].reshape(12, 128, D_MODEL).transpose(1, 0, 2)),
        w_g=f(f(inp["w_ffn_gate"][0]).reshape(8, 128, 22, 128).transpose(2, 1, 0, 3)),
        w_u=f(f(inp["w_ffn_up"][0]).reshape(8, 128, 22, 128).transpose(2, 1, 0, 3)),
        w_d=f(f(inp["w_ffn_down"][0]).reshape(22, 128, D_MODEL).transpose(1, 0, 2)),
        w_pg=f(f(inp["w_ple_gate"][0]).reshape(8, 128, D_MODEL).transpose(1, 0, 2)),
        w_ple=f(f(inp["w_ple"][0]).reshape(2, 128, D_MODEL).transpose(1, 0, 2)),
        vecs=vec,
        consts=make_consts(),
    )
    maps = []
    for b in range(inp["x"].shape[0]):
        m = dict(shared)
        m["x"] = f(inp["x"][b][:SEQ])
        m["p"] = f(inp["p"][0][b][:SEQ])
        maps.append(m)
    return maps


_NC_CACHE = {}


def kernel(**inputs):
    maps = pack_inputs(inputs)
    if "nc" not in _NC_CACHE:
        _NC_CACHE["nc"] = build()
    nc = _NC_CACHE["nc"]
    res = run_bass_kernel_spmd(nc, maps, core_ids=list(range(len(maps))))
    return np.stack([np.asarray(r["out"], dtype=np.float32) for r in res.results], axis=0)
```
